# Optimizing a Trainium2 kernel written in Bass

```python
import math
import jax
import jax.numpy as jnp
from jax import lax
import numpy as np

D_MODEL = 1024
BATCH = 4
SEQ = 8192
DEPTH = 1

GRID_W = 64
CTX_LEN = 256
EPS = 1e-6
N_MOD = 6

GDN_HEADS = 8
GDN_DK = 128
GDN_DV = 128
GDN_CONV = 5
GDN_CHUNK = 64
DT_MIN = 0.001
DT_MAX = 0.1
GDN_QKV_W = GDN_HEADS * (2 * GDN_DK + GDN_DV)

MLA_HEADS = 8
MLA_Q_LORA = 384
MLA_KV_LORA = 256
MLA_NOPE = 128
MLA_ROPE = 64
MLA_V = 128
MLA_QK = MLA_NOPE + MLA_ROPE
Q_BLOCK = 128
ROPE_THETA = 10000.0
ROPE_AXIS = MLA_ROPE // 2
ROPE_FREQS = ROPE_AXIS // 2

D_FF = 4 * D_MODEL
N_BRANCH = 2

IN_SPLITS = (GDN_QKV_W, GDN_HEADS * GDN_DV, 2 * GDN_HEADS, 2 * GDN_HEADS,
             MLA_Q_LORA, MLA_KV_LORA, MLA_ROPE, N_BRANCH * D_MODEL)
IN_COLS = sum(IN_SPLITS)

kernel_name = 'hybrid_gdn_mla_dit_layer'


def _rms(x, w):
    xf = x.astype(jnp.float32)
    y = xf * lax.rsqrt(jnp.mean(xf * xf, axis=-1, keepdims=True) + EPS)
    return y.astype(x.dtype) * w


def _l2norm(x):
    xf = x.astype(jnp.float32)
    return (xf * lax.rsqrt(jnp.sum(xf * xf, axis=-1, keepdims=True) + EPS)).astype(x.dtype)


def _modulate(xn, shift, scale):
    return xn * (1.0 + scale) + shift


def _split_in(proj):
    offsets = np.cumsum(IN_SPLITS)[:-1].tolist()
    return jnp.split(proj, offsets, axis=-1)


def _centred_conv(x, w):
    k_w = w.shape[0]
    t = x.shape[1]
    p = k_w // 2
    xp = jnp.pad(x, ((0, 0), (p, p), (0, 0)))
    return sum(xp[:, i:i + t] * w[i] for i in range(k_w))


def _gdn_inputs(qkv, beta_logit, decay_logit, conv_w, a_log, dt_bias):
    B, T, _ = qkv.shape
    qkv = jax.nn.silu(_centred_conv(qkv, conv_w))
    q, k, v = jnp.split(qkv, [GDN_HEADS * GDN_DK, 2 * GDN_HEADS * GDN_DK], axis=-1)
    q = _l2norm(q.reshape(B, T, GDN_HEADS, GDN_DK))
    k = _l2norm(k.reshape(B, T, GDN_HEADS, GDN_DK))
    v = v.reshape(B, T, GDN_HEADS, GDN_DV)
    beta = jax.nn.sigmoid(beta_logit.astype(jnp.float32)).reshape(B, T, 2, GDN_HEADS)
    g = -jnp.exp(a_log.astype(jnp.float32)) * jax.nn.softplus(
        decay_logit.astype(jnp.float32).reshape(B, T, 2, GDN_HEADS) + dt_bias.astype(jnp.float32))
    return q, k, v, g, beta


def _chunk_gated_delta(q, k, v, g, beta, state0):
    in_dtype = v.dtype
    B, T, H, DK = q.shape
    C = GDN_CHUNK
    n = T // C

    def chunks(t):
        return t.astype(jnp.float32).reshape(B, n, C, H, -1).transpose(0, 3, 1, 2, 4)

    q = chunks(q) * (DK ** -0.5)
    k = chunks(k)
    v = chunks(v)
    gc = jnp.cumsum(chunks(g[..., None])[..., 0], axis=-1)
    beta = chunks(beta[..., None])
    tri = jnp.tril(jnp.ones((C, C), dtype=bool))
    strict = jnp.tril(jnp.ones((C, C), dtype=bool), -1)
    diff = gc[..., :, None] - gc[..., None, :]
    decay = jnp.where(tri, jnp.exp(jnp.where(tri, diff, 0.0)), 0.0)
    kb = k * beta
    vb = v * beta
    a_mat = jnp.where(strict, jnp.einsum('bhnid,bhnjd->bhnij', kb, k) * decay, 0.0)
    eye = jnp.eye(C, dtype=jnp.float32)
    t_mat = lax.linalg.triangular_solve(eye + a_mat, jnp.broadcast_to(eye, a_mat.shape),
                                        left_side=True, lower=True, unit_diagonal=True)
    u = jnp.einsum('bhnij,bhnjd->bhnid', t_mat, vb)
    w = jnp.einsum('bhnij,bhnjd->bhnid', t_mat, kb * jnp.exp(gc)[..., None])
    qk = jnp.where(tri, jnp.einsum('bhnid,bhnjd->bhnij', q, k) * decay, 0.0)
    q_dec = q * jnp.exp(gc)[..., None]
    k_dec = k * jnp.exp(gc[..., -1:] - gc)[..., None]
    g_last = jnp.exp(gc[..., -1])
    xs = tuple(jnp.moveaxis(t, 2, 0) for t in (u, w, qk, q_dec, k_dec, g_last))

    def step(s, inp):
        u_i, w_i, qk_i, qd_i, kd_i, gl_i = inp
        v_new = u_i - jnp.einsum('bhcd,bhde->bhce', w_i, s)
        o_i = jnp.einsum('bhcd,bhde->bhce', qd_i, s) + jnp.einsum('bhcj,bhje->bhce', qk_i, v_new)
        s = s * gl_i[..., None, None] + jnp.einsum('bhcd,bhce->bhde', kd_i, v_new)
        return s, o_i

    s_final, o = lax.scan(step, state0.astype(jnp.float32), xs)
    o = jnp.moveaxis(o, 0, 2).transpose(0, 2, 3, 1, 4).reshape(B, T, H, -1)
    return o.astype(in_dtype), s_final


def _bidirectional_gdn(q, k, v, g, beta, s0_fwd, s0_bwd):
    o_f, s_f = _chunk_gated_delta(q, k, v, g[:, :, 0], beta[:, :, 0], s0_fwd)
    rev = lambda t: jnp.flip(t, axis=1)
    o_b, s_b = _chunk_gated_delta(rev(q), rev(k), rev(v), rev(g[:, :, 1]), rev(beta[:, :, 1]), s0_bwd)
    return o_f + rev(o_b), s_f, s_b


def _gdn_out(o, z, w_norm):
    B, T = o.shape[:2]
    y = _rms(o, w_norm) * jax.nn.silu(z.reshape(B, T, GDN_HEADS, GDN_DV))
    return y.reshape(B, T, GDN_HEADS * GDN_DV)


def _axial_rope(x, row, col):
    inv = ROPE_THETA ** (-jnp.arange(ROPE_FREQS, dtype=jnp.float32) / ROPE_FREQS)

    def rotate(seg, pos):
        ang = pos[:, None] * inv
        cos = jnp.cos(ang)[None, :, None, :].astype(seg.dtype)
        sin = jnp.sin(ang)[None, :, None, :].astype(seg.dtype)
        x1, x2 = seg[..., :ROPE_FREQS], seg[..., ROPE_FREQS:]
        return jnp.concatenate([x1 * cos - x2 * sin, x2 * cos + x1 * sin], axis=-1)

    return jnp.concatenate([rotate(x[..., :ROPE_AXIS], row), rotate(x[..., ROPE_AXIS:], col)], axis=-1)


def _mla_qkv(cq, ckv, k_rope, q_a_norm, kv_a_norm, w_uq, w_ukv, q_norm, k_norm, pos):
    B, T, _ = cq.shape
    q = (_rms(cq, q_a_norm) @ w_uq).reshape(B, T, MLA_HEADS, MLA_QK)
    kv = (_rms(ckv, kv_a_norm) @ w_ukv).reshape(B, T, MLA_HEADS, MLA_NOPE + MLA_V)
    k_nope, v = kv[..., :MLA_NOPE], kv[..., MLA_NOPE:]
    k = jnp.concatenate([k_nope, jnp.broadcast_to(k_rope[:, :, None, :], (B, T, MLA_HEADS, MLA_ROPE))], axis=-1)
    q = _rms(q, q_norm)
    k = _rms(k, k_norm)
    if pos is not None:
        row, col = pos
        q = jnp.concatenate([q[..., :MLA_NOPE], _axial_rope(q[..., MLA_NOPE:], row, col)], axis=-1)
        k = jnp.concatenate([k[..., :MLA_NOPE], _axial_rope(k[..., MLA_NOPE:], row, col)], axis=-1)
    return q, k, v


def _softmax_attend(q, k, v):
    s = jnp.einsum('bqhd,bkhd->bhqk', q, k).astype(jnp.float32) * (MLA_QK ** -0.5)
    p = jax.nn.softmax(s, axis=-1).astype(v.dtype)
    return jnp.einsum('bhqk,bkhd->bqhd', p, v)


def _blockwise_attention(q, k, v):
    B, T, H, _ = q.shape
    nb = T // Q_BLOCK
    qb = jnp.moveaxis(q.reshape(B, nb, Q_BLOCK, H, -1), 1, 0)
    o = lax.map(lambda qq: _softmax_attend(qq, k, v), qb)
    return jnp.moveaxis(o, 0, 1).reshape(B, T, H * v.shape[-1])


def _merge(o_gdn, o_mla, gate_logits, w_branch_gdn, w_branch_mla, w_out):
    g_gdn, g_mla = jnp.split(gate_logits, N_BRANCH, axis=-1)
    y = jax.nn.sigmoid(g_gdn) * (o_gdn @ w_branch_gdn) + jax.nn.sigmoid(g_mla) * (o_mla @ w_branch_mla)
    return y @ w_out


def _sq_relu_mlp(h, w1, w2):
    return jnp.square(jax.nn.relu(h @ w1)) @ w2


def setup_inputs(seed: int = 0) -> dict:
    key = jax.random.key(seed)
    ks = jax.random.split(key, 24)
    D = D_MODEL

    def nrm(k, shape, scale):
        return jax.random.normal(k, shape, jnp.float32) * scale

    def gain(k, n):
        return 1.0 + nrm(k, (DEPTH, n), 0.02)

    dt = jnp.exp(jax.random.uniform(ks[9], (DEPTH, 2, GDN_HEADS), jnp.float32,
                                    minval=math.log(DT_MIN), maxval=math.log(DT_MAX)))
    return {
        'x': nrm(ks[0], (BATCH, SEQ, D), 1.0),
        'c': nrm(ks[1], (BATCH, D), 1.0),
        'ctx': nrm(ks[2], (BATCH, CTX_LEN, D), 1.0),
        'c_ctx': nrm(ks[3], (D,), 1.0),
        'w_mod': nrm(ks[4], (DEPTH, D, N_MOD * D), D ** -0.5),
        'b_mod': nrm(ks[5], (DEPTH, N_MOD * D), 0.01),
        'norm_attn': gain(ks[6], D),
        'norm_mlp': gain(ks[7], D),
        'w_in': nrm(ks[8], (DEPTH, D, IN_COLS), D ** -0.5),
        'conv_qkv': nrm(ks[10], (DEPTH, GDN_CONV, GDN_QKV_W), GDN_CONV ** -0.5),
        'gdn_a_log': jnp.log(jax.random.uniform(ks[11], (DEPTH, 2, GDN_HEADS), jnp.float32, minval=1.0, maxval=16.0)),
        'gdn_dt_bias': dt + jnp.log(-jnp.expm1(-dt)),
        'gdn_out_norm': gain(ks[12], GDN_DV),
        'mla_q_a_norm': gain(ks[13], MLA_Q_LORA),
        'mla_kv_a_norm': gain(ks[14], MLA_KV_LORA),
        'w_uq': nrm(ks[15], (DEPTH, MLA_Q_LORA, MLA_HEADS * MLA_QK), MLA_Q_LORA ** -0.5),
        'w_ukv': nrm(ks[16], (DEPTH, MLA_KV_LORA, MLA_HEADS * (MLA_NOPE + MLA_V)), MLA_KV_LORA ** -0.5),
        'q_norm': gain(ks[17], MLA_QK),
        'k_norm': gain(ks[18], MLA_QK),
        'w_branch_gdn': nrm(ks[19], (DEPTH, GDN_HEADS * GDN_DV, D), (GDN_HEADS * GDN_DV) ** -0.5),
        'w_branch_mla': nrm(ks[20], (DEPTH, MLA_HEADS * MLA_V, D), (MLA_HEADS * MLA_V) ** -0.5),
        'w_out': nrm(ks[21], (DEPTH, D, D), D ** -0.5),
        'w_mlp_in': nrm(ks[22], (DEPTH, D, D_FF), D ** -0.5),
        'w_mlp_out': nrm(ks[23], (DEPTH, D_FF, D), 0.5 * D_FF ** -0.5),
    }


def reference(x, c, ctx, c_ctx, w_mod, b_mod, norm_attn, norm_mlp, w_in, conv_qkv, gdn_a_log, gdn_dt_bias,
              gdn_out_norm, mla_q_a_norm, mla_kv_a_norm, w_uq, w_ukv, q_norm, k_norm, w_branch_gdn,
              w_branch_mla, w_out, w_mlp_in, w_mlp_out):
    B, L, _ = x.shape
    lc = ctx.shape[1]
    rows = L // GRID_W
    row = jnp.repeat(jnp.arange(rows, dtype=jnp.float32), GRID_W)
    col = jnp.tile(jnp.arange(GRID_W, dtype=jnp.float32), rows)
    silu_c = jax.nn.silu(c)[:, None, :]
    silu_cc = jax.nn.silu(c_ctx)
    zero_state = jnp.zeros((B, GDN_HEADS, GDN_DK, GDN_DV), jnp.float32)

    for l in range(DEPTH):
        mod_x = jnp.split(silu_c @ w_mod[l] + b_mod[l], N_MOD, axis=-1)
        mod_c = jnp.split(silu_cc @ w_mod[l] + b_mod[l], N_MOD, axis=-1)

        h = _modulate(_rms(x, norm_attn[l]), mod_x[0], mod_x[1])
        hc = _modulate(_rms(ctx, norm_attn[l]), mod_c[0], mod_c[1])
        qkv, z, beta_lg, decay_lg, cq, ckv, k_rope, gates = _split_in(h @ w_in[l])
        qkv_c, z_c, beta_lg_c, decay_lg_c, cq_c, ckv_c, k_rope_c, gates_c = _split_in(hc @ w_in[l])

        qa_c, ka_c, va_c, g_c, beta_c = _gdn_inputs(qkv_c, beta_lg_c, decay_lg_c, conv_qkv[l], gdn_a_log[l], gdn_dt_bias[l])
        o_gdn_c, s_fwd, s_bwd = _bidirectional_gdn(qa_c, ka_c, va_c, g_c, beta_c, zero_state, zero_state)
        qa, ka, va, g, beta = _gdn_inputs(qkv, beta_lg, decay_lg, conv_qkv[l], gdn_a_log[l], gdn_dt_bias[l])
        o_gdn, _, _ = _bidirectional_gdn(qa, ka, va, g, beta, s_fwd, s_bwd)

        qb, kb, vb = _mla_qkv(cq, ckv, k_rope, mla_q_a_norm[l], mla_kv_a_norm[l], w_uq[l], w_ukv[l],
                              q_norm[l], k_norm[l], (row, col))
        qb_c, kb_c, vb_c = _mla_qkv(cq_c, ckv_c, k_rope_c, mla_q_a_norm[l], mla_kv_a_norm[l], w_uq[l], w_ukv[l],
                                    q_norm[l], k_norm[l], None)
        o_mla = _blockwise_attention(qb, jnp.concatenate([kb_c, kb], axis=1), jnp.concatenate([vb_c, vb], axis=1))

        x = x + mod_x[2] * _merge(_gdn_out(o_gdn, z, gdn_out_norm[l]), o_mla, gates,
                                  w_branch_gdn[l], w_branch_mla[l], w_out[l])
        x = x + mod_x[5] * _sq_relu_mlp(_modulate(_rms(x, norm_mlp[l]), mod_x[3], mod_x[4]), w_mlp_in[l], w_mlp_out[l])

        if l < DEPTH - 1:
            o_mla_c = _softmax_attend(qb_c, kb_c, vb_c).reshape(B, lc, MLA_HEADS * MLA_V)
            ctx = ctx + mod_c[2] * _merge(_gdn_out(o_gdn_c, z_c, gdn_out_norm[l]), o_mla_c, gates_c,
                                          w_branch_gdn[l], w_branch_mla[l], w_out[l])
            ctx = ctx + mod_c[5] * _sq_relu_mlp(_modulate(_rms(ctx, norm_mlp[l]), mod_c[3], mod_c[4]),
                                                w_mlp_in[l], w_mlp_out[l])
    return x
```

```python
import math
import numpy as np
import ml_dtypes
import concourse.bass as bass
import concourse.mybir as mybir
from concourse.bass_utils import run_bass_kernel_spmd

F32 = mybir.dt.float32
BF16 = mybir.dt.bfloat16
AF = mybir.ActivationFunctionType
ALU = mybir.AluOpType
AX = mybir.AxisListType

D = 1024
KD = 8
H = 8
EPS = 1e-6
GRID_W = 64
NCONST = 24


class Buf:
    __slots__ = ("name", "writer", "readers")

    def __init__(self, name):
        self.name = name
        self.writer = None
        self.readers = []


class Op:
    __slots__ = ("eng", "fn", "deps", "signal", "tok", "is_dma", "dsem")

    def __init__(self, eng, fn, is_dma=False):
        self.eng = eng
        self.fn = fn
        self.deps = []
        self.signal = False
        self.tok = None
        self.is_dma = is_dma
        self.dsem = None


class Prog:
    ENGS = ("pe", "act", "dve", "pool", "sp")
    NDSEM = 12

    def __init__(self):
        self.ops = []
        self.bufs = {}
        self.last = {}
        self.dma_since = []

    def _B(self, x):
        b = self.bufs.get(x)
        if b is None:
            b = Buf(x)
            self.bufs[x] = b
        return b

    def op(self, eng, fn, reads=(), writes=(), is_dma=False):
        idx = len(self.ops)
        o = Op(eng, fn, is_dma)
        deps = set()
        for r in reads:
            r = self._B(r)
            if r.writer is not None:
                deps.add(r.writer)
        for w in writes:
            w = self._B(w)
            if w.writer is not None:
                deps.add(w.writer)
            deps.update(w.readers)
        fin = []
        for d in deps:
            od = self.ops[d]
            if od.eng == eng and eng == "pe" and not od.is_dma and not is_dma:
                continue
            fin.append(d)
            od.signal = True
        o.deps = sorted(fin)
        self.ops.append(o)
        for r in reads:
            rb = self._B(r)
            if not is_dma:
                rb.readers = [q for q in rb.readers if self.ops[q].is_dma or self.ops[q].eng != eng]
            rb.readers.append(idx)
        for w in writes:
            w = self._B(w)
            w.writer = idx
            w.readers = []
        self.last[eng] = idx
        if is_dma:
            self.dma_since.append(idx)
        return idx

    def barrier(self):
        deps = sorted(set(list(self.last.values()) + self.dma_since))
        for d in deps:
            self.ops[d].signal = True
        for e in self.ENGS:
            o = Op(e, None)
            o.deps = list(deps)
            self.ops.append(o)
        self.dma_since = []
        for b in self.bufs.values():
            b.writer = None
            b.readers = []

    def emit(self, nc):
        sems = {e: nc.alloc_semaphore("S_" + e) for e in self.ENGS}
        dq = ("sp", "pool", "act")
        dsems = {e: [nc.alloc_semaphore("D_%s_%d" % (e, j)) for j in range(self.NDSEM)] for e in dq}
        cnt = {e: 0 for e in self.ENGS}
        dcnt = {e: [0] * self.NDSEM for e in dq}
        dnext = {e: 0 for e in dq}
        for o in self.ops:
            if o.is_dma:
                j = dnext[o.eng]
                dnext[o.eng] = (j + 1) % self.NDSEM
                prev = dcnt[o.eng][j]
                dcnt[o.eng][j] += 16
                o.dsem = (dsems[o.eng][j], prev)
                o.tok = (dsems[o.eng][j], dcnt[o.eng][j])
            elif o.signal and o.fn is not None:
                cnt[o.eng] += 1
                o.tok = (sems[o.eng], cnt[o.eng])
        per = {e: [] for e in self.ENGS}
        for o in self.ops:
            per[o.eng].append(o)
        ops = self.ops
        tail = [o for o in ops if o.is_dma]

        def run(ename):
            def body(eng):
                seen = {}

                def wait(tok):
                    if tok is None:
                        return
                    s, v = tok
                    k = id(s)
                    if seen.get(k, 0) >= v:
                        return
                    seen[k] = v
                    eng.wait_ge(s, v)

                for o in per[ename]:
                    for d in o.deps:
                        wait(ops[d].tok)
                    if o.fn is None:
                        continue
                    if o.is_dma:
                        s, prev = o.dsem
                        if prev > 0:
                            wait((s, prev))
                        o.fn(eng).then_inc(o.tok[0], 16)
                    else:
                        ins = o.fn(eng)
                        if o.signal:
                            ins.then_inc(o.tok[0], 1)
                if ename == "sp":
                    for o in tail:
                        wait(o.tok)
            return body

        with nc.Block() as block:
            block.tensor(run("pe"))
            block.scalar(run("act"))
            block.vector(run("dve"))
            block.gpsimd(run("pool"))
            block.sync(run("sp"))
        return len(ops)


class Arena:
    def __init__(self, nc, nbytes):
        self.t = nc.alloc_sbuf_tensor("arena", [128, nbytes // 4], F32)
        self.cap = nbytes // 4
        self.off = 0
        self.uid = 0

    def alloc(self, shape, dtype):
        n = 1
        for s in shape:
            n *= s
        words = n if dtype == F32 else (n + 1) // 2
        words = (words + 7) // 8 * 8
        assert self.off + words <= self.cap, "SBUF arena overflow %d+%d>%d" % (self.off, words, self.cap)
        v = self.t[:, self.off:self.off + words]
        self.off += words
        if dtype == BF16:
            v = v.bitcast(BF16)
        v = v[:, 0:n]
        if len(shape) == 2:
            v = v.rearrange("p (a b) -> p a b", a=shape[0])
        elif len(shape) == 3:
            v = v.rearrange("p (a b c) -> p a b c", a=shape[0], b=shape[1])
        self.uid += 1
        return v

    def mark(self):
        return self.off

    def release(self, m):
        self.off = m


def bc(ap, shape):
    return ap.to_broadcast(shape)


def build_nc(NL, NC, dbg=False, stop=99, sub=99):
    NO = NL // 2
    NT = NC + NL
    NTC = NC // 128
    NTL = NL // 128
    NTO = NO // 128
    NG = NTC + NTL
    nc = bass.Bass("TRN2", target_bir_lowering=False)

    def din(name, shape, dt=F32):
        return nc.dram_tensor(name, list(shape), dt, kind="ExternalInput").ap()

    x_d = din("x", [NL, D]); ctx_d = din("ctx", [NC, D])
    cvec_d = din("cvec", [128, 16]); wmod_d = din("w_mod", [D, 6 * D]); bmod_d = din("b_mod", [128, 48])
    nattn_d = din("norm_attn", [128, 8]); nmlp_d = din("norm_mlp", [128, 8])
    wqkv_d = din("w_qkv", [D, 3072]); wzg_d = din("w_zg", [D, 3072]); wbd_d = din("w_bd", [D, 32]); wmla_d = din("w_mla", [D, 704])
    conv_d = din("conv_w", [128, 24 * 5]); alog_d = din("a_log", [16]); dtb_d = din("dt_bias", [16])
    gon_d = din("gdn_out_norm", [128]); qan_d = din("q_a_norm", [384]); kvan_d = din("kv_a_norm", [256])
    qn_d = din("q_norm", [192]); kn_d = din("k_norm", [192])
    wuq_d = din("w_uq", [384, 1536]); wuk_d = din("w_uk", [256, 1024]); wuv_d = din("w_uv", [256, 1024])
    wg_d = din("w_bg", [D, D]); wm_d = din("w_bm", [D, D]); wo_d = din("w_out", [D, D])
    w1_d = din("w_mlp_in", [D, 4 * D]); w2_d = din("w_mlp_out", [4 * D, D])
    const_d = din("consts", [128, NCONST * 128]); cos_d = din("rope_cos", [NT, 64]); sin_d = din("rope_sin", [NT, 64])
    out_d = nc.dram_tensor("out", [NO, D], F32, kind="ExternalOutput").ap()

    def dscr(name, shape, dt):
        return nc.dram_tensor(name, list(shape), dt, kind="Internal").ap()

    of_d = dscr("scr_of", [NO, D], F32); ob_d = dscr("scr_ob", [NO, D], F32)
    qt_d = dscr("scr_qt", [H, 192, NO], BF16); om_d = dscr("scr_om", [H, 128, NO], BF16)
    x1_d = dscr("scr_x1", [NO, D], F32)

    P = Prog()
    A = Arena(nc, 207 * 1024)
    ps = [nc.alloc_psum_tensor("psb%d" % i, [128, 512], F32) for i in range(8)]
    bank_state = {"n": 0, "lo": 0, "hi": 8}

    def nb():
        r = bank_state["hi"] - bank_state["lo"]
        b = bank_state["lo"] + bank_state["n"] % r
        bank_state["n"] += 1
        return b

    def pf(b):
        return ps[b][:]

    def pb16(b):
        return ps[b][:].bitcast(BF16)

    def PB(b):
        return "ps%d" % b

    uid = [0]

    def U(prefix):
        uid[0] += 1
        return "%s#%d" % (prefix, uid[0])

    def dma(q, out, in_, reads, writes):
        P.op(q, lambda e: e.dma_start(out=out, in_=in_), reads=reads, writes=writes, is_dma=True)

    def mm(out, lhsT, rhs, start, stop, reads, bank):
        P.op("pe", lambda e: e.matmul(out, lhsT=lhsT, rhs=rhs, start=start, stop=stop), reads=reads, writes=[PB(bank)])

    def tr(out, in_, ident, reads, bank):
        P.op("pe", lambda e: e.transpose(out, in_, ident), reads=reads, writes=[PB(bank)])

    def act(out, in_, func, reads, writes, scale=None, bias=None, accum=None):
        kw = {}
        if scale is not None:
            kw["scale"] = scale
        if bias is not None:
            kw["bias"] = bias
        if accum is not None:
            kw["accum_out"] = accum
        P.op("act", lambda e: e.activation(out=out, in_=in_, func=func, **kw), reads=reads, writes=writes)

    def tt(eng, out, in0, in1, op, reads, writes):
        P.op(eng, lambda e: e.tensor_tensor(out=out, in0=in0, in1=in1, op=op), reads=reads, writes=writes)

    def ts(eng, out, in0, s1, op0, reads, writes, s2=None, op1=None):
        if op1 is None:
            P.op(eng, lambda e: e.tensor_scalar(out=out, in0=in0, scalar1=s1, scalar2=None, op0=op0), reads=reads, writes=writes)
        else:
            P.op(eng, lambda e: e.tensor_scalar(out=out, in0=in0, scalar1=s1, scalar2=s2, op0=op0, op1=op1), reads=reads, writes=writes)

    def stt(out, in0, scalar, in1, op0, op1, reads, writes):
        P.op("dve", lambda e: e.scalar_tensor_tensor(out=out, in0=in0, scalar=scalar, in1=in1, op0=op0, op1=op1), reads=reads, writes=writes)

    def cp(eng, out, in_, reads, writes):
        if eng == "act":
            act(out, in_, AF.Copy, reads, writes)
        else:
            P.op(eng, lambda e: e.tensor_copy(out=out, in_=in_), reads=reads, writes=writes)

    def red(out, in_, reads, writes):
        P.op("dve", lambda e: e.tensor_reduce(out=out, in_=in_, axis=AX.X, op=ALU.add), reads=reads, writes=writes)

    cF = A.alloc([6, 128], F32)
    cB = A.alloc([NCONST, 128], BF16)
    cols = A.alloc([8], F32)
    stage_all = A.alloc([2048], F32)
    stage = [stage_all[:, 0:1024], stage_all[:, 1024:2048]]
    cvec = A.alloc([16], F32)
    sc = A.alloc([16], F32)
    modF = A.alloc([6, 2, 8], F32)
    bmod = A.alloc([48], F32)
    nrm = A.alloc([2, 8], F32)
    G1 = A.alloc([2, 8], F32); G2 = A.alloc([8], F32)
    m0 = A.mark()
    ctemp = A.alloc([NCONST, 128], F32)
    dma("sp", ctemp.rearrange("p a b -> p (a b)"), const_d, [], ["ctemp"])
    dma("sp", cF[:, 0:4, :].rearrange("p a b -> p (a b)"), const_d[:, 0:4 * 128], [], ["cF"])
    dma("sp", cF[:, 4:6, :].rearrange("p a b -> p (a b)"), const_d[:, 13 * 128:15 * 128], [], ["cF"])
    cp("dve", cB, ctemp, ["ctemp"], ["cB"])
    for j, val in enumerate((EPS, 1.0, math.log(128 ** -0.5), 0.0, math.log(192 ** -0.5))):
        P.op("pool", lambda e, j=j, val=val: e.memset(cols[:, j:j + 1], val), writes=["cols"])
    identF = cF[:, 0, :]; onesF = cF[:, 1, :]
    identB = cB[:, 0, :]; onesB = cB[:, 1, :]
    CF = {"F": 2, "B": 4}
    CB = {"F": 2, "B": 13}

    def rsqrt_small(out, in_, mul, reads, writes, tmp, tmpn):
        act(tmp, in_, AF.Ln, reads, [tmpn], scale=mul, bias=cols[:, 0:1])
        act(out, tmp, AF.Exp, [tmpn], writes, scale=-0.5)

    stg = [0]

    def load_w(dst, src, K, N, name, eng_cycle=("act", "dve")):
        for k in range(K // 128):
            for c0 in range(0, N, 1024):
                c1 = min(N, c0 + 1024)
                i = stg[0] % 2
                stg[0] += 1
                st = stage[i][:, 0:c1 - c0]
                dma("sp", st, src[k * 128:(k + 1) * 128, c0:c1], [], ["stage%d" % i])
                cp(eng_cycle[stg[0] % len(eng_cycle)], dst[:, k, c0:c1], st, ["stage%d" % i], [name])

    dma("sp", cvec, cvec_d, [], ["cvec"])
    dma("sp", bmod, bmod_d, [], ["bmod"])
    dma("sp", nrm[:, 0, :], nattn_d, [], ["nrm"])
    dma("sp", nrm[:, 1, :], nmlp_d, [], ["nrm"])
    act(sc, cvec, AF.Silu, ["cvec"], ["sc"])
    scv = sc.rearrange("p (v k) -> p k v", v=2)
    wst = [A.alloc([8, 512], F32) for _ in range(2)]
    for blk in range(12):
        w = wst[blk % 2]
        dma("sp", w, wmod_d[:, blk * 512:(blk + 1) * 512].rearrange("(k p) n -> p k n", p=128), [], ["wst%d" % (blk % 2)])
        b = nb()
        for t4 in range(4):
            for k in range(8):
                mm(pf(b)[:, t4 * 2:(t4 + 1) * 2], w[:, k, t4 * 128:(t4 + 1) * 128], scv[:, k, :], k == 0, k == 7,
                   ["wst%d" % (blk % 2), "sc"], b)
        e = blk // 2
        t0 = (blk % 2) * 4
        tt("dve", modF[:, e, :, t0:t0 + 4], pf(b)[:, 0:8].rearrange("p (t v) -> p v t", v=2),
           bc(bmod[:, e * 8 + t0:e * 8 + t0 + 4].unsqueeze(1), [128, 2, 4]), ALU.add, [PB(b), "bmod"], ["modF"])
    A.release(m0)
    for v in range(2):
        stt(G1[:, v, :], modF[:, 1, v, :], 1.0, nrm[:, 0, :], ALU.add, ALU.mult, ["modF", "nrm"], ["G1"])
    stt(G2, modF[:, 4, 0, :], 1.0, nrm[:, 1, :], ALU.add, ALU.mult, ["modF", "nrm"], ["G2"])
    SH1 = modF[:, 0, :, :]
    SH2 = modF[:, 3, 0, :]

    P.barrier()

    if stop == 0:
        return nc, P.emit(nc)
    def make_hT(src_rows, v, hT, tag, G=None, SH=None, xt_keep=None):
        xt = xt_keep if xt_keep is not None else A_x[tag_i[0] % 2]
        xn = A_xn
        nm = "xt%d" % (tag_i[0] % 2) if xt_keep is None else tag + "xt"
        tag_i[0] += 1
        dma("sp", xt, src_rows, [], [nm])
        act(A_junk, xt, AF.Square, [nm], ["junk", "ss1"], accum=A_ss[:, 0:1])
        rsqrt_small(A_ss[:, 1:2], A_ss[:, 0:1], 1.0 / D, ["ss1"], ["rs1"], A_ss[:, 2:3], "ss1t")
        act(xn, xt, AF.Copy, [nm, "rs1"], ["xn"], scale=A_ss[:, 1:2])
        b = nb()
        for k in range(8):
            tr(pb16(b)[:, k * 128:(k + 1) * 128], xn[:, k * 128:(k + 1) * 128], identB, ["xn", "cB"], b)
        g = G if G is not None else G1[:, v, :]
        s = SH if SH is not None else SH1[:, v, :]
        tt("dve", A_ht32, pb16(b).rearrange("p (k t) -> p k t", k=8), bc(g.unsqueeze(2), [128, 8, 128]), ALU.mult,
           [PB(b), "G1", "G2"], ["ht32"])
        tt("dve", hT, A_ht32, bc(s.unsqueeze(2), [128, 8, 128]), ALU.add, ["ht32", "modF"], [tag])

    tag_i = [0]
    A_x = [A.alloc([D], F32) for _ in range(2)]
    A_xn = A.alloc([D], BF16)
    A_junk = A.alloc([D], BF16)
    A_ss = A.alloc([8], F32)
    A_ht32 = A.alloc([8, 128], F32)

    def tile_rows(g):
        if g < NTC:
            return ctx_d[g * 128:(g + 1) * 128, :], 1
        t = g - NTC
        return x_d[t * 128:(t + 1) * 128, :], 0

    mg = A.mark()
    Wqkv = A.alloc([8, 3072], BF16)
    Wbd = A.alloc([8, 32], BF16)
    convw = A.alloc([24, 5], F32)
    diagW = A.alloc([120, 128], BF16)
    negA = A.alloc([16], F32); dtb = A.alloc([16], F32)
    load_w(Wqkv, wqkv_d, D, 3072, "Wqkv")
    load_w(Wbd, wbd_d, D, 32, "Wbd")
    dma("sp", convw.rearrange("p a b -> p (a b)"), conv_d, [], ["convw"])
    dma("sp", negA, alog_d.partition_broadcast(128), [], ["negA"])
    dma("sp", dtb, dtb_d.partition_broadcast(128), [], ["dtb"])
    act(negA, negA, AF.Exp, ["negA"], ["negA"])
    ts("dve", negA, negA, -1.0, ALU.mult, ["negA"], ["negA"])
    for ct in range(24):
        for i in range(5):
            ts("pool" if (ct + i) % 2 else "dve", diagW[:, ct * 5 + i, :], identF, convw[:, ct, i:i + 1], ALU.mult,
               ["cF", "convw"], ["diagW"])

    P.barrier()
    hT = [A.alloc([8, 128], BF16) for _ in range(2)]
    Xw = [A.alloc([24, 132], BF16) for _ in range(2)]
    Xw.append(stage_all[:, 0:24 * 132 // 2].bitcast(BF16).rearrange("p (a b) -> p a b", a=24))
    Xe = [A.alloc([24, 4], BF16) for _ in range(4)]
    YT = A.alloc([24, 128], BF16)
    SQ = A.alloc([16, 128], BF16)
    RS = A.alloc([16, 128], F32)
    QKN = A.alloc([16, 128], BF16)
    Ktok = A.alloc([8, 128], BF16); Vtok = A.alloc([8, 128], BF16)
    sca = A.alloc([12, 8], F32)
    Rm = A.alloc([8, 128], F32); Fm = A.alloc([8, 128], F32)
    FB = A.alloc([8, 128], BF16); Fq = A.alloc([8, 128], BF16)
    BF_ = A.alloc([8, 128], BF16); Bl = [A.alloc([8, 128], BF16) for _ in range(2)]
    qkT = A.alloc([8, 128], BF16)
    Dm = [A.alloc([8, 128], BF16) for _ in range(2)]; Wm = [A.alloc([8, 128], BF16) for _ in range(2)]
    ImY = A.alloc([8, 128], BF16)
    Xp = A.alloc([8, 128], BF16); VN = A.alloc([8, 128], BF16); Kd = A.alloc([8, 128], BF16)
    tmpf = Rm
    S32 = A.alloc([8, 128], F32); Sbf = A.alloc([8, 128], BF16)
    o1 = Fm; osb = [A.alloc([8, 128], F32) for _ in range(2)]

    A_lg = [A.alloc([16], F32) for _ in range(3)]

    def gdn_sweep2(dirn, order, out_tiles, o_dram, extra=None):
        c0 = {"F": 2, "B": 13}[dirn]
        Uc = cF[:, CF[dirn], :]; SUc = cF[:, CF[dirn] + 1, :]
        UIm = cB[:, c0 + 2, :]; SUm = cB[:, c0 + 3, :]
        lvl = [cB[:, c0 + 4 + l, :] for l in range(7)]
        dcol = 0 if dirn == "F" else 16
        dsc = 0 if dirn == "F" else 8
        P.op("pool", lambda e: e.memset(S32, 0.0), writes=["S32a", "S32b"])
        P.op("pool", lambda e: e.memset(Sbf, 0.0), writes=["Sbfa", "Sbfb"])
        n_proc = len(order)
        order = list(order) + ([extra] if extra is not None else [])
        n_ord = len(order)
        asc = (dirn == "F")

        def seq_of(g):
            return 0 if g < NTC else 1

        def v4(b):
            return pf(b).rearrange("p (a t) -> p a t", a=4)

        def project(n):
            g = order[n]
            rows, v = tile_rows(g)
            h = hT[n % 2]
            hn = "hT%d" % (n % 2)
            make_hT(rows, v, h, hn)
            yield
            xw = Xw[n % 3]
            xwn = "Xw%d" % (n % 3)
            for c3 in range(8):
                b = nb()
                for j in range(3):
                    ct = c3 * 3 + j
                    for k in range(8):
                        mm(pf(b)[:, j * 128:(j + 1) * 128], Wqkv[:, k, ct * 128:(ct + 1) * 128], h[:, k, :], k == 0, k == 7, ["Wqkv", hn], b)
                cp("act" if c3 % 2 else "dve", xw[:, c3 * 3:(c3 + 1) * 3, 2:130], pf(b)[:, 0:384].rearrange("p (a t) -> p a t", a=3), [PB(b)], [xwn])
                yield
            xe = Xe[n % 4]
            cp("pool", xe[:, :, 0:2], xw[:, :, 2:4], [xwn], ["Xe%d" % (n % 4)])
            cp("pool", xe[:, :, 2:4], xw[:, :, 128:130], [xwn], ["Xe%d" % (n % 4)])
            b = nb()
            for k in range(8):
                mm(pf(b)[:, 0:16], h[:, k, :], Wbd[:, k, dcol:dcol + 16], k == 0, k == 7, [hn, "Wbd"], b)
            cp("act", A_lg[n % 3], pf(b)[:, 0:16], [PB(b)], ["lg%d" % (n % 3)])

        def chunk(m, fill):
            g = order[m]
            want_out = g in out_tiles
            xw = Xw[m % 3]
            xwn = "Xw%d" % (m % 3)
            lg = A_lg[m % 3]
            lgn = "lg%d" % (m % 3)

            def nbr(mm_):
                if mm_ < 0 or mm_ >= n_ord or seq_of(order[mm_]) != seq_of(g):
                    return None
                return Xe[mm_ % 4], "Xe%d" % (mm_ % 4)
            prv = nbr(m - 1) if asc else nbr(m + 1)
            nxt = nbr(m + 1) if asc else nbr(m - 1)
            if prv is not None:
                cp("pool", xw[:, :, 0:2], prv[0][:, :, 2:4], [prv[1], xwn], [xwn])
            else:
                P.op("pool", lambda e, xw=xw: e.memset(xw[:, :, 0:2], 0.0), reads=[xwn], writes=[xwn])
            if nxt is not None:
                cp("pool", xw[:, :, 130:132], nxt[0][:, :, 0:2], [nxt[1], xwn], [xwn])
            else:
                P.op("pool", lambda e, xw=xw: e.memset(xw[:, :, 130:132], 0.0), reads=[xwn], writes=[xwn])
            for c4 in range(6):
                b = nb()
                for j in range(4):
                    ct = c4 * 4 + j
                    o_ = pf(b)[:, j * 128:(j + 1) * 128]
                    for i in range(5):
                        mm(o_, diagW[:, ct * 5 + i, :], xw[:, ct, i:i + 128], i == 0, i == 4, ["diagW", xwn], b)
                act(YT[:, c4 * 4:(c4 + 1) * 4, :], v4(b), AF.Silu, [PB(b)], ["YT"])
            fill()
            tt("pool", SQ, YT[:, 0:16, :], YT[:, 0:16, :], ALU.mult, ["YT"], ["SQ"])
            for c4 in range(4):
                b = nb()
                for j in range(4):
                    mm(pf(b)[:, j * 128:(j + 1) * 128], onesB, SQ[:, c4 * 4 + j, :], True, True, ["cB", "SQ"], b)
                act(RS[:, c4 * 4:(c4 + 1) * 4, :], v4(b), AF.Ln, [PB(b)], ["RS%d" % c4], bias=cols[:, 0:1])
            act(RS[:, 0:8, :], RS[:, 0:8, :], AF.Exp, ["RS0", "RS1"], ["RS0", "RS1"], scale=-0.5, bias=cols[:, 2:3])
            act(RS[:, 8:16, :], RS[:, 8:16, :], AF.Exp, ["RS2", "RS3"], ["RS2", "RS3"], scale=-0.5)
            tt("dve", QKN, YT[:, 0:16, :], RS, ALU.mult, ["YT", "RS0", "RS1", "RS2", "RS3"], ["QKN"])
            bk = nb()
            for h in range(8):
                tr(pb16(bk)[:, h * 128:(h + 1) * 128], QKN[:, 8 + h, :], identB, ["QKN", "cB"], bk)
            bv = nb()
            for h in range(8):
                tr(pb16(bv)[:, h * 128:(h + 1) * 128], YT[:, 16 + h, :], identB, ["YT", "cB"], bv)
            cp("act", Ktok, pb16(bk).rearrange("p (a t) -> p a t", a=8), [PB(bk)], ["Ktok"])
            cp("dve", Vtok, pb16(bv).rearrange("p (a t) -> p a t", a=8), [PB(bv)], ["Vtok"])
            fill()
            beta = sca[:, 0, :]; xx = sca[:, 1, :]; ax = sca[:, 2, :]; g_ = sca[:, 3, :]
            egc = sca[:, 4, :]; negegc = sca[:, 5, :]; gcs = sca[:, 6, :]; edl = sca[:, 7, :]; etot = sca[:, 8, :]
            act(beta, lg[:, 0:8], AF.Sigmoid, [lgn], ["beta"])
            tt("dve", xx, lg[:, 8:16], dtb[:, dsc:dsc + 8], ALU.add, [lgn, "dtb"], ["xx"])
            ts("dve", ax, xx, -1.0, ALU.mult, ["xx"], ["ax"])
            tt("dve", ax, ax, xx, ALU.min, ["xx", "ax"], ["ax"])
            act(ax, ax, AF.Exp, ["ax"], ["ax"])
            act(ax, ax, AF.Ln, ["ax"], ["ax"], bias=cols[:, 1:2])
            ts("dve", xx, xx, 0.0, ALU.max, ["xx"], ["xx"])
            tt("dve", xx, xx, ax, ALU.add, ["xx", "ax"], ["xx"])
            tt("dve", g_, xx, negA[:, dsc:dsc + 8], ALU.mult, ["xx", "negA"], ["g"])
            b = nb()
            mm(pf(b)[:, 0:8], Uc, g_, True, True, ["cF", "g"], b)
            mm(pf(b)[:, 8:16], onesF, g_, True, True, ["cF", "g"], b)
            act(egc, pf(b)[:, 0:8], AF.Exp, [PB(b)], ["egc"])
            act(gcs, pf(b)[:, 0:8], AF.Copy, [PB(b)], ["gcs"])
            act(etot, pf(b)[:, 8:16], AF.Exp, [PB(b)], ["etot"])
            ts("dve", negegc, egc, -1.0, ALU.mult, ["egc"], ["negegc"])
            tt("dve", edl, pf(b)[:, 8:16], gcs, ALU.subtract, [PB(b), "gcs", "etot", "egc"], ["edl"])
            act(edl, edl, AF.Exp, ["edl"], ["edl"])
            fill()
            tt("pool", Rm, bc(Uc.unsqueeze(1), [128, 8, 128]), bc(g_.unsqueeze(2), [128, 8, 128]), ALU.mult, ["cF", "g"], ["Rm"])
            for hh in range(2):
                b = nb()
                for j in range(4):
                    mm(pf(b)[:, j * 128:(j + 1) * 128], SUc, Rm[:, hh * 4 + j, :], True, True, ["cF", "Rm"], b)
                act(Fm[:, hh * 4:(hh + 1) * 4, :], v4(b), AF.Exp, [PB(b)], ["Fm%d" % hh])
            tt("pool", FB, Fm, bc(SUm.unsqueeze(1), [128, 8, 128]), ALU.mult, ["Fm0", "Fm1", "cB"], ["FB"])
            if want_out:
                tt("pool", Fq, Fm, bc(UIm.unsqueeze(1), [128, 8, 128]), ALU.mult, ["Fm0", "Fm1", "cB"], ["Fq"])
            for hh in range(2):
                sl = slice(hh * 4, (hh + 1) * 4)
                b = nb()
                for j in range(4):
                    h = hh * 4 + j
                    mm(pf(b)[:, j * 128:(j + 1) * 128], QKN[:, 8 + h, :], QKN[:, 8 + h, :], True, True, ["QKN"], b)
                tt("dve", BF_[:, sl, :], v4(b), FB[:, sl, :], ALU.mult, [PB(b), "FB"], ["BF%d" % hh])
                if want_out:
                    b2 = nb()
                    for j in range(4):
                        h = hh * 4 + j
                        mm(pf(b2)[:, j * 128:(j + 1) * 128], QKN[:, 8 + h, :], QKN[:, h, :], True, True, ["QKN"], b2)
                    tt("dve", qkT[:, sl, :], v4(b2), Fq[:, sl, :], ALU.mult, [PB(b2), "Fq"], ["qkT%d" % hh])
            tt("pool", Dm[0], bc(identB.unsqueeze(1), [128, 8, 128]), bc(beta.unsqueeze(2), [128, 8, 128]), ALU.mult, ["cB", "beta"], ["D0a", "D0b"])
            hs = ("a", "b")
            for l in range(7):
                fill()
                cur, nx = l % 2, (l + 1) % 2
                Wcur = Dm[0] if l == 0 else Wm[cur]
                wn_ = (lambda hh_: "D0" + hs[hh_]) if l == 0 else (lambda hh_, cur=cur: "W%d%s" % (cur, hs[hh_]))
                Bc = Bl[l % 2]
                bn = "Bl%d" % (l % 2)
                tt("pool", Bc, BF_, bc(lvl[l].unsqueeze(1), [128, 8, 128]), ALU.mult, ["BF0", "BF1", "cB"], [bn])
                for hh in range(2):
                    sl = slice(hh * 4, (hh + 1) * 4)
                    b = nb()
                    for j in range(4):
                        h = hh * 4 + j
                        mm(pf(b)[:, j * 128:(j + 1) * 128], Bc[:, h, :], Dm[cur][:, h, :], True, True, [bn, "D%d%s" % (cur, hs[hh])], b)
                    tt("dve", ImY[:, sl, :], bc(identB.unsqueeze(1), [128, 4, 128]), v4(b), ALU.subtract, [PB(b), "cB"], ["ImY%d" % hh])
                for hh in range(2):
                    sl = slice(hh * 4, (hh + 1) * 4)
                    if l < 6:
                        b = nb()
                        for j in range(4):
                            h = hh * 4 + j
                            mm(pf(b)[:, j * 128:(j + 1) * 128], Wcur[:, h, :], ImY[:, h, :], True, True, [wn_(hh), "ImY%d" % hh], b)
                        cp("act", Dm[nx][:, sl, :], v4(b), [PB(b)], ["D%d%s" % (nx, hs[hh])])
                    b = nb()
                    for j in range(4):
                        h = hh * 4 + j
                        mm(pf(b)[:, j * 128:(j + 1) * 128], ImY[:, h, :], Wcur[:, h, :], True, True, [wn_(hh), "ImY%d" % hh], b)
                    cp("act" if hh else "dve", Wm[nx][:, sl, :], v4(b), [PB(b)], ["W%d%s" % (nx, hs[hh])])
            WT = Wm[1]
            tt("pool", Kd, Ktok, bc(edl.unsqueeze(2), [128, 8, 128]), ALU.mult, ["Ktok", "edl"], ["Kd"])
            for hh in range(2):
                sl = slice(hh * 4, (hh + 1) * 4)
                b = nb()
                for j in range(4):
                    h = hh * 4 + j
                    mm(pf(b)[:, j * 128:(j + 1) * 128], QKN[:, 8 + h, :], Sbf[:, h, :], True, True, ["QKN", "Sbf" + hs[hh]], b)
                tt("dve", tmpf[:, sl, :], v4(b), bc(negegc[:, sl].unsqueeze(2), [128, 4, 128]), ALU.mult, [PB(b), "negegc"], ["Rm"])
                tt("dve", Xp[:, sl, :], tmpf[:, sl, :], Vtok[:, sl, :], ALU.add, ["Rm", "Vtok"], ["Xp%d" % hh])
            for hh in range(2):
                sl = slice(hh * 4, (hh + 1) * 4)
                b = nb()
                for j in range(4):
                    h = hh * 4 + j
                    mm(pf(b)[:, j * 128:(j + 1) * 128], WT[:, h, :], Xp[:, h, :], True, True, ["W1%s" % hs[hh], "Xp%d" % hh], b)
                cp("act", VN[:, sl, :], v4(b), [PB(b)], ["VN%d" % hh])
            if want_out:
                ot = out_tiles[g]
                ob_ = osb[ot % 2]
                obn = "osb%d" % (ot % 2)
                for hh in range(2):
                    sl = slice(hh * 4, (hh + 1) * 4)
                    b = nb()
                    for j in range(4):
                        h = hh * 4 + j
                        mm(pf(b)[:, j * 128:(j + 1) * 128], QKN[:, h, :], Sbf[:, h, :], True, True, ["QKN", "Sbf" + hs[hh]], b)
                    tt("dve", o1[:, sl, :], v4(b), bc(egc[:, sl].unsqueeze(2), [128, 4, 128]), ALU.mult, [PB(b), "egc"], ["Fm%d" % hh])
                    b = nb()
                    for j in range(4):
                        h = hh * 4 + j
                        mm(pf(b)[:, j * 128:(j + 1) * 128], qkT[:, h, :], VN[:, h, :], True, True, ["qkT%d" % hh, "VN%d" % hh], b)
                    tt("dve", ob_[:, sl, :], v4(b), o1[:, sl, :], ALU.add, [PB(b), "Fm%d" % hh], [obn + hs[hh]])
                dma("pool", o_dram[ot * 128:(ot + 1) * 128, :], ob_.rearrange("p a t -> p (a t)"), [obn + "a", obn + "b"], [obn + "a", obn + "b"])
            for hh in range(2):
                sl = slice(hh * 4, (hh + 1) * 4)
                b = nb()
                for j in range(4):
                    h = hh * 4 + j
                    mm(pf(b)[:, j * 128:(j + 1) * 128], Kd[:, h, :], VN[:, h, :], True, True, ["Kd", "VN%d" % hh], b)
                for j in range(4):
                    h = hh * 4 + j
                    stt(S32[:, h, :], S32[:, h, :], etot[:, h:h + 1], pf(b)[:, j * 128:(j + 1) * 128], ALU.mult, ALU.add,
                        [PB(b), "S32" + hs[hh], "etot"], ["S32" + hs[hh]])
                cp("act", Sbf[:, sl, :], S32[:, sl, :], ["S32" + hs[hh]], ["Sbf" + hs[hh]])

        def drain(gen):
            for _ in gen:
                pass

        drain(project(0))
        if n_ord > 1:
            drain(project(1))
        for m in range(n_proc):
            gen = project(m + 2) if m + 2 < n_ord else iter(())

            def fill(gen=gen):
                next(gen, None)
            chunk(m, fill)
            drain(gen)

    own = {NTC + t: t for t in range(NTO)}
    orderF = list(range(NTC)) + [NTC + t for t in range(NTO)]
    orderB = list(range(NTC - 1, -1, -1)) + [NTC + t for t in range(NTL - 1, -1, -1)]
    P.barrier()
    gdn_sweep2("F", orderF, own, of_d, extra=NTC + NTO)
    if stop == 1:
        return nc, P.emit(nc)
    gdn_sweep2("B", orderB, own, ob_d)
    P.barrier()
    A.release(mg)

    if stop == 2:
        return nc, P.emit(nc)
    mm_ = A.mark()
    Wuk = A.alloc([2, 1024], BF16); Wuv = A.alloc([2, 1024], BF16)
    ckvnT = A.alloc([2, NT], BF16); krT = A.alloc([NT], BF16)
    rstdk = A.alloc([NG, 8], F32)
    m2b = A.mark()
    Wmla = A.alloc([8, 704], BF16); Wuq = A.alloc([3, 1536], BF16)
    load_w(Wmla, wmla_d, D, 704, "Wmla"); load_w(Wuq, wuq_d, 384, 1536, "Wuq")
    load_w(Wuk, wuk_d, 256, 1024, "Wuk"); load_w(Wuv, wuv_d, 256, 1024, "Wuv")
    qan_b = A.alloc([384], F32); kvan_b = A.alloc([256], F32); gq_b = A.alloc([8, 192], F32); gk_b = A.alloc([192], F32)
    dma("sp", qan_b, qan_d.partition_broadcast(128), [], ["qan"])
    dma("sp", kvan_b, kvan_d.partition_broadcast(128), [], ["kvan"])
    dma("sp", gq_b[:, 0, :], qn_d.partition_broadcast(128), [], ["gq"])
    dma("sp", gk_b, kn_d.partition_broadcast(128), [], ["gk"])
    tt("dve", gq_b[:, 0, 0:128], gq_b[:, 0, 0:128], gk_b[:, 0:128], ALU.mult, ["gq", "gk"], ["gq"])
    for h in range(1, 8):
        cp("pool", gq_b[:, h, :], gq_b[:, 0, :], ["gq"], ["gq"])
    cqnT = A.alloc([3, NO], BF16)
    hT2 = A.alloc([8, 128], BF16)
    lat = A.alloc([704], F32); latn = A.alloc([640], BF16)
    sqa = A.alloc([256], F32); sqk = A.alloc([1024], F32); sqq = A.alloc([1536], F32); sqc = A.alloc([384], F32)
    sqr = A.alloc([64], F32)
    ssm = A.alloc([48], F32)
    cs = A.alloc([2, 64], F32)
    kr32 = A.alloc([64], F32); krt = A.alloc([64], F32); krtmp = A.alloc([64], F32); krb = A.alloc([64], BF16)
    q32 = A.alloc([8, 192], F32); qrt = A.alloc([8, 64], F32); qtmp = A.alloc([8, 64], F32); qbf = A.alloc([8, 192], BF16)
    qTs = A.alloc([8, 2, 128], BF16)

    def rope(dst, src, tmp, cosv, sinv, nh, rd, nm):
        s5 = src.rearrange("p h (a c f) -> p h a c f", a=2, c=2)
        t5 = tmp.rearrange("p h (a c f) -> p h a c f", a=2, c=2)
        sn5 = sinv.rearrange("p (a c f) -> p a c f", a=2, c=2)
        for c in range(2):
            for a_ in range(2):
                tt("dve", t5[:, :, a_, c, :], s5[:, :, a_, 1 - c, :], bc(sn5[:, a_, c, :].unsqueeze(1), [128, nh, 16]), ALU.mult,
                   rd, [nm + "tmp"])
        tt("dve", dst, src, bc(cosv.unsqueeze(1), [128, nh, 64]), ALU.mult, rd, [nm])
        tt("dve", dst, dst, tmp, ALU.add, [nm, nm + "tmp"], [nm])

    for g in range(NG):
        rows, v = tile_rows(g)
        is_own = g in own
        make_hT(rows, v, hT2, "hT2")
        ncol = 704 if is_own else 320
        c_lo = 0 if is_own else 384
        b0 = nb()
        w0 = min(512, ncol)
        for k in range(8):
            mm(pf(b0)[:, 0:w0], hT2[:, k, :], Wmla[:, k, c_lo:c_lo + w0], k == 0, k == 7, ["hT2", "Wmla"], b0)
        cp("dve", lat[:, c_lo:c_lo + w0], pf(b0)[:, 0:w0], [PB(b0)], ["lat"])
        if ncol > 512:
            b1 = nb()
            for k in range(8):
                mm(pf(b1)[:, 0:ncol - 512], hT2[:, k, :], Wmla[:, k, 512:ncol], k == 0, k == 7, ["hT2", "Wmla"], b1)
            cp("act", lat[:, 512:ncol], pf(b1)[:, 0:ncol - 512], [PB(b1)], ["lat"])
        if sub == 1:
            return nc, P.emit(nc)
        dma("sp", cs[:, 0, :], cos_d[g * 128:(g + 1) * 128, :], [], ["cs"])
        dma("sp", cs[:, 1, :], sin_d[g * 128:(g + 1) * 128, :], [], ["cs"])
        act(sqa, lat[:, 384:640], AF.Square, ["lat"], ["sqa", "ss0"], accum=ssm[:, 0:1])
        rsqrt_small(ssm[:, 1:2], ssm[:, 0:1], 1.0 / 256, ["ss0"], ["ss1k"], ssm[:, 2:3], "ss2k")
        stt(latn[:, 384:640], lat[:, 384:640], ssm[:, 1:2], kvan_b, ALU.mult, ALU.mult, ["lat", "ss1k", "kvan"], ["latn_kv"])
        b = nb()
        for j in range(2):
            tr(pb16(b)[:, j * 128:(j + 1) * 128], latn[:, 384 + j * 128:384 + (j + 1) * 128], identB, ["latn_kv", "cB"], b)
        cp("act", ckvnT[:, :, g * 128:(g + 1) * 128], pb16(b)[:, 0:256].rearrange("p (a t) -> p a t", a=2), [PB(b)], ["ckvnT"])
        if sub == 2:
            return nc, P.emit(nc)
        for half in range(2):
            b = nb()
            for kk in range(2):
                mm(pf(b), ckvnT[:, kk, g * 128:(g + 1) * 128], Wuk[:, kk, half * 512:(half + 1) * 512], kk == 0, kk == 1, ["ckvnT", "Wuk"], b)
            act(sqk[:, half * 512:(half + 1) * 512], pf(b), AF.Square, [PB(b)], ["sqk"])
        red(ssm[:, 8:16], sqk.rearrange("p (h d) -> p h d", h=8), ["sqk"], ["ss8"])
        act(sqr, lat[:, 640:704], AF.Square, ["lat"], ["sqr", "ss3"], accum=ssm[:, 3:4])
        ts("dve", ssm[:, 8:16], ssm[:, 8:16], ssm[:, 3:4], ALU.add, ["ss8", "ss3"], ["ss8"])
        act(ssm[:, 16:24], ssm[:, 8:16], AF.Ln, ["ss8"], ["ss16"], scale=1.0 / 192, bias=cols[:, 0:1])
        act(rstdk[:, g, :], ssm[:, 16:24], AF.Exp, ["ss16"], ["rstdk"], scale=-0.5, bias=cols[:, 4:5])
        if sub == 3:
            return nc, P.emit(nc)
        tt("dve", kr32, lat[:, 640:704], gk_b[:, 128:192], ALU.mult, ["lat", "gk"], ["kr32"])
        rope(krt.unsqueeze(1), kr32.unsqueeze(1), krtmp.unsqueeze(1), cs[:, 0, :], cs[:, 1, :], 1, ["kr32", "cs"], "krt")
        cp("dve", krb, krt, ["krt"], ["krb"])
        b = nb()
        tr(pb16(b)[0:64, 0:128], krb, identB, ["krb", "cB"], b)
        cp("act", krT[0:64, g * 128:(g + 1) * 128], pb16(b)[0:64, 0:128], [PB(b)], ["krT"])
        if sub == 4:
            return nc, P.emit(nc)
        if is_own:
            ot = own[g]
            act(sqc, lat[:, 0:384], AF.Square, ["lat"], ["sqc", "ss4"], accum=ssm[:, 4:5])
            rsqrt_small(ssm[:, 5:6], ssm[:, 4:5], 1.0 / 384, ["ss4"], ["ss5"], ssm[:, 6:7], "ss6")
            stt(latn[:, 0:384], lat[:, 0:384], ssm[:, 5:6], qan_b, ALU.mult, ALU.mult, ["lat", "ss5", "qan"], ["latn_q"])
            b = nb()
            for j in range(3):
                tr(pb16(b)[:, j * 128:(j + 1) * 128], latn[:, j * 128:(j + 1) * 128], identB, ["latn_q", "cB"], b)
            cp("act", cqnT[:, :, ot * 128:(ot + 1) * 128], pb16(b)[:, 0:384].rearrange("p (a t) -> p a t", a=3), [PB(b)], ["cqnT"])
            if sub == 5:
                return nc, P.emit(nc)
            q2 = q32.rearrange("p h d -> p (h d)")
            for j in range(3):
                b = nb()
                for kk in range(3):
                    mm(pf(b), cqnT[:, kk, ot * 128:(ot + 1) * 128], Wuq[:, kk, j * 512:(j + 1) * 512], kk == 0, kk == 2, ["cqnT", "Wuq"], b)
                cp("dve", q2[:, j * 512:(j + 1) * 512], pf(b), [PB(b)], ["q32"])
                act(sqq[:, j * 512:(j + 1) * 512], q2[:, j * 512:(j + 1) * 512], AF.Square, ["q32"], ["sqq"])
            red(ssm[:, 24:32], sqq.rearrange("p (h d) -> p h d", h=8), ["sqq"], ["ss24"])
            act(ssm[:, 32:40], ssm[:, 24:32], AF.Ln, ["ss24"], ["ss32"], scale=1.0 / 192, bias=cols[:, 0:1])
            act(ssm[:, 32:40], ssm[:, 32:40], AF.Exp, ["ss32"], ["ss32"], scale=-0.5)
            tt("dve", q32, q32, bc(ssm[:, 32:40].unsqueeze(2), [128, 8, 192]), ALU.mult, ["q32", "ss32"], ["q32"])
            tt("dve", q32, q32, gq_b, ALU.mult, ["q32", "gq"], ["q32"])
            rope(qrt, q32[:, :, 128:192], qtmp, cs[:, 0, :], cs[:, 1, :], 8, ["q32", "cs"], "qrt")
            cp("dve", qbf[:, :, 0:128], q32[:, :, 0:128], ["q32"], ["qbfn"])
            cp("pool", qbf[:, :, 128:192], qrt, ["qrt"], ["qbfr"])
            if sub == 6:
                return nc, P.emit(nc)
            for hh in range(2):
                bn_ = nb()
                for j in range(4):
                    h = hh * 4 + j
                    tr(pb16(bn_)[:, j * 128:(j + 1) * 128], qbf[:, h, 0:128], identB, ["qbfn", "cB"], bn_)
                cp("act", qTs[:, hh * 4:(hh + 1) * 4, 0, :], pb16(bn_)[:, 0:512].rearrange("p (a t) -> p a t", a=4), [PB(bn_)], ["qTs%d" % hh])
                br_ = nb()
                for j in range(4):
                    h = hh * 4 + j
                    tr(pb16(br_)[0:64, j * 128:(j + 1) * 128], qbf[:, h, 128:192], identB, ["qbfr", "cB"], br_)
                cp("dve", qTs[0:64, hh * 4:(hh + 1) * 4, 1, :], pb16(br_)[0:64, 0:512].rearrange("p (a t) -> p a t", a=4), [PB(br_)], ["qTr%d" % hh])
            if sub == 7:
                return nc, P.emit(nc)
            dma("pool", qt_d[:, 0:128, ot * 128:(ot + 1) * 128].rearrange("h d t -> d h t"), qTs[:, :, 0, :], ["qTs0", "qTs1"], ["qTs0", "qTs1", "qt_d"])
            dma("pool", qt_d[:, 128:192, ot * 128:(ot + 1) * 128].rearrange("h d t -> d h t"), qTs[0:64, :, 1, :], ["qTr0", "qTr1"], ["qTr0", "qTr1", "qt_d"])

    if stop == 3:
        return nc, P.emit(nc)
    P.barrier()
    A.release(m2b)
    m3 = A.mark()
    KnT = A.alloc([NT], BF16); Vh = A.alloc([NG, 128], BF16)
    QTn = A.alloc([NO], BF16); QTr = A.alloc([NO], BF16)
    PT = [A.alloc([512], BF16) for _ in range(3)]
    rec = A.alloc([512], F32); acc = A.alloc([512], F32)
    oT = [A.alloc([512], BF16) for _ in range(2)]
    NQB = NO // 512
    NKB = (NT + 511) // 512
    bank_state.update(n=0, lo=0, hi=4)
    for h in range(8):
        dma("sp", QTn, qt_d[h, 0:128, :], ["qt_d"], ["QTn"])
        dma("sp", QTr[0:64, :], qt_d[h, 128:192, :], ["qt_d"], ["QTr"])
        for kb in range(NKB):
            c0, c1 = kb * 512, min(NT, (kb + 1) * 512)
            b = nb()
            for kk in range(2):
                mm(pf(b)[:, 0:c1 - c0], Wuk[:, kk, h * 128:(h + 1) * 128], ckvnT[:, kk, c0:c1], kk == 0, kk == 1, ["Wuk", "ckvnT"], b)
            cp("dve" if kb % 2 else "act", KnT[:, c0:c1], pf(b)[:, 0:c1 - c0], [PB(b)], ["KnT"])
        for g4 in range(0, NG, 4):
            ng = min(4, NG - g4)
            b = nb()
            for j in range(ng):
                for kk in range(2):
                    mm(pf(b)[:, j * 128:(j + 1) * 128], ckvnT[:, kk, (g4 + j) * 128:(g4 + j + 1) * 128], Wuv[:, kk, h * 128:(h + 1) * 128],
                       kk == 0, kk == 1, ["ckvnT", "Wuv"], b)
            cp("dve", Vh[:, g4:g4 + ng, :], pf(b)[:, 0:ng * 128].rearrange("p (a t) -> p a t", a=ng), [PB(b)], ["Vh"])
        for qb in range(NQB):
            qs = slice(qb * 512, (qb + 1) * 512)
            ba, bs_ = (4, 5) if (h * NQB + qb) % 2 == 0 else (6, 7)

            def st(g):
                b = nb()
                mm(pf(b), KnT[:, g * 128:(g + 1) * 128], QTn[:, qs], True, False, ["KnT", "QTn"], b)
                mm(pf(b), krT[0:64, g * 128:(g + 1) * 128], QTr[0:64, qs], False, True, ["krT", "QTr"], b)
                return b
            bcur = st(0)
            for g in range(NG):
                bnext = st(g + 1) if g + 1 < NG else None
                pt = PT[g % 3]
                ptn = "PT%d" % (g % 3)
                act(pt, pf(bcur), AF.Exp, [PB(bcur), "rstdk"], [ptn], scale=rstdk[:, g, h:h + 1])
                mm(pf(ba), Vh[:, g, :], pt, g == 0, g == NG - 1, ["Vh", ptn], ba)
                if g == 0:
                    cp("dve", acc, pt, [ptn], ["acc"])
                else:
                    tt("dve", acc, acc, pt, ALU.add, ["acc", ptn], ["acc"])
                bcur = bnext
            mm(pf(bs_), onesF, acc, True, True, ["cF", "acc"], bs_)
            P.op("dve", lambda e, bs_=bs_: e.reciprocal(out=rec, in_=pf(bs_)), reads=[PB(bs_)], writes=["rec"])
            o_ = oT[qb % 2]
            on = "oT%d" % (qb % 2)
            tt("dve", o_, pf(ba), rec, ALU.mult, [PB(ba), "rec"], [on])
            dma("pool", om_d[h, :, qs], o_, [on], [on, "om_d"])
    bank_state.update(n=0, lo=0, hi=8)
    P.barrier()
    A.release(m3)
    A.release(mm_)

    if stop == 4:
        return nc, P.emit(nc)
    m4 = A.mark()
    Wzg = A.alloc([8, 3072], BF16); Wg = A.alloc([8, D], BF16); Wmm = A.alloc([8, D], BF16); Wo = A.alloc([8, D], BF16)
    load_w(Wzg, wzg_d, D, 3072, "Wzg"); load_w(Wg, wg_d, D, D, "Wg"); load_w(Wmm, wm_d, D, D, "Wmm"); load_w(Wo, wo_d, D, D, "Wo")
    modb = A.alloc([2, D], F32)
    dgm = A.alloc([128], F32)
    for vi, e_ in enumerate((2, 5)):
        for t in range(8):
            ts("dve", dgm, identF, modF[:, e_, 0, t:t + 1], ALU.mult, ["cF", "modF"], ["dgm"])
            b = nb()
            mm(pf(b)[:, 0:128], onesF, dgm, True, True, ["cF", "dgm"], b)
            cp("act", modb[:, vi, t * 128:(t + 1) * 128], pf(b)[:, 0:128], [PB(b)], ["modb"])
    gon_b = A.alloc([128], F32)
    dma("sp", gon_b, gon_d.partition_broadcast(128), [], ["gon"])
    hT4 = A.alloc([8, 128], BF16)
    xk = A.alloc([D], F32)
    zs = A.alloc([D], F32); sg = A.alloc([2 * D], F32)
    ofb = A.alloc([D], F32); obb = A.alloc([D], F32)
    ssg = A.alloc([24], F32)
    yg = A.alloc([D], BF16); ygT = A.alloc([8, 128], BF16); omT = A.alloc([8, 128], BF16)
    t32 = A.alloc([D], F32); t32b = A.alloc([D], F32); ybf = A.alloc([D], BF16); yT = A.alloc([8, 128], BF16)
    x1t = [A.alloc([D], F32) for _ in range(2)]
    for ot in range(NTO):
        rows = x_d[ot * 128:(ot + 1) * 128, :]
        make_hT(rows, 0, hT4, "hT4", xt_keep=xk)
        dma("sp", ofb, of_d[ot * 128:(ot + 1) * 128, :], [], ["ofb"])
        dma("sp", obb, ob_d[ot * 128:(ot + 1) * 128, :], [], ["obb"])
        dma("sp", omT, om_d[:, :, ot * 128:(ot + 1) * 128].rearrange("h d t -> d h t"), [], ["omT"])
        for j in range(6):
            b = nb()
            for k in range(8):
                mm(pf(b), hT4[:, k, :], Wzg[:, k, j * 512:(j + 1) * 512], k == 0, k == 7, ["hT4", "Wzg"], b)
            if j < 2:
                act(zs[:, j * 512:(j + 1) * 512], pf(b), AF.Silu, [PB(b)], ["zs"])
            else:
                act(sg[:, (j - 2) * 512:(j - 1) * 512], pf(b), AF.Sigmoid, [PB(b)], ["sg"])
        tt("pool", ofb, ofb, obb, ALU.add, ["ofb", "obb"], ["ofb"])
        tt("pool", t32, ofb, ofb, ALU.mult, ["ofb"], ["t32"])
        red(ssg[:, 0:8], t32.rearrange("p (h d) -> p h d", h=8), ["t32"], ["ssg0"])
        act(ssg[:, 8:16], ssg[:, 0:8], AF.Ln, ["ssg0"], ["ssg8"], scale=1.0 / 128, bias=cols[:, 0:1])
        act(ssg[:, 8:16], ssg[:, 8:16], AF.Exp, ["ssg8"], ["ssg8"], scale=-0.5)
        o3 = ofb.rearrange("p (h d) -> p h d", h=8)
        tt("dve", o3, o3, bc(ssg[:, 8:16].unsqueeze(2), [128, 8, 128]), ALU.mult, ["ofb", "ssg8"], ["ofb"])
        tt("dve", o3, o3, bc(gon_b.unsqueeze(1), [128, 8, 128]), ALU.mult, ["ofb", "gon"], ["ofb"])
        tt("dve", yg, ofb, zs, ALU.mult, ["ofb", "zs"], ["yg"])
        b = nb()
        for k in range(8):
            tr(pb16(b)[:, k * 128:(k + 1) * 128], yg[:, k * 128:(k + 1) * 128], identB, ["yg", "cB"], b)
        cp("act", ygT, pb16(b).rearrange("p (a t) -> p a t", a=8), [PB(b)], ["ygT"])
        for half in range(2):
            hs_ = slice(half * 512, (half + 1) * 512)
            b1 = nb()
            for k in range(8):
                mm(pf(b1), ygT[:, k, :], Wg[:, k, hs_], k == 0, k == 7, ["ygT", "Wg"], b1)
            b2 = nb()
            for k in range(8):
                mm(pf(b2), omT[:, k, :], Wmm[:, k, hs_], k == 0, k == 7, ["omT", "Wmm"], b2)
            tt("dve", t32[:, hs_], pf(b1), sg[:, hs_], ALU.mult, [PB(b1), "sg"], ["t32"])
            tt("dve", t32b[:, hs_], pf(b2), sg[:, D + half * 512:D + (half + 1) * 512], ALU.mult, [PB(b2), "sg"], ["t32b"])
            tt("pool", ybf[:, hs_], t32[:, hs_], t32b[:, hs_], ALU.add, ["t32", "t32b"], ["ybf"])
        b = nb()
        for k in range(8):
            tr(pb16(b)[:, k * 128:(k + 1) * 128], ybf[:, k * 128:(k + 1) * 128], identB, ["ybf", "cB"], b)
        cp("act", yT, pb16(b).rearrange("p (a t) -> p a t", a=8), [PB(b)], ["yT"])
        xo = x1t[ot % 2]
        xon = "x1t%d" % (ot % 2)
        for half in range(2):
            hs_ = slice(half * 512, (half + 1) * 512)
            b = nb()
            for k in range(8):
                mm(pf(b), yT[:, k, :], Wo[:, k, hs_], k == 0, k == 7, ["yT", "Wo"], b)
            tt("dve", t32[:, hs_], pf(b), modb[:, 0, hs_], ALU.mult, [PB(b), "modb"], ["t32"])
            tt("pool", xo[:, hs_], t32[:, hs_], xk[:, hs_], ALU.add, ["t32", "hT4xt"], [xon])
        dma("pool", x1_d[ot * 128:(ot + 1) * 128, :], xo, [xon], [xon, "x1_d"])
    P.barrier()
    A.release(m4)

    if stop == 5:
        return nc, P.emit(nc)
    W1 = A.alloc([8, 4 * D], BF16); W2 = A.alloc([32, D], BF16)
    load_w(W1, w1_d, D, 4 * D, "W1"); load_w(W2, w2_d, 4 * D, D, "W2")
    modb2 = A.alloc([D], F32)
    dgm2 = A.alloc([128], F32)
    for t in range(8):
        ts("dve", dgm2, identF, modF[:, 5, 0, t:t + 1], ALU.mult, ["cF", "modF"], ["dgm2"])
        b = nb()
        mm(pf(b)[:, 0:128], onesF, dgm2, True, True, ["cF", "dgm2"], b)
        cp("act", modb2[:, t * 128:(t + 1) * 128], pf(b)[:, 0:128], [PB(b)], ["modb2"])
    hT5 = A.alloc([8, 128], BF16)
    xk2 = [A.alloc([D], F32) for _ in range(2)]
    rl = A.alloc([4, 128], BF16); aT = A.alloc([32, 128], BF16)
    t5 = A.alloc([D], F32); outt = [A.alloc([D], F32) for _ in range(2)]
    for ot in range(NTO):
        xkk = xk2[ot % 2]
        make_hT(x1_d[ot * 128:(ot + 1) * 128, :], 0, hT5, "hT5x%d" % (ot % 2), G=G2, SH=SH2, xt_keep=xkk)
        for f4 in range(8):
            b = nb()
            for j in range(4):
                f = f4 * 4 + j
                for k in range(8):
                    mm(pf(b)[:, j * 128:(j + 1) * 128], W1[:, k, f * 128:(f + 1) * 128], hT5[:, k, :], k == 0, k == 7, ["W1", "hT5x%d" % (ot % 2)], b)
            act(rl, pf(b).rearrange("p (a t) -> p a t", a=4), AF.Relu, [PB(b)], ["rl"])
            tt("pool", aT[:, f4 * 4:(f4 + 1) * 4, :], rl, rl, ALU.mult, ["rl"], ["aT"])
        oo = outt[ot % 2]
        for half in range(2):
            hs_ = slice(half * 512, (half + 1) * 512)
            b = nb()
            for f in range(32):
                mm(pf(b), aT[:, f, :], W2[:, f, hs_], f == 0, f == 31, ["aT", "W2"], b)
            tt("dve", t5[:, hs_], pf(b), modb2[:, hs_], ALU.mult, [PB(b), "modb2"], ["t5%d" % half])
            tt("pool", oo[:, hs_], t5[:, hs_], xkk[:, hs_], ALU.add, ["t5%d" % half, "hT5x%dxt" % (ot % 2)], ["oo%d%d" % (ot % 2, half)])
        dma("pool", out_d[ot * 128:(ot + 1) * 128, :], oo, ["oo%d0" % (ot % 2), "oo%d1" % (ot % 2)], ["oo%d0" % (ot % 2), "oo%d1" % (ot % 2)])
    nops = P.emit(nc)
    return nc, nops


def _consts():
    i = np.arange(128)
    P_, Q_ = np.meshgrid(i, i, indexing="ij")
    c = np.zeros((NCONST, 128, 128), np.float32)
    c[0] = np.eye(128); c[1] = 1.0
    c[2] = (P_ <= Q_); c[3] = (P_ > Q_); c[4] = (Q_ >= P_); c[5] = (Q_ > P_)
    c[13] = (P_ >= Q_); c[14] = (P_ < Q_); c[15] = (Q_ <= P_); c[16] = (Q_ < P_)
    for l in range(7):
        bs = 2 ** (l + 1)
        same = (P_ // bs) == (Q_ // bs)
        c[6 + l] = same & ((Q_ % bs) >= bs // 2) & ((P_ % bs) < bs // 2)
        c[17 + l] = same & ((P_ % bs) >= bs // 2) & ((Q_ % bs) < bs // 2)
    return np.ascontiguousarray(c.transpose(1, 0, 2).reshape(128, NCONST * 128))


def _fm(v, n):
    return np.ascontiguousarray(np.asarray(v, np.float32).reshape(n, 128).T)


def _core_inputs(inp, b, s, NL, NC):
    f32 = lambda a: np.ascontiguousarray(np.asarray(a, np.float32))
    w_in = np.asarray(inp["w_in"][0], np.float32)
    o_z, o_b, o_d, o_cq, o_ckv, o_kr, o_g = 3072, 4096, 4112, 4128, 4512, 4768, 4832
    x = np.asarray(inp["x"][b], np.float32)[:NL]
    ctx = np.asarray(inp["ctx"][b], np.float32)[:NC]
    conv = np.asarray(inp["conv_qkv"][0], np.float32)
    dF, dB = (0, 1) if s == 0 else (1, 0)
    pos = np.arange(NL)
    if s == 1:
        x = x[::-1]; ctx = ctx[::-1]; conv = conv[::-1]; pos = pos[::-1]
    beta = lambda d: w_in[:, o_b + 8 * d:o_b + 8 * d + 8]
    dec = lambda d: w_in[:, o_d + 8 * d:o_d + 8 * d + 8]
    w_bd = np.concatenate([beta(dF), dec(dF), beta(dB), dec(dB)], axis=1)
    inv = (10000.0 ** (-np.arange(16, dtype=np.float32) / np.float32(16))).astype(np.float32)
    row = (pos // GRID_W).astype(np.float32); col = (pos % GRID_W).astype(np.float32)
    ang = np.stack([row[:, None] * inv[None, :], col[:, None] * inv[None, :]], axis=1).astype(np.float32)
    cs_l = np.cos(ang).astype(np.float32); sn_l = np.sin(ang).astype(np.float32)
    cos_t = np.ones((NC + NL, 2, 2, 16), np.float32); sin_t = np.zeros((NC + NL, 2, 2, 16), np.float32)
    cos_t[NC:, :, 0, :] = cs_l; cos_t[NC:, :, 1, :] = cs_l
    sin_t[NC:, :, 0, :] = -sn_l; sin_t[NC:, :, 1, :] = sn_l
    w_ukv = np.asarray(inp["w_ukv"][0], np.float32).reshape(256, 8, 256)
    return {
        "x": f32(x), "ctx": f32(ctx),
        "cvec": f32(np.concatenate([_fm(inp["c"][b], 8), _fm(inp["c_ctx"], 8)], axis=1)),
        "w_mod": f32(inp["w_mod"][0]), "b_mod": _fm(inp["b_mod"][0], 48),
        "norm_attn": _fm(inp["norm_attn"][0], 8), "norm_mlp": _fm(inp["norm_mlp"][0], 8),
        "w_qkv": f32(w_in[:, 0:3072]), "w_zg": f32(np.concatenate([w_in[:, o_z:o_b], w_in[:, o_g:o_g + 2048]], axis=1)),
        "w_bd": f32(w_bd), "w_mla": f32(w_in[:, o_cq:o_g]),
        "conv_w": f32(conv.T.reshape(24, 128, 5).transpose(1, 0, 2).reshape(128, 120)),
        "a_log": f32(np.concatenate([inp["gdn_a_log"][0][dF], inp["gdn_a_log"][0][dB]])),
        "dt_bias": f32(np.concatenate([inp["gdn_dt_bias"][0][dF], inp["gdn_dt_bias"][0][dB]])),
        "gdn_out_norm": f32(inp["gdn_out_norm"][0]), "q_a_norm": f32(inp["mla_q_a_norm"][0]), "kv_a_norm": f32(inp["mla_kv_a_norm"][0]),
        "q_norm": f32(inp["q_norm"][0]), "k_norm": f32(inp["k_norm"][0]),
        "w_uq": f32(inp["w_uq"][0]), "w_uk": f32(w_ukv[:, :, 0:128].reshape(256, 1024)), "w_uv": f32(w_ukv[:, :, 128:256].reshape(256, 1024)),
        "w_bg": f32(inp["w_branch_gdn"][0]), "w_bm": f32(inp["w_branch_mla"][0]), "w_out": f32(inp["w_out"][0]),
        "w_mlp_in": f32(inp["w_mlp_in"][0]), "w_mlp_out": f32(inp["w_mlp_out"][0]),
        "consts": _consts(), "rope_cos": f32(cos_t.reshape(NC + NL, 64)), "rope_sin": f32(sin_t.reshape(NC + NL, 64)),
    }


_NC_CACHE = {}


def run(inp, B, NL, NC):
    key = (NL, NC)
    if key not in _NC_CACHE:
        _NC_CACHE[key] = build_nc(NL, NC)[0]
    nc = _NC_CACHE[key]
    cores = [(b, s) for b in range(B) for s in range(2)]
    in_maps = [_core_inputs(inp, b, s, NL, NC) for (b, s) in cores]
    res = run_bass_kernel_spmd(nc, in_maps, core_ids=list(range(len(cores))))
    NO = NL // 2
    out = np.zeros((B, NL, D), np.float32)
    for (b, s), r in zip(cores, res.results):
        o = np.asarray(r["out"], np.float32)
        if s == 0:
            out[b, :NO] = o
        else:
            out[b, NO:] = o[::-1]
    return out


def kernel(**inputs):
    inp = {k: np.asarray(v) for k, v in inputs.items()}
    B, NL, _ = inp["x"].shape
    NC = inp["ctx"].shape[1]
    return run(inp, B, NL, NC)
```

```python
import math
import numpy as np
import ml_dtypes
import concourse.bass as bass
import concourse.mybir as mybir
from concourse.bass_utils import run_bass_kernel_spmd

F32 = mybir.dt.float32
BF16 = mybir.dt.bfloat16
AF = mybir.ActivationFunctionType
ALU = mybir.AluOpType
AX = mybir.AxisListType

D = 1024
KD = 8
H = 8
EPS = 1e-6
GRID_W = 64
NCONST = 24


class Buf:
    __slots__ = ("name", "writer", "readers")

    def __init__(self, name):
        self.name = name
        self.writer = None
        self.readers = []


class Op:
    __slots__ = ("eng", "fn", "deps", "signal", "tok", "is_dma", "dsem")

    def __init__(self, eng, fn, is_dma=False):
        self.eng = eng
        self.fn = fn
        self.deps = []
        self.signal = False
        self.tok = None
        self.is_dma = is_dma
        self.dsem = None


class Prog:
    ENGS = ("pe", "act", "dve", "pool", "sp")
    NDSEM = 12

    def __init__(self):
        self.ops = []
        self.bufs = {}
        self.last = {}
        self.dma_since = []

    def _B(self, x):
        b = self.bufs.get(x)
        if b is None:
            b = Buf(x)
            self.bufs[x] = b
        return b

    def op(self, eng, fn, reads=(), writes=(), is_dma=False):
        idx = len(self.ops)
        o = Op(eng, fn, is_dma)
        deps = set()
        for r in reads:
            r = self._B(r)
            if r.writer is not None:
                deps.add(r.writer)
        for w in writes:
            w = self._B(w)
            if w.writer is not None:
                deps.add(w.writer)
            deps.update(w.readers)
        fin = []
        for d in deps:
            od = self.ops[d]
            if od.eng == eng and eng == "pe" and not od.is_dma and not is_dma:
                continue
            fin.append(d)
            od.signal = True
        o.deps = sorted(fin)
        self.ops.append(o)
        for r in reads:
            rb = self._B(r)
            if not is_dma:
                rb.readers = [q for q in rb.readers if self.ops[q].is_dma or self.ops[q].eng != eng]
            rb.readers.append(idx)
        for w in writes:
            w = self._B(w)
            w.writer = idx
            w.readers = []
        self.last[eng] = idx
        if is_dma:
            self.dma_since.append(idx)
        return idx

    def barrier(self):
        deps = sorted(set(list(self.last.values()) + self.dma_since))
        for d in deps:
            self.ops[d].signal = True
        for e in self.ENGS:
            o = Op(e, None)
            o.deps = list(deps)
            self.ops.append(o)
        self.dma_since = []
        for b in self.bufs.values():
            b.writer = None
            b.readers = []

    def emit(self, nc):
        sems = {e: nc.alloc_semaphore("S_" + e) for e in self.ENGS}
        dq = ("sp", "pool", "act")
        dsems = {e: [nc.alloc_semaphore("D_%s_%d" % (e, j)) for j in range(self.NDSEM)] for e in dq}
        cnt = {e: 0 for e in self.ENGS}
        dcnt = {e: [0] * self.NDSEM for e in dq}
        dnext = {e: 0 for e in dq}
        for o in self.ops:
            if o.is_dma:
                j = dnext[o.eng]
                dnext[o.eng] = (j + 1) % self.NDSEM
                prev = dcnt[o.eng][j]
                dcnt[o.eng][j] += 16
                o.dsem = (dsems[o.eng][j], prev)
                o.tok = (dsems[o.eng][j], dcnt[o.eng][j])
            elif o.signal and o.fn is not None:
                cnt[o.eng] += 1
                o.tok = (sems[o.eng], cnt[o.eng])
        per = {e: [] for e in self.ENGS}
        for o in self.ops:
            per[o.eng].append(o)
        ops = self.ops
        tail = [o for o in ops if o.is_dma]

        def run(ename):
            def body(eng):
                seen = {}

                def wait(tok):
                    if tok is None:
                        return
                    s, v = tok
                    k = id(s)
                    if seen.get(k, 0) >= v:
                        return
                    seen[k] = v
                    eng.wait_ge(s, v)

                for o in per[ename]:
                    for d in o.deps:
                        wait(ops[d].tok)
                    if o.fn is None:
                        continue
                    if o.is_dma:
                        s, prev = o.dsem
                        if prev > 0:
                            wait((s, prev))
                        o.fn(eng).then_inc(o.tok[0], 16)
                    else:
                        ins = o.fn(eng)
                        if o.signal:
                            ins.then_inc(o.tok[0], 1)
                if ename == "sp":
                    for o in tail:
                        wait(o.tok)
            return body

        with nc.Block() as block:
            block.tensor(run("pe"))
            block.scalar(run("act"))
            block.vector(run("dve"))
            block.gpsimd(run("pool"))
            block.sync(run("sp"))
        return len(ops)


class Arena:
    def __init__(self, nc, nbytes):
        self.t = nc.alloc_sbuf_tensor("arena", [128, nbytes // 4], F32)
        self.cap = nbytes // 4
        self.off = 0
        self.uid = 0

    def alloc(self, shape, dtype):
        n = 1
        for s in shape:
            n *= s
        words = n if dtype == F32 else (n + 1) // 2
        words = (words + 7) // 8 * 8
        assert self.off + words <= self.cap, "SBUF arena overflow %d+%d>%d" % (self.off, words, self.cap)
        v = self.t[:, self.off:self.off + words]
        self.off += words
        if dtype == BF16:
            v = v.bitcast(BF16)
        v = v[:, 0:n]
        if len(shape) == 2:
            v = v.rearrange("p (a b) -> p a b", a=shape[0])
        elif len(shape) == 3:
            v = v.rearrange("p (a b c) -> p a b c", a=shape[0], b=shape[1])
        self.uid += 1
        return v

    def mark(self):
        return self.off

    def release(self, m):
        self.off = m


def bc(ap, shape):
    return ap.to_broadcast(shape)


def build_nc(NL, NC, dbg=False, stop=99, sub=99):
    NO = NL // 2
    NT = NC + NL
    NTC = NC // 128
    NTL = NL // 128
    NTO = NO // 128
    NG = NTC + NTL
    nc = bass.Bass("TRN2", target_bir_lowering=False)

    def din(name, shape, dt=F32):
        return nc.dram_tensor(name, list(shape), dt, kind="ExternalInput").ap()

    x_d = din("x", [NL, D]); ctx_d = din("ctx", [NC, D])
    cvec_d = din("cvec", [128, 16]); wmod_d = din("w_mod", [D, 6 * D]); bmod_d = din("b_mod", [128, 48])
    nattn_d = din("norm_attn", [128, 8]); nmlp_d = din("norm_mlp", [128, 8])
    wqkv_d = din("w_qkv", [D, 3072]); wzg_d = din("w_zg", [D, 3072]); wbd_d = din("w_bd", [D, 32]); wmla_d = din("w_mla", [D, 704])
    conv_d = din("conv_w", [128, 24 * 5]); alog_d = din("a_log", [16]); dtb_d = din("dt_bias", [16])
    gon_d = din("gdn_out_norm", [128]); qan_d = din("q_a_norm", [384]); kvan_d = din("kv_a_norm", [256])
    qn_d = din("q_norm", [192]); kn_d = din("k_norm", [192])
    wuq_d = din("w_uq", [384, 1536]); wuk_d = din("w_uk", [256, 1024]); wuv_d = din("w_uv", [256, 1024])
    wg_d = din("w_bg", [D, D]); wm_d = din("w_bm", [D, D]); wo_d = din("w_out", [D, D])
    w1_d = din("w_mlp_in", [D, 4 * D]); w2_d = din("w_mlp_out", [4 * D, D])
    const_d = din("consts", [128, NCONST * 128]); cos_d = din("rope_cos", [NT, 64]); sin_d = din("rope_sin", [NT, 64])
    out_d = nc.dram_tensor("out", [NO, D], F32, kind="ExternalOutput").ap()

    def dscr(name, shape, dt):
        return nc.dram_tensor(name, list(shape), dt, kind="Internal").ap()

    of_d = dscr("scr_of", [NO, D], F32); ob_d = dscr("scr_ob", [NO, D], F32)
    qt_d = dscr("scr_qt", [H, 192, NO], BF16); om_d = dscr("scr_om", [H, 128, NO], BF16)
    x1_d = dscr("scr_x1", [NO, D], F32)

    P = Prog()
    A = Arena(nc, 207 * 1024)
    ps = [nc.alloc_psum_tensor("psb%d" % i, [128, 512], F32) for i in range(8)]
    bank_state = {"n": 0, "lo": 0, "hi": 8}

    def nb():
        r = bank_state["hi"] - bank_state["lo"]
        b = bank_state["lo"] + bank_state["n"] % r
        bank_state["n"] += 1
        return b

    def pf(b):
        return ps[b][:]

    def pb16(b):
        return ps[b][:].bitcast(BF16)

    def PB(b):
        return "ps%d" % b

    uid = [0]

    def U(prefix):
        uid[0] += 1
        return "%s#%d" % (prefix, uid[0])

    def dma(q, out, in_, reads, writes):
        P.op(q, lambda e: e.dma_start(out=out, in_=in_), reads=reads, writes=writes, is_dma=True)

    def mm(out, lhsT, rhs, start, stop, reads, bank):
        P.op("pe", lambda e: e.matmul(out, lhsT=lhsT, rhs=rhs, start=start, stop=stop), reads=reads, writes=[PB(bank)])

    def tr(out, in_, ident, reads, bank):
        P.op("pe", lambda e: e.transpose(out, in_, ident), reads=reads, writes=[PB(bank)])

    def act(out, in_, func, reads, writes, scale=None, bias=None, accum=None):
        kw = {}
        if scale is not None:
            kw["scale"] = scale
        if bias is not None:
            kw["bias"] = bias
        if accum is not None:
            kw["accum_out"] = accum
        P.op("act", lambda e: e.activation(out=out, in_=in_, func=func, **kw), reads=reads, writes=writes)

    def tt(eng, out, in0, in1, op, reads, writes):
        P.op(eng, lambda e: e.tensor_tensor(out=out, in0=in0, in1=in1, op=op), reads=reads, writes=writes)

    def ts(eng, out, in0, s1, op0, reads, writes, s2=None, op1=None):
        if op1 is None:
            P.op(eng, lambda e: e.tensor_scalar(out=out, in0=in0, scalar1=s1, scalar2=None, op0=op0), reads=reads, writes=writes)
        else:
            P.op(eng, lambda e: e.tensor_scalar(out=out, in0=in0, scalar1=s1, scalar2=s2, op0=op0, op1=op1), reads=reads, writes=writes)

    def stt(out, in0, scalar, in1, op0, op1, reads, writes):
        P.op("dve", lambda e: e.scalar_tensor_tensor(out=out, in0=in0, scalar=scalar, in1=in1, op0=op0, op1=op1), reads=reads, writes=writes)

    def cp(eng, out, in_, reads, writes):
        if eng == "act":
            act(out, in_, AF.Copy, reads, writes)
        else:
            P.op(eng, lambda e: e.tensor_copy(out=out, in_=in_), reads=reads, writes=writes)

    def red(out, in_, reads, writes):
        P.op("dve", lambda e: e.tensor_reduce(out=out, in_=in_, axis=AX.X, op=ALU.add), reads=reads, writes=writes)

    cF = A.alloc([6, 128], F32)
    cB = A.alloc([NCONST, 128], BF16)
    cols = A.alloc([8], F32)
    stage_all = A.alloc([2048], F32)
    stage = [stage_all[:, 0:1024], stage_all[:, 1024:2048]]
    cvec = A.alloc([16], F32)
    sc = A.alloc([16], F32)
    modF = A.alloc([6, 2, 8], F32)
    bmod = A.alloc([48], F32)
    nrm = A.alloc([2, 8], F32)
    G1 = A.alloc([2, 8], F32); G2 = A.alloc([8], F32)
    m0 = A.mark()
    ctemp = A.alloc([NCONST, 128], F32)
    dma("sp", ctemp.rearrange("p a b -> p (a b)"), const_d, [], ["ctemp"])
    dma("sp", cF[:, 0:4, :].rearrange("p a b -> p (a b)"), const_d[:, 0:4 * 128], [], ["cF"])
    dma("sp", cF[:, 4:6, :].rearrange("p a b -> p (a b)"), const_d[:, 13 * 128:15 * 128], [], ["cF"])
    cp("dve", cB, ctemp, ["ctemp"], ["cB"])
    for j, val in enumerate((EPS, 1.0, math.log(128 ** -0.5), 0.0, math.log(192 ** -0.5))):
        P.op("pool", lambda e, j=j, val=val: e.memset(cols[:, j:j + 1], val), writes=["cols"])
    identF = cF[:, 0, :]; onesF = cF[:, 1, :]
    identB = cB[:, 0, :]; onesB = cB[:, 1, :]
    CF = {"F": 2, "B": 4}
    CB = {"F": 2, "B": 13}

    def rsqrt_small(out, in_, mul, reads, writes, tmp, tmpn):
        act(tmp, in_, AF.Ln, reads, [tmpn], scale=mul, bias=cols[:, 0:1])
        act(out, tmp, AF.Exp, [tmpn], writes, scale=-0.5)

    stg = [0]

    def load_w(dst, src, K, N, name, eng_cycle=("act", "dve")):
        for k in range(K // 128):
            for c0 in range(0, N, 1024):
                c1 = min(N, c0 + 1024)
                i = stg[0] % 2
                stg[0] += 1
                st = stage[i][:, 0:c1 - c0]
                dma("sp", st, src[k * 128:(k + 1) * 128, c0:c1], [], ["stage%d" % i])
                cp(eng_cycle[stg[0] % len(eng_cycle)], dst[:, k, c0:c1], st, ["stage%d" % i], [name])

    dma("sp", cvec, cvec_d, [], ["cvec"])
    dma("sp", bmod, bmod_d, [], ["bmod"])
    dma("sp", nrm[:, 0, :], nattn_d, [], ["nrm"])
    dma("sp", nrm[:, 1, :], nmlp_d, [], ["nrm"])
    act(sc, cvec, AF.Silu, ["cvec"], ["sc"])
    scv = sc.rearrange("p (v k) -> p k v", v=2)
    wst = [A.alloc([8, 512], F32) for _ in range(2)]
    for blk in range(12):
        w = wst[blk % 2]
        dma("sp", w, wmod_d[:, blk * 512:(blk + 1) * 512].rearrange("(k p) n -> p k n", p=128), [], ["wst%d" % (blk % 2)])
        b = nb()
        for t4 in range(4):
            for k in range(8):
                mm(pf(b)[:, t4 * 2:(t4 + 1) * 2], w[:, k, t4 * 128:(t4 + 1) * 128], scv[:, k, :], k == 0, k == 7,
                   ["wst%d" % (blk % 2), "sc"], b)
        e = blk // 2
        t0 = (blk % 2) * 4
        tt("dve", modF[:, e, :, t0:t0 + 4], pf(b)[:, 0:8].rearrange("p (t v) -> p v t", v=2),
           bc(bmod[:, e * 8 + t0:e * 8 + t0 + 4].unsqueeze(1), [128, 2, 4]), ALU.add, [PB(b), "bmod"], ["modF"])
    A.release(m0)
    for v in range(2):
        stt(G1[:, v, :], modF[:, 1, v, :], 1.0, nrm[:, 0, :], ALU.add, ALU.mult, ["modF", "nrm"], ["G1"])
    stt(G2, modF[:, 4, 0, :], 1.0, nrm[:, 1, :], ALU.add, ALU.mult, ["modF", "nrm"], ["G2"])
    SH1 = modF[:, 0, :, :]
    SH2 = modF[:, 3, 0, :]

    P.barrier()

    if stop == 0:
        return nc, P.emit(nc)
    def make_hT(src_rows, v, hT, tag, G=None, SH=None, xt_keep=None):
        xt = xt_keep if xt_keep is not None else A_x[tag_i[0] % 2]
        xn = A_xn
        nm = "xt%d" % (tag_i[0] % 2) if xt_keep is None else tag + "xt"
        tag_i[0] += 1
        dma("sp", xt, src_rows, [], [nm])
        act(A_junk, xt, AF.Square, [nm], ["junk", "ss1"], accum=A_ss[:, 0:1])
        rsqrt_small(A_ss[:, 1:2], A_ss[:, 0:1], 1.0 / D, ["ss1"], ["rs1"], A_ss[:, 2:3], "ss1t")
        act(xn, xt, AF.Copy, [nm, "rs1"], ["xn"], scale=A_ss[:, 1:2])
        b = nb()
        for k in range(8):
            tr(pb16(b)[:, k * 128:(k + 1) * 128], xn[:, k * 128:(k + 1) * 128], identB, ["xn", "cB"], b)
        g = G if G is not None else G1[:, v, :]
        s = SH if SH is not None else SH1[:, v, :]
        tt("dve", A_ht32, pb16(b).rearrange("p (k t) -> p k t", k=8), bc(g.unsqueeze(2), [128, 8, 128]), ALU.mult,
           [PB(b), "G1", "G2"], ["ht32"])
        tt("dve", hT, A_ht32, bc(s.unsqueeze(2), [128, 8, 128]), ALU.add, ["ht32", "modF"], [tag])

    tag_i = [0]
    A_x = [A.alloc([D], F32) for _ in range(2)]
    A_xn = A.alloc([D], BF16)
    A_junk = A.alloc([D], BF16)
    A_ss = A.alloc([8], F32)
    A_ht32 = A.alloc([8, 128], F32)

    def tile_rows(g):
        if g < NTC:
            return ctx_d[g * 128:(g + 1) * 128, :], 1
        t = g - NTC
        return x_d[t * 128:(t + 1) * 128, :], 0

    mg = A.mark()
    Wqkv = A.alloc([8, 3072], BF16)
    Wbd = A.alloc([8, 32], BF16)
    convw = A.alloc([24, 5], F32)
    diagW = A.alloc([120, 128], BF16)
    negA = A.alloc([16], F32); dtb = A.alloc([16], F32)
    load_w(Wqkv, wqkv_d, D, 3072, "Wqkv")
    load_w(Wbd, wbd_d, D, 32, "Wbd")
    dma("sp", convw.rearrange("p a b -> p (a b)"), conv_d, [], ["convw"])
    dma("sp", negA, alog_d.partition_broadcast(128), [], ["negA"])
    dma("sp", dtb, dtb_d.partition_broadcast(128), [], ["dtb"])
    act(negA, negA, AF.Exp, ["negA"], ["negA"])
    ts("dve", negA, negA, -1.0, ALU.mult, ["negA"], ["negA"])
    for ct in range(24):
        for i in range(5):
            ts("pool" if (ct + i) % 2 else "dve", diagW[:, ct * 5 + i, :], identF, convw[:, ct, i:i + 1], ALU.mult,
               ["cF", "convw"], ["diagW"])

    P.barrier()
    hT = [A.alloc([8, 128], BF16) for _ in range(2)]
    Xw = [A.alloc([24, 132], BF16) for _ in range(2)]
    Xw.append(stage_all[:, 0:24 * 132 // 2].bitcast(BF16).rearrange("p (a b) -> p a b", a=24))
    Xe = [A.alloc([24, 4], BF16) for _ in range(4)]
    YT = A.alloc([24, 128], BF16)
    SQ = A.alloc([16, 128], BF16)
    RS = A.alloc([16, 128], F32)
    QKN = A.alloc([16, 128], BF16)
    Ktok = A.alloc([8, 128], BF16); Vtok = A.alloc([8, 128], BF16)
    sca = A.alloc([12, 8], F32)
    Rm = A.alloc([8, 128], F32); Fm = A.alloc([8, 128], F32)
    FB = A.alloc([8, 128], BF16); Fq = A.alloc([8, 128], BF16)
    BF_ = A.alloc([8, 128], BF16); Bl = [A.alloc([8, 128], BF16) for _ in range(2)]
    qkT = A.alloc([8, 128], BF16)
    Dm = [A.alloc([8, 128], BF16) for _ in range(2)]; Wm = [A.alloc([8, 128], BF16) for _ in range(2)]
    ImY = A.alloc([8, 128], BF16)
    Xp = A.alloc([8, 128], BF16); VN = A.alloc([8, 128], BF16); Kd = A.alloc([8, 128], BF16)
    tmpf = Rm
    S32 = A.alloc([8, 128], F32); Sbf = A.alloc([8, 128], BF16)
    o1 = Fm; osb = [A.alloc([8, 128], F32) for _ in range(2)]

    A_lg = [A.alloc([16], F32) for _ in range(3)]

    def gdn_sweep2(dirn, order, out_tiles, o_dram, extra=None):
        c0 = {"F": 2, "B": 13}[dirn]
        Uc = cF[:, CF[dirn], :]; SUc = cF[:, CF[dirn] + 1, :]
        UIm = cB[:, c0 + 2, :]; SUm = cB[:, c0 + 3, :]
        lvl = [cB[:, c0 + 4 + l, :] for l in range(7)]
        dcol = 0 if dirn == "F" else 16
        dsc = 0 if dirn == "F" else 8
        P.op("pool", lambda e: e.memset(S32, 0.0), writes=["S32a", "S32b"])
        P.op("pool", lambda e: e.memset(Sbf, 0.0), writes=["Sbfa", "Sbfb"])
        n_proc = len(order)
        order = list(order) + ([extra] if extra is not None else [])
        n_ord = len(order)
        asc = (dirn == "F")

        def seq_of(g):
            return 0 if g < NTC else 1

        def v4(b):
            return pf(b).rearrange("p (a t) -> p a t", a=4)

        def project(n):
            g = order[n]
            rows, v = tile_rows(g)
            h = hT[n % 2]
            hn = "hT%d" % (n % 2)
            make_hT(rows, v, h, hn)
            yield
            xw = Xw[n % 3]
            xwn = "Xw%d" % (n % 3)
            for c3 in range(8):
                b = nb()
                for j in range(3):
                    ct = c3 * 3 + j
                    for k in range(8):
                        mm(pf(b)[:, j * 128:(j + 1) * 128], Wqkv[:, k, ct * 128:(ct + 1) * 128], h[:, k, :], k == 0, k == 7, ["Wqkv", hn], b)
                cp("act" if c3 % 2 else "dve", xw[:, c3 * 3:(c3 + 1) * 3, 2:130], pf(b)[:, 0:384].rearrange("p (a t) -> p a t", a=3), [PB(b)], [xwn])
                yield
            xe = Xe[n % 4]
            cp("pool", xe[:, :, 0:2], xw[:, :, 2:4], [xwn], ["Xe%d" % (n % 4)])
            cp("pool", xe[:, :, 2:4], xw[:, :, 128:130], [xwn], ["Xe%d" % (n % 4)])
            b = nb()
            for k in range(8):
                mm(pf(b)[:, 0:16], h[:, k, :], Wbd[:, k, dcol:dcol + 16], k == 0, k == 7, [hn, "Wbd"], b)
            cp("act", A_lg[n % 3], pf(b)[:, 0:16], [PB(b)], ["lg%d" % (n % 3)])

        def chunk(m, fill):
            g = order[m]
            want_out = g in out_tiles
            xw = Xw[m % 3]
            xwn = "Xw%d" % (m % 3)
            lg = A_lg[m % 3]
            lgn = "lg%d" % (m % 3)

            def nbr(mm_):
                if mm_ < 0 or mm_ >= n_ord or seq_of(order[mm_]) != seq_of(g):
                    return None
                return Xe[mm_ % 4], "Xe%d" % (mm_ % 4)
            prv = nbr(m - 1) if asc else nbr(m + 1)
            nxt = nbr(m + 1) if asc else nbr(m - 1)
            if prv is not None:
                cp("pool", xw[:, :, 0:2], prv[0][:, :, 2:4], [prv[1], xwn], [xwn])
            else:
                P.op("pool", lambda e, xw=xw: e.memset(xw[:, :, 0:2], 0.0), reads=[xwn], writes=[xwn])
            if nxt is not None:
                cp("pool", xw[:, :, 130:132], nxt[0][:, :, 0:2], [nxt[1], xwn], [xwn])
            else:
                P.op("pool", lambda e, xw=xw: e.memset(xw[:, :, 130:132], 0.0), reads=[xwn], writes=[xwn])
            for c4 in range(6):
                b = nb()
                for j in range(4):
                    ct = c4 * 4 + j
                    o_ = pf(b)[:, j * 128:(j + 1) * 128]
                    for i in range(5):
                        mm(o_, diagW[:, ct * 5 + i, :], xw[:, ct, i:i + 128], i == 0, i == 4, ["diagW", xwn], b)
                act(YT[:, c4 * 4:(c4 + 1) * 4, :], v4(b), AF.Silu, [PB(b)], ["YT"])
            fill()
            tt("pool", SQ, YT[:, 0:16, :], YT[:, 0:16, :], ALU.mult, ["YT"], ["SQ"])
            for c4 in range(4):
                b = nb()
                for j in range(4):
                    mm(pf(b)[:, j * 128:(j + 1) * 128], onesB, SQ[:, c4 * 4 + j, :], True, True, ["cB", "SQ"], b)
                act(RS[:, c4 * 4:(c4 + 1) * 4, :], v4(b), AF.Ln, [PB(b)], ["RS%d" % c4], bias=cols[:, 0:1])
            act(RS[:, 0:8, :], RS[:, 0:8, :], AF.Exp, ["RS0", "RS1"], ["RS0", "RS1"], scale=-0.5, bias=cols[:, 2:3])
            act(RS[:, 8:16, :], RS[:, 8:16, :], AF.Exp, ["RS2", "RS3"], ["RS2", "RS3"], scale=-0.5)
            tt("dve", QKN, YT[:, 0:16, :], RS, ALU.mult, ["YT", "RS0", "RS1", "RS2", "RS3"], ["QKN"])
            bk = nb()
            for h in range(8):
                tr(pb16(bk)[:, h * 128:(h + 1) * 128], QKN[:, 8 + h, :], identB, ["QKN", "cB"], bk)
            bv = nb()
            for h in range(8):
                tr(pb16(bv)[:, h * 128:(h + 1) * 128], YT[:, 16 + h, :], identB, ["YT", "cB"], bv)
            cp("act", Ktok, pb16(bk).rearrange("p (a t) -> p a t", a=8), [PB(bk)], ["Ktok"])
            cp("dve", Vtok, pb16(bv).rearrange("p (a t) -> p a t", a=8), [PB(bv)], ["Vtok"])
            fill()
            beta = sca[:, 0, :]; xx = sca[:, 1, :]; ax = sca[:, 2, :]; g_ = sca[:, 3, :]
            egc = sca[:, 4, :]; negegc = sca[:, 5, :]; gcs = sca[:, 6, :]; edl = sca[:, 7, :]; etot = sca[:, 8, :]
            act(beta, lg[:, 0:8], AF.Sigmoid, [lgn], ["beta"])
            tt("dve", xx, lg[:, 8:16], dtb[:, dsc:dsc + 8], ALU.add, [lgn, "dtb"], ["xx"])
            ts("dve", ax, xx, -1.0, ALU.mult, ["xx"], ["ax"])
            tt("dve", ax, ax, xx, ALU.min, ["xx", "ax"], ["ax"])
            act(ax, ax, AF.Exp, ["ax"], ["ax"])
            act(ax, ax, AF.Ln, ["ax"], ["ax"], bias=cols[:, 1:2])
            ts("dve", xx, xx, 0.0, ALU.max, ["xx"], ["xx"])
            tt("dve", xx, xx, ax, ALU.add, ["xx", "ax"], ["xx"])
            tt("dve", g_, xx, negA[:, dsc:dsc + 8], ALU.mult, ["xx", "negA"], ["g"])
            b = nb()
            mm(pf(b)[:, 0:8], Uc, g_, True, True, ["cF", "g"], b)
            mm(pf(b)[:, 8:16], onesF, g_, True, True, ["cF", "g"], b)
            act(egc, pf(b)[:, 0:8], AF.Exp, [PB(b)], ["egc"])
            act(gcs, pf(b)[:, 0:8], AF.Copy, [PB(b)], ["gcs"])
            act(etot, pf(b)[:, 8:16], AF.Exp, [PB(b)], ["etot"])
            ts("dve", negegc, egc, -1.0, ALU.mult, ["egc"], ["negegc"])
            tt("dve", edl, pf(b)[:, 8:16], gcs, ALU.subtract, [PB(b), "gcs", "etot", "egc"], ["edl"])
            act(edl, edl, AF.Exp, ["edl"], ["edl"])
            fill()
            tt("pool", Rm, bc(Uc.unsqueeze(1), [128, 8, 128]), bc(g_.unsqueeze(2), [128, 8, 128]), ALU.mult, ["cF", "g"], ["Rm"])
            for hh in range(2):
                b = nb()
                for j in range(4):
                    mm(pf(b)[:, j * 128:(j + 1) * 128], SUc, Rm[:, hh * 4 + j, :], True, True, ["cF", "Rm"], b)
                act(Fm[:, hh * 4:(hh + 1) * 4, :], v4(b), AF.Exp, [PB(b)], ["Fm%d" % hh])
            tt("pool", FB, Fm, bc(SUm.unsqueeze(1), [128, 8, 128]), ALU.mult, ["Fm0", "Fm1", "cB"], ["FB"])
            if want_out:
                tt("pool", Fq, Fm, bc(UIm.unsqueeze(1), [128, 8, 128]), ALU.mult, ["Fm0", "Fm1", "cB"], ["Fq"])
            for hh in range(2):
                sl = slice(hh * 4, (hh + 1) * 4)
                b = nb()
                for j in range(4):
                    h = hh * 4 + j
                    mm(pf(b)[:, j * 128:(j + 1) * 128], QKN[:, 8 + h, :], QKN[:, 8 + h, :], True, True, ["QKN"], b)
                tt("dve", BF_[:, sl, :], v4(b), FB[:, sl, :], ALU.mult, [PB(b), "FB"], ["BF%d" % hh])
                if want_out:
                    b2 = nb()
                    for j in range(4):
                        h = hh * 4 + j
                        mm(pf(b2)[:, j * 128:(j + 1) * 128], QKN[:, 8 + h, :], QKN[:, h, :], True, True, ["QKN"], b2)
                    tt("dve", qkT[:, sl, :], v4(b2), Fq[:, sl, :], ALU.mult, [PB(b2), "Fq"], ["qkT%d" % hh])
            tt("pool", Dm[0], bc(identB.unsqueeze(1), [128, 8, 128]), bc(beta.unsqueeze(2), [128, 8, 128]), ALU.mult, ["cB", "beta"], ["D0a", "D0b"])
            hs = ("a", "b")
            for l in range(7):
                fill()
                cur, nx = l % 2, (l + 1) % 2
                Wcur = Dm[0] if l == 0 else Wm[cur]
                wn_ = (lambda hh_: "D0" + hs[hh_]) if l == 0 else (lambda hh_, cur=cur: "W%d%s" % (cur, hs[hh_]))
                Bc = Bl[l % 2]
                bn = "Bl%d" % (l % 2)
                tt("pool", Bc, BF_, bc(lvl[l].unsqueeze(1), [128, 8, 128]), ALU.mult, ["BF0", "BF1", "cB"], [bn])
                for hh in range(2):
                    sl = slice(hh * 4, (hh + 1) * 4)
                    b = nb()
                    for j in range(4):
                        h = hh * 4 + j
                        mm(pf(b)[:, j * 128:(j + 1) * 128], Bc[:, h, :], Dm[cur][:, h, :], True, True, [bn, "D%d%s" % (cur, hs[hh])], b)
                    tt("dve", ImY[:, sl, :], bc(identB.unsqueeze(1), [128, 4, 128]), v4(b), ALU.subtract, [PB(b), "cB"], ["ImY%d" % hh])
                for hh in range(2):
                    sl = slice(hh * 4, (hh + 1) * 4)
                    if l < 6:
                        b = nb()
                        for j in range(4):
                            h = hh * 4 + j
                            mm(pf(b)[:, j * 128:(j + 1) * 128], Wcur[:, h, :], ImY[:, h, :], True, True, [wn_(hh), "ImY%d" % hh], b)
                        cp("act", Dm[nx][:, sl, :], v4(b), [PB(b)], ["D%d%s" % (nx, hs[hh])])
                    b = nb()
                    for j in range(4):
                        h = hh * 4 + j
                        mm(pf(b)[:, j * 128:(j + 1) * 128], ImY[:, h, :], Wcur[:, h, :], True, True, [wn_(hh), "ImY%d" % hh], b)
                    cp("act" if hh else "dve", Wm[nx][:, sl, :], v4(b), [PB(b)], ["W%d%s" % (nx, hs[hh])])
            WT = Wm[1]
            tt("pool", Kd, Ktok, bc(edl.unsqueeze(2), [128, 8, 128]), ALU.mult, ["Ktok", "edl"], ["Kd"])
            for hh in range(2):
                sl = slice(hh * 4, (hh + 1) * 4)
                b = nb()
                for j in range(4):
                    h = hh * 4 + j
                    mm(pf(b)[:, j * 128:(j + 1) * 128], QKN[:, 8 + h, :], Sbf[:, h, :], True, True, ["QKN", "Sbf" + hs[hh]], b)
                tt("dve", tmpf[:, sl, :], v4(b), bc(negegc[:, sl].unsqueeze(2), [128, 4, 128]), ALU.mult, [PB(b), "negegc"], ["Rm"])
                tt("dve", Xp[:, sl, :], tmpf[:, sl, :], Vtok[:, sl, :], ALU.add, ["Rm", "Vtok"], ["Xp%d" % hh])
            for hh in range(2):
                sl = slice(hh * 4, (hh + 1) * 4)
                b = nb()
                for j in range(4):
                    h = hh * 4 + j
                    mm(pf(b)[:, j * 128:(j + 1) * 128], WT[:, h, :], Xp[:, h, :], True, True, ["W1%s" % hs[hh], "Xp%d" % hh], b)
                cp("act", VN[:, sl, :], v4(b), [PB(b)], ["VN%d" % hh])
            if want_out:
                ot = out_tiles[g]
                ob_ = osb[ot % 2]
                obn = "osb%d" % (ot % 2)
                for hh in range(2):
                    sl = slice(hh * 4, (hh + 1) * 4)
                    b = nb()
                    for j in range(4):
                        h = hh * 4 + j
                        mm(pf(b)[:, j * 128:(j + 1) * 128], QKN[:, h, :], Sbf[:, h, :], True, True, ["QKN", "Sbf" + hs[hh]], b)
                    tt("dve", o1[:, sl, :], v4(b), bc(egc[:, sl].unsqueeze(2), [128, 4, 128]), ALU.mult, [PB(b), "egc"], ["Fm%d" % hh])
                    b = nb()
                    for j in range(4):
                        h = hh * 4 + j
                        mm(pf(b)[:, j * 128:(j + 1) * 128], qkT[:, h, :], VN[:, h, :], True, True, ["qkT%d" % hh, "VN%d" % hh], b)
                    tt("dve", ob_[:, sl, :], v4(b), o1[:, sl, :], ALU.add, [PB(b), "Fm%d" % hh], [obn + hs[hh]])
                dma("pool", o_dram[ot * 128:(ot + 1) * 128, :], ob_.rearrange("p a t -> p (a t)"), [obn + "a", obn + "b"], [obn + "a", obn + "b"])
            for hh in range(2):
                sl = slice(hh * 4, (hh + 1) * 4)
                b = nb()
                for j in range(4):
                    h = hh * 4 + j
                    mm(pf(b)[:, j * 128:(j + 1) * 128], Kd[:, h, :], VN[:, h, :], True, True, ["Kd", "VN%d" % hh], b)
                for j in range(4):
                    h = hh * 4 + j
                    stt(S32[:, h, :], S32[:, h, :], etot[:, h:h + 1], pf(b)[:, j * 128:(j + 1) * 128], ALU.mult, ALU.add,
                        [PB(b), "S32" + hs[hh], "etot"], ["S32" + hs[hh]])
                cp("act", Sbf[:, sl, :], S32[:, sl, :], ["S32" + hs[hh]], ["Sbf" + hs[hh]])

        def drain(gen):
            for _ in gen:
                pass

        drain(project(0))
        if n_ord > 1:
            drain(project(1))
        for m in range(n_proc):
            gen = project(m + 2) if m + 2 < n_ord else iter(())

            def fill(gen=gen):
                next(gen, None)
            chunk(m, fill)
            drain(gen)

    own = {NTC + t: t for t in range(NTO)}
    orderF = list(range(NTC)) + [NTC + t for t in range(NTO)]
    orderB = list(range(NTC - 1, -1, -1)) + [NTC + t for t in range(NTL - 1, -1, -1)]
    P.barrier()
    gdn_sweep2("F", orderF, own, of_d, extra=NTC + NTO)
    if stop == 1:
        return nc, P.emit(nc)
    gdn_sweep2("B", orderB, own, ob_d)
    P.barrier()
    A.release(mg)

    if stop == 2:
        return nc, P.emit(nc)
    mm_ = A.mark()
    Wuk = A.alloc([2, 1024], BF16); Wuv = A.alloc([2, 1024], BF16)
    ckvnT = A.alloc([2, NT], BF16); krT = A.alloc([NT], BF16)
    rstdk = A.alloc([NG, 8], F32)
    m2b = A.mark()
    Wmla = A.alloc([8, 704], BF16); Wuq = A.alloc([3, 1536], BF16)
    load_w(Wmla, wmla_d, D, 704, "Wmla"); load_w(Wuq, wuq_d, 384, 1536, "Wuq")
    load_w(Wuk, wuk_d, 256, 1024, "Wuk"); load_w(Wuv, wuv_d, 256, 1024, "Wuv")
    qan_b = A.alloc([384], F32); kvan_b = A.alloc([256], F32); gq_b = A.alloc([8, 192], F32); gk_b = A.alloc([192], F32)
    dma("sp", qan_b, qan_d.partition_broadcast(128), [], ["qan"])
    dma("sp", kvan_b, kvan_d.partition_broadcast(128), [], ["kvan"])
    dma("sp", gq_b[:, 0, :], qn_d.partition_broadcast(128), [], ["gq"])
    dma("sp", gk_b, kn_d.partition_broadcast(128), [], ["gk"])
    tt("dve", gq_b[:, 0, 0:128], gq_b[:, 0, 0:128], gk_b[:, 0:128], ALU.mult, ["gq", "gk"], ["gq"])
    for h in range(1, 8):
        cp("pool", gq_b[:, h, :], gq_b[:, 0, :], ["gq"], ["gq"])
    cqnT = A.alloc([3, NO], BF16)
    hT2 = A.alloc([8, 128], BF16)
    lat = A.alloc([704], F32); latn = A.alloc([640], BF16)
    sqa = A.alloc([256], F32); sqk = A.alloc([1024], F32); sqq = A.alloc([1536], F32); sqc = A.alloc([384], F32)
    sqr = A.alloc([64], F32)
    ssm = A.alloc([48], F32)
    cs = A.alloc([2, 64], F32)
    kr32 = A.alloc([64], F32); krt = A.alloc([64], F32); krtmp = A.alloc([64], F32); krb = A.alloc([128], BF16)
    q32 = A.alloc([8, 192], F32); qrt = A.alloc([8, 64], F32); qtmp = A.alloc([8, 64], F32); qbf = A.alloc([8, 192], BF16)
    qTs = A.alloc([8, 2, 128], BF16)

    def rope(dst, src, tmp, cosv, sinv, nh, rd, nm):
        s5 = src.rearrange("p h (a c f) -> p h a c f", a=2, c=2)
        t5 = tmp.rearrange("p h (a c f) -> p h a c f", a=2, c=2)
        sn5 = sinv.rearrange("p (a c f) -> p a c f", a=2, c=2)
        for c in range(2):
            for a_ in range(2):
                tt("dve", t5[:, :, a_, c, :], s5[:, :, a_, 1 - c, :], bc(sn5[:, a_, c, :].unsqueeze(1), [128, nh, 16]), ALU.mult,
                   rd, [nm + "tmp"])
        tt("dve", dst, src, bc(cosv.unsqueeze(1), [128, nh, 64]), ALU.mult, rd, [nm])
        tt("dve", dst, dst, tmp, ALU.add, [nm, nm + "tmp"], [nm])

    for g in range(NG):
        rows, v = tile_rows(g)
        is_own = g in own
        make_hT(rows, v, hT2, "hT2")
        ncol = 704 if is_own else 320
        c_lo = 0 if is_own else 384
        b0 = nb()
        w0 = min(512, ncol)
        for k in range(8):
            mm(pf(b0)[:, 0:w0], hT2[:, k, :], Wmla[:, k, c_lo:c_lo + w0], k == 0, k == 7, ["hT2", "Wmla"], b0)
        cp("dve", lat[:, c_lo:c_lo + w0], pf(b0)[:, 0:w0], [PB(b0)], ["lat"])
        if ncol > 512:
            b1 = nb()
            for k in range(8):
                mm(pf(b1)[:, 0:ncol - 512], hT2[:, k, :], Wmla[:, k, 512:ncol], k == 0, k == 7, ["hT2", "Wmla"], b1)
            cp("act", lat[:, 512:ncol], pf(b1)[:, 0:ncol - 512], [PB(b1)], ["lat"])
        if sub == 1:
            return nc, P.emit(nc)
        dma("sp", cs[:, 0, :], cos_d[g * 128:(g + 1) * 128, :], [], ["cs"])
        dma("sp", cs[:, 1, :], sin_d[g * 128:(g + 1) * 128, :], [], ["cs"])
        act(sqa, lat[:, 384:640], AF.Square, ["lat"], ["sqa", "ss0"], accum=ssm[:, 0:1])
        rsqrt_small(ssm[:, 1:2], ssm[:, 0:1], 1.0 / 256, ["ss0"], ["ss1k"], ssm[:, 2:3], "ss2k")
        stt(latn[:, 384:640], lat[:, 384:640], ssm[:, 1:2], kvan_b, ALU.mult, ALU.mult, ["lat", "ss1k", "kvan"], ["latn_kv"])
        b = nb()
        for j in range(2):
            tr(pb16(b)[:, j * 128:(j + 1) * 128], latn[:, 384 + j * 128:384 + (j + 1) * 128], identB, ["latn_kv", "cB"], b)
        cp("act", ckvnT[:, :, g * 128:(g + 1) * 128], pb16(b)[:, 0:256].rearrange("p (a t) -> p a t", a=2), [PB(b)], ["ckvnT"])
        if sub == 2:
            return nc, P.emit(nc)
        for half in range(2):
            b = nb()
            for kk in range(2):
                mm(pf(b), ckvnT[:, kk, g * 128:(g + 1) * 128], Wuk[:, kk, half * 512:(half + 1) * 512], kk == 0, kk == 1, ["ckvnT", "Wuk"], b)
            act(sqk[:, half * 512:(half + 1) * 512], pf(b), AF.Square, [PB(b)], ["sqk"])
        red(ssm[:, 8:16], sqk.rearrange("p (h d) -> p h d", h=8), ["sqk"], ["ss8"])
        act(sqr, lat[:, 640:704], AF.Square, ["lat"], ["sqr", "ss3"], accum=ssm[:, 3:4])
        ts("dve", ssm[:, 8:16], ssm[:, 8:16], ssm[:, 3:4], ALU.add, ["ss8", "ss3"], ["ss8"])
        act(ssm[:, 16:24], ssm[:, 8:16], AF.Ln, ["ss8"], ["ss16"], scale=1.0 / 192, bias=cols[:, 0:1])
        act(rstdk[:, g, :], ssm[:, 16:24], AF.Exp, ["ss16"], ["rstdk"], scale=-0.5, bias=cols[:, 4:5])
        if sub == 3:
            return nc, P.emit(nc)
        tt("dve", kr32, lat[:, 640:704], gk_b[:, 128:192], ALU.mult, ["lat", "gk"], ["kr32"])
        rope(krt.unsqueeze(1), kr32.unsqueeze(1), krtmp.unsqueeze(1), cs[:, 0, :], cs[:, 1, :], 1, ["kr32", "cs"], "krt")
        cp("dve", krb[:, 0:64], krt, ["krt"], ["krb"])
        cp("dve", krb[:, 64:128], krt, ["krt"], ["krb"])
        b = nb()
        tr(pb16(b)[:, 0:128], krb, identB, ["krb", "cB"], b)
        cp("act", krT[:, g * 128:(g + 1) * 128], pb16(b)[:, 0:128], [PB(b)], ["krT"])
        if sub == 4:
            return nc, P.emit(nc)
        if is_own:
            ot = own[g]
            act(sqc, lat[:, 0:384], AF.Square, ["lat"], ["sqc", "ss4"], accum=ssm[:, 4:5])
            rsqrt_small(ssm[:, 5:6], ssm[:, 4:5], 1.0 / 384, ["ss4"], ["ss5"], ssm[:, 6:7], "ss6")
            stt(latn[:, 0:384], lat[:, 0:384], ssm[:, 5:6], qan_b, ALU.mult, ALU.mult, ["lat", "ss5", "qan"], ["latn_q"])
            b = nb()
            for j in range(3):
                tr(pb16(b)[:, j * 128:(j + 1) * 128], latn[:, j * 128:(j + 1) * 128], identB, ["latn_q", "cB"], b)
            cp("act", cqnT[:, :, ot * 128:(ot + 1) * 128], pb16(b)[:, 0:384].rearrange("p (a t) -> p a t", a=3), [PB(b)], ["cqnT"])
            if sub == 5:
                return nc, P.emit(nc)
            q2 = q32.rearrange("p h d -> p (h d)")
            for j in range(3):
                b = nb()
                for kk in range(3):
                    mm(pf(b), cqnT[:, kk, ot * 128:(ot + 1) * 128], Wuq[:, kk, j * 512:(j + 1) * 512], kk == 0, kk == 2, ["cqnT", "Wuq"], b)
                cp("dve", q2[:, j * 512:(j + 1) * 512], pf(b), [PB(b)], ["q32"])
                act(sqq[:, j * 512:(j + 1) * 512], q2[:, j * 512:(j + 1) * 512], AF.Square, ["q32"], ["sqq"])
            red(ssm[:, 24:32], sqq.rearrange("p (h d) -> p h d", h=8), ["sqq"], ["ss24"])
            act(ssm[:, 32:40], ssm[:, 24:32], AF.Ln, ["ss24"], ["ss32"], scale=1.0 / 192, bias=cols[:, 0:1])
            act(ssm[:, 32:40], ssm[:, 32:40], AF.Exp, ["ss32"], ["ss32"], scale=-0.5)
            tt("dve", q32, q32, bc(ssm[:, 32:40].unsqueeze(2), [128, 8, 192]), ALU.mult, ["q32", "ss32"], ["q32"])
            tt("dve", q32, q32, gq_b, ALU.mult, ["q32", "gq"], ["q32"])
            rope(qrt, q32[:, :, 128:192], qtmp, cs[:, 0, :], cs[:, 1, :], 8, ["q32", "cs"], "qrt")
            cp("dve", qbf[:, :, 0:128], q32[:, :, 0:128], ["q32"], ["qbfn"])
            cp("pool", qbf[:, :, 128:192], qrt, ["qrt"], ["qbfr"])
            if sub == 6:
                return nc, P.emit(nc)
            for hh in range(2):
                bn_ = nb()
                for j in range(4):
                    h = hh * 4 + j
                    tr(pb16(bn_)[:, j * 128:(j + 1) * 128], qbf[:, h, 0:128], identB, ["qbfn", "cB"], bn_)
                cp("act", qTs[:, hh * 4:(hh + 1) * 4, 0, :], pb16(bn_)[:, 0:512].rearrange("p (a t) -> p a t", a=4), [PB(bn_)], ["qTs%d" % hh])
                br_ = nb()
                for j in range(4):
                    h = hh * 4 + j
                    tr(pb16(br_)[0:64, j * 128:(j + 1) * 128], qbf[:, h, 128:192], identB, ["qbfr", "cB"], br_)
                cp("dve", qTs[0:64, hh * 4:(hh + 1) * 4, 1, :], pb16(br_)[0:64, 0:512].rearrange("p (a t) -> p a t", a=4), [PB(br_)], ["qTr%d" % hh])
            if sub == 7:
                return nc, P.emit(nc)
            dma("pool", qt_d[:, 0:128, ot * 128:(ot + 1) * 128].rearrange("h d t -> d h t"), qTs[:, :, 0, :], ["qTs0", "qTs1"], ["qTs0", "qTs1", "qt_d"])
            dma("pool", qt_d[:, 128:192, ot * 128:(ot + 1) * 128].rearrange("h d t -> d h t"), qTs[0:64, :, 1, :], ["qTr0", "qTr1"], ["qTr0", "qTr1", "qt_d"])

    if stop == 3:
        return nc, P.emit(nc)
    P.barrier()
    A.release(m2b)
    m3 = A.mark()
    KnT = A.alloc([NT], BF16); Vh = A.alloc([NG, 128], BF16)
    QTn = A.alloc([NO], BF16); QTr = A.alloc([NO], BF16)
    PT = [A.alloc([512], BF16) for _ in range(4)]
    rec = A.alloc([512], F32); acc = A.alloc([512], F32)
    oT = [A.alloc([512], BF16) for _ in range(2)]
    NQB = NO // 512
    NKB = (NT + 511) // 512
    bank_state.update(n=0, lo=0, hi=4)
    for h in range(8):
        dma("sp", QTn, qt_d[h, 0:128, :], ["qt_d"], ["QTn"])
        dma("sp", QTr[0:64, :], qt_d[h, 128:192, :], ["qt_d"], ["QTr"])
        dma("sp", QTr[64:128, :], qt_d[h, 128:192, :], ["qt_d"], ["QTr"])
        for kb in range(NKB):
            c0, c1 = kb * 512, min(NT, (kb + 1) * 512)
            b = nb()
            for kk in range(2):
                mm(pf(b)[:, 0:c1 - c0], Wuk[:, kk, h * 128:(h + 1) * 128], ckvnT[:, kk, c0:c1], kk == 0, kk == 1, ["Wuk", "ckvnT"], b)
            cp("dve" if kb % 2 else "act", KnT[:, c0:c1], pf(b)[:, 0:c1 - c0], [PB(b)], ["KnT"])
        for g4 in range(0, NG, 4):
            ng = min(4, NG - g4)
            b = nb()
            for j in range(ng):
                for kk in range(2):
                    mm(pf(b)[:, j * 128:(j + 1) * 128], ckvnT[:, kk, (g4 + j) * 128:(g4 + j + 1) * 128], Wuv[:, kk, h * 128:(h + 1) * 128],
                       kk == 0, kk == 1, ["ckvnT", "Wuv"], b)
            cp("dve", Vh[:, g4:g4 + ng, :], pf(b)[:, 0:ng * 128].rearrange("p (a t) -> p a t", a=ng), [PB(b)], ["Vh"])
        for qb in range(NQB):
            qs = slice(qb * 512, (qb + 1) * 512)
            ba, bs_ = (4, 5) if (h * NQB + qb) % 2 == 0 else (6, 7)

            def st2(gp):
                g0, g1 = 2 * gp, 2 * gp + 1
                b0 = nb(); b1 = nb()
                mm(pf(b0), KnT[:, g0 * 128:(g0 + 1) * 128], QTn[:, qs], True, False, ["KnT", "QTn"], b0)
                mm(pf(b1), KnT[:, g1 * 128:(g1 + 1) * 128], QTn[:, qs], True, False, ["KnT", "QTn"], b1)
                mm(pf(b0), krT[0:64, g0 * 128:(g0 + 1) * 128], QTr[0:64, qs], False, True, ["krT", "QTr"], b0)
                mm(pf(b1), krT[64:128, g1 * 128:(g1 + 1) * 128], QTr[64:128, qs], False, True, ["krT", "QTr"], b1)
                return (b0, b1)
            assert NG % 2 == 0
            bc_ = st2(0)
            for gp in range(NG // 2):
                bn_ = st2(gp + 1) if gp + 1 < NG // 2 else None
                for u in range(2):
                    g = 2 * gp + u
                    pt = PT[g % 4]
                    ptn = "PT%d" % (g % 4)
                    act(pt, pf(bc_[u]), AF.Exp, [PB(bc_[u]), "rstdk"], [ptn], scale=rstdk[:, g, h:h + 1])
                    mm(pf(ba), Vh[:, g, :], pt, g == 0, g == NG - 1, ["Vh", ptn], ba)
                    if g == 0:
                        cp("dve", acc, pt, [ptn], ["acc"])
                    else:
                        tt("dve", acc, acc, pt, ALU.add, ["acc", ptn], ["acc"])
                bc_ = bn_
            mm(pf(bs_), onesF, acc, True, True, ["cF", "acc"], bs_)
            P.op("dve", lambda e, bs_=bs_: e.reciprocal(out=rec, in_=pf(bs_)), reads=[PB(bs_)], writes=["rec"])
            o_ = oT[qb % 2]
            on = "oT%d" % (qb % 2)
            tt("dve", o_, pf(ba), rec, ALU.mult, [PB(ba), "rec"], [on])
            dma("pool", om_d[h, :, qs], o_, [on], [on, "om_d"])
    bank_state.update(n=0, lo=0, hi=8)
    P.barrier()
    A.release(m3)
    A.release(mm_)

    if stop == 4:
        return nc, P.emit(nc)
    m4 = A.mark()
    Wzg = A.alloc([8, 3072], BF16); Wg = A.alloc([8, D], BF16); Wmm = A.alloc([8, D], BF16); Wo = A.alloc([8, D], BF16)
    load_w(Wzg, wzg_d, D, 3072, "Wzg"); load_w(Wg, wg_d, D, D, "Wg"); load_w(Wmm, wm_d, D, D, "Wmm"); load_w(Wo, wo_d, D, D, "Wo")
    modb = A.alloc([2, D], F32)
    dgm = A.alloc([128], F32)
    for vi, e_ in enumerate((2, 5)):
        for t in range(8):
            ts("dve", dgm, identF, modF[:, e_, 0, t:t + 1], ALU.mult, ["cF", "modF"], ["dgm"])
            b = nb()
            mm(pf(b)[:, 0:128], onesF, dgm, True, True, ["cF", "dgm"], b)
            cp("act", modb[:, vi, t * 128:(t + 1) * 128], pf(b)[:, 0:128], [PB(b)], ["modb"])
    gon_b = A.alloc([128], F32)
    dma("sp", gon_b, gon_d.partition_broadcast(128), [], ["gon"])
    hT4 = A.alloc([8, 128], BF16)
    xk = A.alloc([D], F32)
    zs = A.alloc([D], F32); sg = A.alloc([2 * D], F32)
    ofb = A.alloc([D], F32); obb = A.alloc([D], F32)
    ssg = A.alloc([24], F32)
    yg = A.alloc([D], BF16); ygT = A.alloc([8, 128], BF16); omT = A.alloc([8, 128], BF16)
    t32 = A.alloc([D], F32); t32b = A.alloc([D], F32); ybf = A.alloc([D], BF16); yT = A.alloc([8, 128], BF16)
    x1t = [A.alloc([D], F32) for _ in range(2)]
    for ot in range(NTO):
        rows = x_d[ot * 128:(ot + 1) * 128, :]
        make_hT(rows, 0, hT4, "hT4", xt_keep=xk)
        dma("sp", ofb, of_d[ot * 128:(ot + 1) * 128, :], [], ["ofb"])
        dma("sp", obb, ob_d[ot * 128:(ot + 1) * 128, :], [], ["obb"])
        dma("sp", omT, om_d[:, :, ot * 128:(ot + 1) * 128].rearrange("h d t -> d h t"), [], ["omT"])
        for j in range(6):
            b = nb()
            for k in range(8):
                mm(pf(b), hT4[:, k, :], Wzg[:, k, j * 512:(j + 1) * 512], k == 0, k == 7, ["hT4", "Wzg"], b)
            if j < 2:
                act(zs[:, j * 512:(j + 1) * 512], pf(b), AF.Silu, [PB(b)], ["zs"])
            else:
                act(sg[:, (j - 2) * 512:(j - 1) * 512], pf(b), AF.Sigmoid, [PB(b)], ["sg"])
        tt("pool", ofb, ofb, obb, ALU.add, ["ofb", "obb"], ["ofb"])
        tt("pool", t32, ofb, ofb, ALU.mult, ["ofb"], ["t32"])
        red(ssg[:, 0:8], t32.rearrange("p (h d) -> p h d", h=8), ["t32"], ["ssg0"])
        act(ssg[:, 8:16], ssg[:, 0:8], AF.Ln, ["ssg0"], ["ssg8"], scale=1.0 / 128, bias=cols[:, 0:1])
        act(ssg[:, 8:16], ssg[:, 8:16], AF.Exp, ["ssg8"], ["ssg8"], scale=-0.5)
        o3 = ofb.rearrange("p (h d) -> p h d", h=8)
        tt("dve", o3, o3, bc(ssg[:, 8:16].unsqueeze(2), [128, 8, 128]), ALU.mult, ["ofb", "ssg8"], ["ofb"])
        tt("dve", o3, o3, bc(gon_b.unsqueeze(1), [128, 8, 128]), ALU.mult, ["ofb", "gon"], ["ofb"])
        tt("dve", yg, ofb, zs, ALU.mult, ["ofb", "zs"], ["yg"])
        b = nb()
        for k in range(8):
            tr(pb16(b)[:, k * 128:(k + 1) * 128], yg[:, k * 128:(k + 1) * 128], identB, ["yg", "cB"], b)
        cp("act", ygT, pb16(b).rearrange("p (a t) -> p a t", a=8), [PB(b)], ["ygT"])
        for half in range(2):
            hs_ = slice(half * 512, (half + 1) * 512)
            b1 = nb()
            for k in range(8):
                mm(pf(b1), ygT[:, k, :], Wg[:, k, hs_], k == 0, k == 7, ["ygT", "Wg"], b1)
            b2 = nb()
            for k in range(8):
                mm(pf(b2), omT[:, k, :], Wmm[:, k, hs_], k == 0, k == 7, ["omT", "Wmm"], b2)
            tt("dve", t32[:, hs_], pf(b1), sg[:, hs_], ALU.mult, [PB(b1), "sg"], ["t32"])
            tt("dve", t32b[:, hs_], pf(b2), sg[:, D + half * 512:D + (half + 1) * 512], ALU.mult, [PB(b2), "sg"], ["t32b"])
            tt("pool", ybf[:, hs_], t32[:, hs_], t32b[:, hs_], ALU.add, ["t32", "t32b"], ["ybf"])
        b = nb()
        for k in range(8):
            tr(pb16(b)[:, k * 128:(k + 1) * 128], ybf[:, k * 128:(k + 1) * 128], identB, ["ybf", "cB"], b)
        cp("act", yT, pb16(b).rearrange("p (a t) -> p a t", a=8), [PB(b)], ["yT"])
        xo = x1t[ot % 2]
        xon = "x1t%d" % (ot % 2)
        for half in range(2):
            hs_ = slice(half * 512, (half + 1) * 512)
            b = nb()
            for k in range(8):
                mm(pf(b), yT[:, k, :], Wo[:, k, hs_], k == 0, k == 7, ["yT", "Wo"], b)
            tt("dve", t32[:, hs_], pf(b), modb[:, 0, hs_], ALU.mult, [PB(b), "modb"], ["t32"])
            tt("pool", xo[:, hs_], t32[:, hs_], xk[:, hs_], ALU.add, ["t32", "hT4xt"], [xon])
        dma("pool", x1_d[ot * 128:(ot + 1) * 128, :], xo, [xon], [xon, "x1_d"])
    P.barrier()
    A.release(m4)

    if stop == 5:
        return nc, P.emit(nc)
    W1 = A.alloc([8, 4 * D], BF16); W2 = A.alloc([32, D], BF16)
    load_w(W1, w1_d, D, 4 * D, "W1"); load_w(W2, w2_d, 4 * D, D, "W2")
    modb2 = A.alloc([D], F32)
    dgm2 = A.alloc([128], F32)
    for t in range(8):
        ts("dve", dgm2, identF, modF[:, 5, 0, t:t + 1], ALU.mult, ["cF", "modF"], ["dgm2"])
        b = nb()
        mm(pf(b)[:, 0:128], onesF, dgm2, True, True, ["cF", "dgm2"], b)
        cp("act", modb2[:, t * 128:(t + 1) * 128], pf(b)[:, 0:128], [PB(b)], ["modb2"])
    hT5 = A.alloc([8, 128], BF16)
    xk2 = [A.alloc([D], F32) for _ in range(2)]
    rl = A.alloc([4, 128], BF16); aT = A.alloc([32, 128], BF16)
    t5 = A.alloc([D], F32); outt = [A.alloc([D], F32) for _ in range(2)]
    for ot in range(NTO):
        xkk = xk2[ot % 2]
        make_hT(x1_d[ot * 128:(ot + 1) * 128, :], 0, hT5, "hT5x%d" % (ot % 2), G=G2, SH=SH2, xt_keep=xkk)
        for f4 in range(8):
            b = nb()
            for j in range(4):
                f = f4 * 4 + j
                for k in range(8):
                    mm(pf(b)[:, j * 128:(j + 1) * 128], W1[:, k, f * 128:(f + 1) * 128], hT5[:, k, :], k == 0, k == 7, ["W1", "hT5x%d" % (ot % 2)], b)
            act(rl, pf(b).rearrange("p (a t) -> p a t", a=4), AF.Relu, [PB(b)], ["rl"])
            tt("pool", aT[:, f4 * 4:(f4 + 1) * 4, :], rl, rl, ALU.mult, ["rl"], ["aT"])
        oo = outt[ot % 2]
        for half in range(2):
            hs_ = slice(half * 512, (half + 1) * 512)
            b = nb()
            for f in range(32):
                mm(pf(b), aT[:, f, :], W2[:, f, hs_], f == 0, f == 31, ["aT", "W2"], b)
            tt("dve", t5[:, hs_], pf(b), modb2[:, hs_], ALU.mult, [PB(b), "modb2"], ["t5%d" % half])
            tt("pool", oo[:, hs_], t5[:, hs_], xkk[:, hs_], ALU.add, ["t5%d" % half, "hT5x%dxt" % (ot % 2)], ["oo%d%d" % (ot % 2, half)])
        dma("pool", out_d[ot * 128:(ot + 1) * 128, :], oo, ["oo%d0" % (ot % 2), "oo%d1" % (ot % 2)], ["oo%d0" % (ot % 2), "oo%d1" % (ot % 2)])
    nops = P.emit(nc)
    return nc, nops


def _consts():
    i = np.arange(128)
    P_, Q_ = np.meshgrid(i, i, indexing="ij")
    c = np.zeros((NCONST, 128, 128), np.float32)
    c[0] = np.eye(128); c[1] = 1.0
    c[2] = (P_ <= Q_); c[3] = (P_ > Q_); c[4] = (Q_ >= P_); c[5] = (Q_ > P_)
    c[13] = (P_ >= Q_); c[14] = (P_ < Q_); c[15] = (Q_ <= P_); c[16] = (Q_ < P_)
    for l in range(7):
        bs = 2 ** (l + 1)
        same = (P_ // bs) == (Q_ // bs)
        c[6 + l] = same & ((Q_ % bs) >= bs // 2) & ((P_ % bs) < bs // 2)
        c[17 + l] = same & ((P_ % bs) >= bs // 2) & ((Q_ % bs) < bs // 2)
    return np.ascontiguousarray(c.transpose(1, 0, 2).reshape(128, NCONST * 128))


def _fm(v, n):
    return np.ascontiguousarray(np.asarray(v, np.float32).reshape(n, 128).T)


def _core_inputs(inp, b, s, NL, NC):
    f32 = lambda a: np.ascontiguousarray(np.asarray(a, np.float32))
    w_in = np.asarray(inp["w_in"][0], np.float32)
    o_z, o_b, o_d, o_cq, o_ckv, o_kr, o_g = 3072, 4096, 4112, 4128, 4512, 4768, 4832
    x = np.asarray(inp["x"][b], np.float32)[:NL]
    ctx = np.asarray(inp["ctx"][b], np.float32)[:NC]
    conv = np.asarray(inp["conv_qkv"][0], np.float32)
    dF, dB = (0, 1) if s == 0 else (1, 0)
    pos = np.arange(NL)
    if s == 1:
        x = x[::-1]; ctx = ctx[::-1]; conv = conv[::-1]; pos = pos[::-1]
    beta = lambda d: w_in[:, o_b + 8 * d:o_b + 8 * d + 8]
    dec = lambda d: w_in[:, o_d + 8 * d:o_d + 8 * d + 8]
    w_bd = np.concatenate([beta(dF), dec(dF), beta(dB), dec(dB)], axis=1)
    inv = (10000.0 ** (-np.arange(16, dtype=np.float32) / np.float32(16))).astype(np.float32)
    row = (pos // GRID_W).astype(np.float32); col = (pos % GRID_W).astype(np.float32)
    ang = np.stack([row[:, None] * inv[None, :], col[:, None] * inv[None, :]], axis=1).astype(np.float32)
    cs_l = np.cos(ang).astype(np.float32); sn_l = np.sin(ang).astype(np.float32)
    cos_t = np.ones((NC + NL, 2, 2, 16), np.float32); sin_t = np.zeros((NC + NL, 2, 2, 16), np.float32)
    cos_t[NC:, :, 0, :] = cs_l; cos_t[NC:, :, 1, :] = cs_l
    sin_t[NC:, :, 0, :] = -sn_l; sin_t[NC:, :, 1, :] = sn_l
    w_ukv = np.asarray(inp["w_ukv"][0], np.float32).reshape(256, 8, 256)
    return {
        "x": f32(x), "ctx": f32(ctx),
        "cvec": f32(np.concatenate([_fm(inp["c"][b], 8), _fm(inp["c_ctx"], 8)], axis=1)),
        "w_mod": f32(inp["w_mod"][0]), "b_mod": _fm(inp["b_mod"][0], 48),
        "norm_attn": _fm(inp["norm_attn"][0], 8), "norm_mlp": _fm(inp["norm_mlp"][0], 8),
        "w_qkv": f32(w_in[:, 0:3072]), "w_zg": f32(np.concatenate([w_in[:, o_z:o_b], w_in[:, o_g:o_g + 2048]], axis=1)),
        "w_bd": f32(w_bd), "w_mla": f32(w_in[:, o_cq:o_g]),
        "conv_w": f32(conv.T.reshape(24, 128, 5).transpose(1, 0, 2).reshape(128, 120)),
        "a_log": f32(np.concatenate([inp["gdn_a_log"][0][dF], inp["gdn_a_log"][0][dB]])),
        "dt_bias": f32(np.concatenate([inp["gdn_dt_bias"][0][dF], inp["gdn_dt_bias"][0][dB]])),
        "gdn_out_norm": f32(inp["gdn_out_norm"][0]), "q_a_norm": f32(inp["mla_q_a_norm"][0]), "kv_a_norm": f32(inp["mla_kv_a_norm"][0]),
        "q_norm": f32(inp["q_norm"][0]), "k_norm": f32(inp["k_norm"][0]),
        "w_uq": f32(inp["w_uq"][0]), "w_uk": f32(w_ukv[:, :, 0:128].reshape(256, 1024)), "w_uv": f32(w_ukv[:, :, 128:256].reshape(256, 1024)),
        "w_bg": f32(inp["w_branch_gdn"][0]), "w_bm": f32(inp["w_branch_mla"][0]), "w_out": f32(inp["w_out"][0]),
        "w_mlp_in": f32(inp["w_mlp_in"][0]), "w_mlp_out": f32(inp["w_mlp_out"][0]),
        "consts": _consts(), "rope_cos": f32(cos_t.reshape(NC + NL, 64)), "rope_sin": f32(sin_t.reshape(NC + NL, 64)),
    }


_NC_CACHE = {}


def run(inp, B, NL, NC):
    key = (NL, NC)
    if key not in _NC_CACHE:
        _NC_CACHE[key] = build_nc(NL, NC)[0]
    nc = _NC_CACHE[key]
    cores = [(b, s) for b in range(B) for s in range(2)]
    in_maps = [_core_inputs(inp, b, s, NL, NC) for (b, s) in cores]
    res = run_bass_kernel_spmd(nc, in_maps, core_ids=list(range(len(cores))))
    NO = NL // 2
    out = np.zeros((B, NL, D), np.float32)
    for (b, s), r in zip(cores, res.results):
        o = np.asarray(r["out"], np.float32)
        if s == 0:
            out[b, :NO] = o
        else:
            out[b, NO:] = o[::-1]
    return out


def kernel(**inputs):
    inp = {k: np.asarray(v) for k, v in inputs.items()}
    B, NL, _ = inp["x"].shape
    NC = inp["ctx"].shape[1]
    return run(inp, B, NL, NC)
```

```python
import math
import numpy as np
import ml_dtypes
import concourse.bass as bass
import concourse.mybir as mybir
from concourse.bass_utils import run_bass_kernel_spmd

F32 = mybir.dt.float32
BF16 = mybir.dt.bfloat16
AF = mybir.ActivationFunctionType
ALU = mybir.AluOpType
AX = mybir.AxisListType

D = 1024
KD = 8
H = 8
EPS = 1e-6
GRID_W = 64
NCONST = 24


class Buf:
    __slots__ = ("name", "writer", "readers")

    def __init__(self, name):
        self.name = name
        self.writer = None
        self.readers = []


class Op:
    __slots__ = ("eng", "fn", "deps", "signal", "tok", "is_dma", "dsem")

    def __init__(self, eng, fn, is_dma=False):
        self.eng = eng
        self.fn = fn
        self.deps = []
        self.signal = False
        self.tok = None
        self.is_dma = is_dma
        self.dsem = None


class Prog:
    ENGS = ("pe", "act", "dve", "pool", "sp")
    NDSEM = 12

    def __init__(self):
        self.ops = []
        self.bufs = {}
        self.last = {}
        self.dma_since = []

    def _B(self, x):
        b = self.bufs.get(x)
        if b is None:
            b = Buf(x)
            self.bufs[x] = b
        return b

    def op(self, eng, fn, reads=(), writes=(), is_dma=False):
        idx = len(self.ops)
        o = Op(eng, fn, is_dma)
        deps = set()
        for r in reads:
            r = self._B(r)
            if r.writer is not None:
                deps.add(r.writer)
        for w in writes:
            w = self._B(w)
            if w.writer is not None:
                deps.add(w.writer)
            deps.update(w.readers)
        fin = []
        for d in deps:
            od = self.ops[d]
            if od.eng == eng and eng == "pe" and not od.is_dma and not is_dma:
                continue
            fin.append(d)
            od.signal = True
        o.deps = sorted(fin)
        self.ops.append(o)
        for r in reads:
            rb = self._B(r)
            if not is_dma:
                rb.readers = [q for q in rb.readers if self.ops[q].is_dma or self.ops[q].eng != eng]
            rb.readers.append(idx)
        for w in writes:
            w = self._B(w)
            w.writer = idx
            w.readers = []
        self.last[eng] = idx
        if is_dma:
            self.dma_since.append(idx)
        return idx

    def barrier(self):
        deps = sorted(set(list(self.last.values()) + self.dma_since))
        for d in deps:
            self.ops[d].signal = True
        for e in self.ENGS:
            o = Op(e, None)
            o.deps = list(deps)
            self.ops.append(o)
        self.dma_since = []
        for b in self.bufs.values():
            b.writer = None
            b.readers = []

    def emit(self, nc):
        sems = {e: nc.alloc_semaphore("S_" + e) for e in self.ENGS}
        dq = ("sp", "pool", "act")
        dsems = {e: [nc.alloc_semaphore("D_%s_%d" % (e, j)) for j in range(self.NDSEM)] for e in dq}
        cnt = {e: 0 for e in self.ENGS}
        dcnt = {e: [0] * self.NDSEM for e in dq}
        dnext = {e: 0 for e in dq}
        for o in self.ops:
            if o.is_dma:
                j = dnext[o.eng]
                dnext[o.eng] = (j + 1) % self.NDSEM
                prev = dcnt[o.eng][j]
                dcnt[o.eng][j] += 16
                o.dsem = (dsems[o.eng][j], prev)
                o.tok = (dsems[o.eng][j], dcnt[o.eng][j])
            elif o.signal and o.fn is not None:
                cnt[o.eng] += 1
                o.tok = (sems[o.eng], cnt[o.eng])
        per = {e: [] for e in self.ENGS}
        for o in self.ops:
            per[o.eng].append(o)
        ops = self.ops
        tail = [o for o in ops if o.is_dma]

        def run(ename):
            def body(eng):
                seen = {}

                def wait(tok):
                    if tok is None:
                        return
                    s, v = tok
                    k = id(s)
                    if seen.get(k, 0) >= v:
                        return
                    seen[k] = v
                    eng.wait_ge(s, v)

                for o in per[ename]:
                    for d in o.deps:
                        wait(ops[d].tok)
                    if o.fn is None:
                        continue
                    if o.is_dma:
                        s, prev = o.dsem
                        if prev > 0:
                            wait((s, prev))
                        o.fn(eng).then_inc(o.tok[0], 16)
                    else:
                        ins = o.fn(eng)
                        if o.signal:
                            ins.then_inc(o.tok[0], 1)
                if ename == "sp":
                    for o in tail:
                        wait(o.tok)
            return body

        with nc.Block() as block:
            block.tensor(run("pe"))
            block.scalar(run("act"))
            block.vector(run("dve"))
            block.gpsimd(run("pool"))
            block.sync(run("sp"))
        return len(ops)


class Arena:
    def __init__(self, nc, nbytes):
        self.t = nc.alloc_sbuf_tensor("arena", [128, nbytes // 4], F32)
        self.cap = nbytes // 4
        self.off = 0
        self.uid = 0

    def alloc(self, shape, dtype):
        n = 1
        for s in shape:
            n *= s
        words = n if dtype == F32 else (n + 1) // 2
        words = (words + 7) // 8 * 8
        assert self.off + words <= self.cap, "SBUF arena overflow %d+%d>%d" % (self.off, words, self.cap)
        v = self.t[:, self.off:self.off + words]
        self.off += words
        if dtype == BF16:
            v = v.bitcast(BF16)
        v = v[:, 0:n]
        if len(shape) == 2:
            v = v.rearrange("p (a b) -> p a b", a=shape[0])
        elif len(shape) == 3:
            v = v.rearrange("p (a b c) -> p a b c", a=shape[0], b=shape[1])
        self.uid += 1
        return v

    def mark(self):
        return self.off

    def release(self, m):
        self.off = m


def bc(ap, shape):
    return ap.to_broadcast(shape)


def build_nc(NL, NC, dbg=False, stop=99, sub=99):
    NO = NL // 2
    NT = NC + NL
    NTC = NC // 128
    NTL = NL // 128
    NTO = NO // 128
    NG = NTC + NTL
    nc = bass.Bass("TRN2", target_bir_lowering=False)

    def din(name, shape, dt=F32):
        return nc.dram_tensor(name, list(shape), dt, kind="ExternalInput").ap()

    x_d = din("x", [NL, D]); ctx_d = din("ctx", [NC, D])
    cvec_d = din("cvec", [128, 16]); wmod_d = din("w_mod", [D, 6 * D]); bmod_d = din("b_mod", [128, 48])
    nattn_d = din("norm_attn", [128, 8]); nmlp_d = din("norm_mlp", [128, 8])
    wqkv_d = din("w_qkv", [D, 3072]); wzg_d = din("w_zg", [D, 3072]); wbd_d = din("w_bd", [D, 32]); wmla_d = din("w_mla", [D, 704])
    conv_d = din("conv_w", [128, 24 * 5]); alog_d = din("a_log", [16]); dtb_d = din("dt_bias", [16])
    gon_d = din("gdn_out_norm", [128]); qan_d = din("q_a_norm", [384]); kvan_d = din("kv_a_norm", [256])
    qn_d = din("q_norm", [192]); kn_d = din("k_norm", [192])
    wuq_d = din("w_uq", [384, 1536]); wuk_d = din("w_uk", [256, 1024]); wuv_d = din("w_uv", [256, 1024])
    wg_d = din("w_bg", [D, D]); wm_d = din("w_bm", [D, D]); wo_d = din("w_out", [D, D])
    w1_d = din("w_mlp_in", [D, 4 * D]); w2_d = din("w_mlp_out", [4 * D, D])
    const_d = din("consts", [128, NCONST * 128]); cos_d = din("rope_cos", [NT, 64]); sin_d = din("rope_sin", [NT, 64])
    out_d = nc.dram_tensor("out", [NO, D], F32, kind="ExternalOutput").ap()

    def dscr(name, shape, dt):
        return nc.dram_tensor(name, list(shape), dt, kind="Internal").ap()

    of_d = dscr("scr_of", [NO, D], F32); ob_d = dscr("scr_ob", [NO, D], F32)
    qt_d = dscr("scr_qt", [H, 192, NO], BF16); om_d = dscr("scr_om", [H, 128, NO], BF16)
    x1_d = dscr("scr_x1", [NO, D], F32)

    P = Prog()
    A = Arena(nc, 207 * 1024)
    ps = [nc.alloc_psum_tensor("psb%d" % i, [128, 512], F32) for i in range(8)]
    bank_state = {"n": 0, "lo": 0, "hi": 8}

    def nb():
        r = bank_state["hi"] - bank_state["lo"]
        b = bank_state["lo"] + bank_state["n"] % r
        bank_state["n"] += 1
        return b

    def pf(b):
        return ps[b][:]

    def pb16(b):
        return ps[b][:].bitcast(BF16)

    def PB(b):
        return "ps%d" % b

    uid = [0]

    def U(prefix):
        uid[0] += 1
        return "%s#%d" % (prefix, uid[0])

    def dma(q, out, in_, reads, writes):
        P.op(q, lambda e: e.dma_start(out=out, in_=in_), reads=reads, writes=writes, is_dma=True)

    def mm(out, lhsT, rhs, start, stop, reads, bank):
        P.op("pe", lambda e: e.matmul(out, lhsT=lhsT, rhs=rhs, start=start, stop=stop), reads=reads, writes=[PB(bank)])

    def tr(out, in_, ident, reads, bank):
        P.op("pe", lambda e: e.transpose(out, in_, ident), reads=reads, writes=[PB(bank)])

    def act(out, in_, func, reads, writes, scale=None, bias=None, accum=None):
        kw = {}
        if scale is not None:
            kw["scale"] = scale
        if bias is not None:
            kw["bias"] = bias
        if accum is not None:
            kw["accum_out"] = accum
        P.op("act", lambda e: e.activation(out=out, in_=in_, func=func, **kw), reads=reads, writes=writes)

    def tt(eng, out, in0, in1, op, reads, writes):
        P.op(eng, lambda e: e.tensor_tensor(out=out, in0=in0, in1=in1, op=op), reads=reads, writes=writes)

    def ts(eng, out, in0, s1, op0, reads, writes, s2=None, op1=None):
        if op1 is None:
            P.op(eng, lambda e: e.tensor_scalar(out=out, in0=in0, scalar1=s1, scalar2=None, op0=op0), reads=reads, writes=writes)
        else:
            P.op(eng, lambda e: e.tensor_scalar(out=out, in0=in0, scalar1=s1, scalar2=s2, op0=op0, op1=op1), reads=reads, writes=writes)

    def stt(out, in0, scalar, in1, op0, op1, reads, writes):
        P.op("dve", lambda e: e.scalar_tensor_tensor(out=out, in0=in0, scalar=scalar, in1=in1, op0=op0, op1=op1), reads=reads, writes=writes)

    def cp(eng, out, in_, reads, writes):
        if eng == "act":
            act(out, in_, AF.Copy, reads, writes)
        else:
            P.op(eng, lambda e: e.tensor_copy(out=out, in_=in_), reads=reads, writes=writes)

    def red(out, in_, reads, writes):
        P.op("dve", lambda e: e.tensor_reduce(out=out, in_=in_, axis=AX.X, op=ALU.add), reads=reads, writes=writes)

    cF = A.alloc([6, 128], F32)
    cB = A.alloc([NCONST, 128], BF16)
    cols = A.alloc([8], F32)
    stage_all = A.alloc([2048], F32)
    stage = [stage_all[:, 0:1024], stage_all[:, 1024:2048]]
    cvec = A.alloc([16], F32)
    sc = A.alloc([16], F32)
    modF = A.alloc([6, 2, 8], F32)
    bmod = A.alloc([48], F32)
    nrm = A.alloc([2, 8], F32)
    G1 = A.alloc([2, 8], F32); G2 = A.alloc([8], F32)
    m0 = A.mark()
    ctemp = A.alloc([NCONST, 128], F32)
    dma("sp", ctemp.rearrange("p a b -> p (a b)"), const_d, [], ["ctemp"])
    dma("sp", cF[:, 0:4, :].rearrange("p a b -> p (a b)"), const_d[:, 0:4 * 128], [], ["cF"])
    dma("sp", cF[:, 4:6, :].rearrange("p a b -> p (a b)"), const_d[:, 13 * 128:15 * 128], [], ["cF"])
    cp("dve", cB, ctemp, ["ctemp"], ["cB"])
    for j, val in enumerate((EPS, 1.0, math.log(128 ** -0.5), 0.0, math.log(192 ** -0.5))):
        P.op("pool", lambda e, j=j, val=val: e.memset(cols[:, j:j + 1], val), writes=["cols"])
    identF = cF[:, 0, :]; onesF = cF[:, 1, :]
    identB = cB[:, 0, :]; onesB = cB[:, 1, :]
    CF = {"F": 2, "B": 4}
    CB = {"F": 2, "B": 13}

    def rsqrt_small(out, in_, mul, reads, writes, tmp, tmpn):
        act(tmp, in_, AF.Ln, reads, [tmpn], scale=mul, bias=cols[:, 0:1])
        act(out, tmp, AF.Exp, [tmpn], writes, scale=-0.5)

    stg = [0]

    def load_w(dst, src, K, N, name, eng_cycle=("act", "dve")):
        for k in range(K // 128):
            for c0 in range(0, N, 1024):
                c1 = min(N, c0 + 1024)
                i = stg[0] % 2
                stg[0] += 1
                st = stage[i][:, 0:c1 - c0]
                dma("sp", st, src[k * 128:(k + 1) * 128, c0:c1], [], ["stage%d" % i])
                cp(eng_cycle[stg[0] % len(eng_cycle)], dst[:, k, c0:c1], st, ["stage%d" % i], [name])

    dma("sp", cvec, cvec_d, [], ["cvec"])
    dma("sp", bmod, bmod_d, [], ["bmod"])
    dma("sp", nrm[:, 0, :], nattn_d, [], ["nrm"])
    dma("sp", nrm[:, 1, :], nmlp_d, [], ["nrm"])
    act(sc, cvec, AF.Silu, ["cvec"], ["sc"])
    scv = sc.rearrange("p (v k) -> p k v", v=2)
    wst = [A.alloc([8, 512], F32) for _ in range(2)]
    for blk in range(12):
        w = wst[blk % 2]
        dma("sp", w, wmod_d[:, blk * 512:(blk + 1) * 512].rearrange("(k p) n -> p k n", p=128), [], ["wst%d" % (blk % 2)])
        b = nb()
        for t4 in range(4):
            for k in range(8):
                mm(pf(b)[:, t4 * 2:(t4 + 1) * 2], w[:, k, t4 * 128:(t4 + 1) * 128], scv[:, k, :], k == 0, k == 7,
                   ["wst%d" % (blk % 2), "sc"], b)
        e = blk // 2
        t0 = (blk % 2) * 4
        tt("dve", modF[:, e, :, t0:t0 + 4], pf(b)[:, 0:8].rearrange("p (t v) -> p v t", v=2),
           bc(bmod[:, e * 8 + t0:e * 8 + t0 + 4].unsqueeze(1), [128, 2, 4]), ALU.add, [PB(b), "bmod"], ["modF"])
    A.release(m0)
    for v in range(2):
        stt(G1[:, v, :], modF[:, 1, v, :], 1.0, nrm[:, 0, :], ALU.add, ALU.mult, ["modF", "nrm"], ["G1"])
    stt(G2, modF[:, 4, 0, :], 1.0, nrm[:, 1, :], ALU.add, ALU.mult, ["modF", "nrm"], ["G2"])
    SH1 = modF[:, 0, :, :]
    SH2 = modF[:, 3, 0, :]

    P.barrier()

    if stop == 0:
        return nc, P.emit(nc)
    def make_hT(src_rows, v, hT, tag, G=None, SH=None, xt_keep=None):
        xt = xt_keep if xt_keep is not None else A_x[tag_i[0] % 2]
        xn = A_xn
        nm = "xt%d" % (tag_i[0] % 2) if xt_keep is None else tag + "xt"
        tag_i[0] += 1
        dma("sp", xt, src_rows, [], [nm])
        act(A_junk, xt, AF.Square, [nm], ["junk", "ss1"], accum=A_ss[:, 0:1])
        rsqrt_small(A_ss[:, 1:2], A_ss[:, 0:1], 1.0 / D, ["ss1"], ["rs1"], A_ss[:, 2:3], "ss1t")
        act(xn, xt, AF.Copy, [nm, "rs1"], ["xn"], scale=A_ss[:, 1:2])
        b = nb()
        for k in range(8):
            tr(pb16(b)[:, k * 128:(k + 1) * 128], xn[:, k * 128:(k + 1) * 128], identB, ["xn", "cB"], b)
        g = G if G is not None else G1[:, v, :]
        s = SH if SH is not None else SH1[:, v, :]
        tt("dve", A_ht32, pb16(b).rearrange("p (k t) -> p k t", k=8), bc(g.unsqueeze(2), [128, 8, 128]), ALU.mult,
           [PB(b), "G1", "G2"], ["ht32"])
        tt("dve", hT, A_ht32, bc(s.unsqueeze(2), [128, 8, 128]), ALU.add, ["ht32", "modF"], [tag])

    tag_i = [0]
    A_x = [A.alloc([D], F32) for _ in range(2)]
    A_xn = A.alloc([D], BF16)
    A_junk = A.alloc([D], BF16)
    A_ss = A.alloc([8], F32)
    A_ht32 = A.alloc([8, 128], F32)

    def tile_rows(g):
        if g < NTC:
            return ctx_d[g * 128:(g + 1) * 128, :], 1
        t = g - NTC
        return x_d[t * 128:(t + 1) * 128, :], 0

    mg = A.mark()
    Wqkv = A.alloc([8, 3072], BF16)
    Wbd = A.alloc([8, 32], BF16)
    convw = A.alloc([24, 5], F32)
    diagW = A.alloc([120, 128], BF16)
    negA = A.alloc([16], F32); dtb = A.alloc([16], F32)
    load_w(Wqkv, wqkv_d, D, 3072, "Wqkv")
    load_w(Wbd, wbd_d, D, 32, "Wbd")
    dma("sp", convw.rearrange("p a b -> p (a b)"), conv_d, [], ["convw"])
    dma("sp", negA, alog_d.partition_broadcast(128), [], ["negA"])
    dma("sp", dtb, dtb_d.partition_broadcast(128), [], ["dtb"])
    act(negA, negA, AF.Exp, ["negA"], ["negA"])
    ts("dve", negA, negA, -1.0, ALU.mult, ["negA"], ["negA"])
    for ct in range(24):
        for i in range(5):
            ts("pool" if (ct + i) % 2 else "dve", diagW[:, ct * 5 + i, :], identF, convw[:, ct, i:i + 1], ALU.mult,
               ["cF", "convw"], ["diagW"])

    P.barrier()
    hT = [A.alloc([8, 128], BF16) for _ in range(2)]
    Xw = [A.alloc([24, 132], BF16) for _ in range(2)]
    Xw.append(stage_all[:, 0:24 * 132 // 2].bitcast(BF16).rearrange("p (a b) -> p a b", a=24))
    Xe = [A.alloc([24, 4], BF16) for _ in range(4)]
    YT = A.alloc([24, 128], BF16)
    SQ = A.alloc([16, 128], BF16)
    RS = A.alloc([16, 128], F32)
    QKN = A.alloc([16, 128], BF16)
    Ktok = A.alloc([8, 128], BF16); Vtok = A.alloc([8, 128], BF16)
    sca = A.alloc([12, 8], F32)
    Rm = A.alloc([8, 128], F32); Fm = A.alloc([8, 128], F32)
    FB = A.alloc([8, 128], BF16); Fq = A.alloc([8, 128], BF16)
    BF_ = A.alloc([8, 128], BF16); Bl = [A.alloc([8, 128], BF16) for _ in range(2)]
    qkT = A.alloc([8, 128], BF16)
    Dm = [A.alloc([8, 128], BF16) for _ in range(2)]; Wm = [A.alloc([8, 128], BF16) for _ in range(2)]
    ImY = A.alloc([8, 128], BF16)
    Xp = A.alloc([8, 128], BF16); VN = A.alloc([8, 128], BF16); Kd = A.alloc([8, 128], BF16)
    tmpf = Rm
    S32 = A.alloc([8, 128], F32); Sbf = A.alloc([8, 128], BF16)
    o1 = Fm; osb = [A.alloc([8, 128], F32) for _ in range(2)]

    A_lg = [A.alloc([16], F32) for _ in range(3)]

    def gdn_sweep2(dirn, order, out_tiles, o_dram, extra=None):
        c0 = {"F": 2, "B": 13}[dirn]
        Uc = cF[:, CF[dirn], :]; SUc = cF[:, CF[dirn] + 1, :]
        UIm = cB[:, c0 + 2, :]; SUm = cB[:, c0 + 3, :]
        lvl = [cB[:, c0 + 4 + l, :] for l in range(7)]
        dcol = 0 if dirn == "F" else 16
        dsc = 0 if dirn == "F" else 8
        P.op("pool", lambda e: e.memset(S32, 0.0), writes=["S32a", "S32b"])
        P.op("pool", lambda e: e.memset(Sbf, 0.0), writes=["Sbfa", "Sbfb"])
        n_proc = len(order)
        order = list(order) + ([extra] if extra is not None else [])
        n_ord = len(order)
        asc = (dirn == "F")

        def seq_of(g):
            return 0 if g < NTC else 1

        def v4(b):
            return pf(b).rearrange("p (a t) -> p a t", a=4)

        def project(n):
            g = order[n]
            rows, v = tile_rows(g)
            h = hT[n % 2]
            hn = "hT%d" % (n % 2)
            make_hT(rows, v, h, hn)
            yield
            xw = Xw[n % 3]
            xwn = "Xw%d" % (n % 3)
            for c3 in range(8):
                b = nb()
                for j in range(3):
                    ct = c3 * 3 + j
                    for k in range(8):
                        mm(pf(b)[:, j * 128:(j + 1) * 128], Wqkv[:, k, ct * 128:(ct + 1) * 128], h[:, k, :], k == 0, k == 7, ["Wqkv", hn], b)
                cp("act" if c3 % 2 else "dve", xw[:, c3 * 3:(c3 + 1) * 3, 2:130], pf(b)[:, 0:384].rearrange("p (a t) -> p a t", a=3), [PB(b)], [xwn])
                yield
            xe = Xe[n % 4]
            cp("pool", xe[:, :, 0:2], xw[:, :, 2:4], [xwn], ["Xe%d" % (n % 4)])
            cp("pool", xe[:, :, 2:4], xw[:, :, 128:130], [xwn], ["Xe%d" % (n % 4)])
            b = nb()
            for k in range(8):
                mm(pf(b)[:, 0:16], h[:, k, :], Wbd[:, k, dcol:dcol + 16], k == 0, k == 7, [hn, "Wbd"], b)
            cp("act", A_lg[n % 3], pf(b)[:, 0:16], [PB(b)], ["lg%d" % (n % 3)])

        def chunk(m, fill):
            g = order[m]
            want_out = g in out_tiles
            xw = Xw[m % 3]
            xwn = "Xw%d" % (m % 3)
            lg = A_lg[m % 3]
            lgn = "lg%d" % (m % 3)

            def nbr(mm_):
                if mm_ < 0 or mm_ >= n_ord or seq_of(order[mm_]) != seq_of(g):
                    return None
                return Xe[mm_ % 4], "Xe%d" % (mm_ % 4)
            prv = nbr(m - 1) if asc else nbr(m + 1)
            nxt = nbr(m + 1) if asc else nbr(m - 1)
            if prv is not None:
                cp("pool", xw[:, :, 0:2], prv[0][:, :, 2:4], [prv[1], xwn], [xwn])
            else:
                P.op("pool", lambda e, xw=xw: e.memset(xw[:, :, 0:2], 0.0), reads=[xwn], writes=[xwn])
            if nxt is not None:
                cp("pool", xw[:, :, 130:132], nxt[0][:, :, 0:2], [nxt[1], xwn], [xwn])
            else:
                P.op("pool", lambda e, xw=xw: e.memset(xw[:, :, 130:132], 0.0), reads=[xwn], writes=[xwn])
            for c4 in range(6):
                b = nb()
                for j in range(4):
                    ct = c4 * 4 + j
                    o_ = pf(b)[:, j * 128:(j + 1) * 128]
                    for i in range(5):
                        mm(o_, diagW[:, ct * 5 + i, :], xw[:, ct, i:i + 128], i == 0, i == 4, ["diagW", xwn], b)
                act(YT[:, c4 * 4:(c4 + 1) * 4, :], v4(b), AF.Silu, [PB(b)], ["YT"])
            fill()
            tt("pool", SQ, YT[:, 0:16, :], YT[:, 0:16, :], ALU.mult, ["YT"], ["SQ"])
            for c4 in range(4):
                b = nb()
                for j in range(4):
                    mm(pf(b)[:, j * 128:(j + 1) * 128], onesB, SQ[:, c4 * 4 + j, :], True, True, ["cB", "SQ"], b)
                act(RS[:, c4 * 4:(c4 + 1) * 4, :], v4(b), AF.Ln, [PB(b)], ["RS%d" % c4], bias=cols[:, 0:1])
            act(RS[:, 0:8, :], RS[:, 0:8, :], AF.Exp, ["RS0", "RS1"], ["RS0", "RS1"], scale=-0.5, bias=cols[:, 2:3])
            act(RS[:, 8:16, :], RS[:, 8:16, :], AF.Exp, ["RS2", "RS3"], ["RS2", "RS3"], scale=-0.5)
            tt("dve", QKN, YT[:, 0:16, :], RS, ALU.mult, ["YT", "RS0", "RS1", "RS2", "RS3"], ["QKN"])
            bk = nb()
            for h in range(8):
                tr(pb16(bk)[:, h * 128:(h + 1) * 128], QKN[:, 8 + h, :], identB, ["QKN", "cB"], bk)
            bv = nb()
            for h in range(8):
                tr(pb16(bv)[:, h * 128:(h + 1) * 128], YT[:, 16 + h, :], identB, ["YT", "cB"], bv)
            cp("act", Ktok, pb16(bk).rearrange("p (a t) -> p a t", a=8), [PB(bk)], ["Ktok"])
            cp("dve", Vtok, pb16(bv).rearrange("p (a t) -> p a t", a=8), [PB(bv)], ["Vtok"])
            fill()
            beta = sca[:, 0, :]; xx = sca[:, 1, :]; ax = sca[:, 2, :]; g_ = sca[:, 3, :]
            egc = sca[:, 4, :]; negegc = sca[:, 5, :]; gcs = sca[:, 6, :]; edl = sca[:, 7, :]; etot = sca[:, 8, :]
            act(beta, lg[:, 0:8], AF.Sigmoid, [lgn], ["beta"])
            tt("dve", xx, lg[:, 8:16], dtb[:, dsc:dsc + 8], ALU.add, [lgn, "dtb"], ["xx"])
            ts("dve", ax, xx, -1.0, ALU.mult, ["xx"], ["ax"])
            tt("dve", ax, ax, xx, ALU.min, ["xx", "ax"], ["ax"])
            act(ax, ax, AF.Exp, ["ax"], ["ax"])
            act(ax, ax, AF.Ln, ["ax"], ["ax"], bias=cols[:, 1:2])
            ts("dve", xx, xx, 0.0, ALU.max, ["xx"], ["xx"])
            tt("dve", xx, xx, ax, ALU.add, ["xx", "ax"], ["xx"])
            tt("dve", g_, xx, negA[:, dsc:dsc + 8], ALU.mult, ["xx", "negA"], ["g"])
            b = nb()
            mm(pf(b)[:, 0:8], Uc, g_, True, True, ["cF", "g"], b)
            mm(pf(b)[:, 8:16], onesF, g_, True, True, ["cF", "g"], b)
            act(egc, pf(b)[:, 0:8], AF.Exp, [PB(b)], ["egc"])
            act(gcs, pf(b)[:, 0:8], AF.Copy, [PB(b)], ["gcs"])
            act(etot, pf(b)[:, 8:16], AF.Exp, [PB(b)], ["etot"])
            ts("dve", negegc, egc, -1.0, ALU.mult, ["egc"], ["negegc"])
            tt("dve", edl, pf(b)[:, 8:16], gcs, ALU.subtract, [PB(b), "gcs", "etot", "egc"], ["edl"])
            act(edl, edl, AF.Exp, ["edl"], ["edl"])
            fill()
            tt("pool", Rm, bc(Uc.unsqueeze(1), [128, 8, 128]), bc(g_.unsqueeze(2), [128, 8, 128]), ALU.mult, ["cF", "g"], ["Rm"])
            for hh in range(2):
                b = nb()
                for j in range(4):
                    mm(pf(b)[:, j * 128:(j + 1) * 128], SUc, Rm[:, hh * 4 + j, :], True, True, ["cF", "Rm"], b)
                act(Fm[:, hh * 4:(hh + 1) * 4, :], v4(b), AF.Exp, [PB(b)], ["Fm%d" % hh])
            tt("pool", FB, Fm, bc(SUm.unsqueeze(1), [128, 8, 128]), ALU.mult, ["Fm0", "Fm1", "cB"], ["FB"])
            if want_out:
                tt("pool", Fq, Fm, bc(UIm.unsqueeze(1), [128, 8, 128]), ALU.mult, ["Fm0", "Fm1", "cB"], ["Fq"])
            for hh in range(2):
                sl = slice(hh * 4, (hh + 1) * 4)
                b = nb()
                for j in range(4):
                    h = hh * 4 + j
                    mm(pf(b)[:, j * 128:(j + 1) * 128], QKN[:, 8 + h, :], QKN[:, 8 + h, :], True, True, ["QKN"], b)
                tt("dve", BF_[:, sl, :], v4(b), FB[:, sl, :], ALU.mult, [PB(b), "FB"], ["BF%d" % hh])
                if want_out:
                    b2 = nb()
                    for j in range(4):
                        h = hh * 4 + j
                        mm(pf(b2)[:, j * 128:(j + 1) * 128], QKN[:, 8 + h, :], QKN[:, h, :], True, True, ["QKN"], b2)
                    tt("dve", qkT[:, sl, :], v4(b2), Fq[:, sl, :], ALU.mult, [PB(b2), "Fq"], ["qkT%d" % hh])
            tt("pool", Dm[0], bc(identB.unsqueeze(1), [128, 8, 128]), bc(beta.unsqueeze(2), [128, 8, 128]), ALU.mult, ["cB", "beta"], ["D0a", "D0b"])
            hs = ("a", "b")
            for l in range(7):
                fill()
                cur, nx = l % 2, (l + 1) % 2
                Wcur = Dm[0] if l == 0 else Wm[cur]
                wn_ = (lambda hh_: "D0" + hs[hh_]) if l == 0 else (lambda hh_, cur=cur: "W%d%s" % (cur, hs[hh_]))
                Bc = Bl[l % 2]
                bn = "Bl%d" % (l % 2)
                tt("pool", Bc, BF_, bc(lvl[l].unsqueeze(1), [128, 8, 128]), ALU.mult, ["BF0", "BF1", "cB"], [bn])
                for hh in range(2):
                    sl = slice(hh * 4, (hh + 1) * 4)
                    b = nb()
                    for j in range(4):
                        h = hh * 4 + j
                        mm(pf(b)[:, j * 128:(j + 1) * 128], Bc[:, h, :], Dm[cur][:, h, :], True, True, [bn, "D%d%s" % (cur, hs[hh])], b)
                    tt("dve", ImY[:, sl, :], bc(identB.unsqueeze(1), [128, 4, 128]), v4(b), ALU.subtract, [PB(b), "cB"], ["ImY%d" % hh])
                for hh in range(2):
                    sl = slice(hh * 4, (hh + 1) * 4)
                    if l < 6:
                        b = nb()
                        for j in range(4):
                            h = hh * 4 + j
                            mm(pf(b)[:, j * 128:(j + 1) * 128], Wcur[:, h, :], ImY[:, h, :], True, True, [wn_(hh), "ImY%d" % hh], b)
                        cp("act", Dm[nx][:, sl, :], v4(b), [PB(b)], ["D%d%s" % (nx, hs[hh])])
                    b = nb()
                    for j in range(4):
                        h = hh * 4 + j
                        mm(pf(b)[:, j * 128:(j + 1) * 128], ImY[:, h, :], Wcur[:, h, :], True, True, [wn_(hh), "ImY%d" % hh], b)
                    cp("act" if hh else "dve", Wm[nx][:, sl, :], v4(b), [PB(b)], ["W%d%s" % (nx, hs[hh])])
            WT = Wm[1]
            tt("pool", Kd, Ktok, bc(edl.unsqueeze(2), [128, 8, 128]), ALU.mult, ["Ktok", "edl"], ["Kd"])
            for hh in range(2):
                sl = slice(hh * 4, (hh + 1) * 4)
                b = nb()
                for j in range(4):
                    h = hh * 4 + j
                    mm(pf(b)[:, j * 128:(j + 1) * 128], QKN[:, 8 + h, :], Sbf[:, h, :], True, True, ["QKN", "Sbf" + hs[hh]], b)
                tt("dve", tmpf[:, sl, :], v4(b), bc(negegc[:, sl].unsqueeze(2), [128, 4, 128]), ALU.mult, [PB(b), "negegc"], ["Rm"])
                tt("dve", Xp[:, sl, :], tmpf[:, sl, :], Vtok[:, sl, :], ALU.add, ["Rm", "Vtok"], ["Xp%d" % hh])
            for hh in range(2):
                sl = slice(hh * 4, (hh + 1) * 4)
                b = nb()
                for j in range(4):
                    h = hh * 4 + j
                    mm(pf(b)[:, j * 128:(j + 1) * 128], WT[:, h, :], Xp[:, h, :], True, True, ["W1%s" % hs[hh], "Xp%d" % hh], b)
                cp("act", VN[:, sl, :], v4(b), [PB(b)], ["VN%d" % hh])
            if want_out:
                ot = out_tiles[g]
                ob_ = osb[ot % 2]
                obn = "osb%d" % (ot % 2)
                for hh in range(2):
                    sl = slice(hh * 4, (hh + 1) * 4)
                    b = nb()
                    for j in range(4):
                        h = hh * 4 + j
                        mm(pf(b)[:, j * 128:(j + 1) * 128], QKN[:, h, :], Sbf[:, h, :], True, True, ["QKN", "Sbf" + hs[hh]], b)
                    tt("dve", o1[:, sl, :], v4(b), bc(egc[:, sl].unsqueeze(2), [128, 4, 128]), ALU.mult, [PB(b), "egc"], ["Fm%d" % hh])
                    b = nb()
                    for j in range(4):
                        h = hh * 4 + j
                        mm(pf(b)[:, j * 128:(j + 1) * 128], qkT[:, h, :], VN[:, h, :], True, True, ["qkT%d" % hh, "VN%d" % hh], b)
                    tt("dve", ob_[:, sl, :], v4(b), o1[:, sl, :], ALU.add, [PB(b), "Fm%d" % hh], [obn + hs[hh]])
                dma("pool", o_dram[ot * 128:(ot + 1) * 128, :], ob_.rearrange("p a t -> p (a t)"), [obn + "a", obn + "b"], [obn + "a", obn + "b"])
            for hh in range(2):
                sl = slice(hh * 4, (hh + 1) * 4)
                b = nb()
                for j in range(4):
                    h = hh * 4 + j
                    mm(pf(b)[:, j * 128:(j + 1) * 128], Kd[:, h, :], VN[:, h, :], True, True, ["Kd", "VN%d" % hh], b)
                for j in range(4):
                    h = hh * 4 + j
                    stt(S32[:, h, :], S32[:, h, :], etot[:, h:h + 1], pf(b)[:, j * 128:(j + 1) * 128], ALU.mult, ALU.add,
                        [PB(b), "S32" + hs[hh], "etot"], ["S32" + hs[hh]])
                cp("act", Sbf[:, sl, :], S32[:, sl, :], ["S32" + hs[hh]], ["Sbf" + hs[hh]])

        def drain(gen):
            for _ in gen:
                pass

        drain(project(0))
        if n_ord > 1:
            drain(project(1))
        for m in range(n_proc):
            gen = project(m + 2) if m + 2 < n_ord else iter(())

            def fill(gen=gen):
                next(gen, None)
            chunk(m, fill)
            drain(gen)

    own = {NTC + t: t for t in range(NTO)}
    orderF = list(range(NTC)) + [NTC + t for t in range(NTO)]
    orderB = list(range(NTC - 1, -1, -1)) + [NTC + t for t in range(NTL - 1, -1, -1)]
    P.barrier()
    gdn_sweep2("F", orderF, own, of_d, extra=NTC + NTO)
    if stop == 1:
        return nc, P.emit(nc)
    gdn_sweep2("B", orderB, own, ob_d)
    P.barrier()
    A.release(mg)

    if stop == 2:
        return nc, P.emit(nc)
    mm_ = A.mark()
    Wuk = A.alloc([2, 1024], BF16); Wuv = A.alloc([2, 1024], BF16)
    ckvnT = A.alloc([2, NT], BF16); krT = A.alloc([NT], BF16)
    rstdk = A.alloc([NG, 8], F32)
    m2b = A.mark()
    Wmla = A.alloc([8, 704], BF16); Wuq = A.alloc([3, 1536], BF16)
    load_w(Wmla, wmla_d, D, 704, "Wmla"); load_w(Wuq, wuq_d, 384, 1536, "Wuq")
    load_w(Wuk, wuk_d, 256, 1024, "Wuk"); load_w(Wuv, wuv_d, 256, 1024, "Wuv")
    qan_b = A.alloc([384], F32); kvan_b = A.alloc([256], F32); gq_b = A.alloc([8, 192], F32); gk_b = A.alloc([192], F32)
    dma("sp", qan_b, qan_d.partition_broadcast(128), [], ["qan"])
    dma("sp", kvan_b, kvan_d.partition_broadcast(128), [], ["kvan"])
    dma("sp", gq_b[:, 0, :], qn_d.partition_broadcast(128), [], ["gq"])
    dma("sp", gk_b, kn_d.partition_broadcast(128), [], ["gk"])
    tt("dve", gq_b[:, 0, 0:128], gq_b[:, 0, 0:128], gk_b[:, 0:128], ALU.mult, ["gq", "gk"], ["gq"])
    for h in range(1, 8):
        cp("pool", gq_b[:, h, :], gq_b[:, 0, :], ["gq"], ["gq"])
    cqnT = A.alloc([3, NO], BF16)
    hT2 = A.alloc([8, 128], BF16)
    lat = A.alloc([704], F32); latn = A.alloc([640], BF16)
    sqa = A.alloc([256], F32); sqk = A.alloc([1024], F32); sqq = A.alloc([1536], F32); sqc = A.alloc([384], F32)
    sqr = A.alloc([64], F32)
    ssm = A.alloc([48], F32)
    cs = A.alloc([2, 64], F32)
    kr32 = A.alloc([64], F32); krt = A.alloc([64], F32); krtmp = A.alloc([64], F32); krb = A.alloc([128], BF16)
    q32 = A.alloc([8, 192], F32); qrt = A.alloc([8, 64], F32); qtmp = A.alloc([8, 64], F32); qbf = A.alloc([8, 192], BF16)
    qTs = A.alloc([8, 2, 128], BF16)

    def rope(dst, src, tmp, cosv, sinv, nh, rd, nm):
        s5 = src.rearrange("p h (a c f) -> p h a c f", a=2, c=2)
        t5 = tmp.rearrange("p h (a c f) -> p h a c f", a=2, c=2)
        sn5 = sinv.rearrange("p (a c f) -> p a c f", a=2, c=2)
        for c in range(2):
            for a_ in range(2):
                tt("dve", t5[:, :, a_, c, :], s5[:, :, a_, 1 - c, :], bc(sn5[:, a_, c, :].unsqueeze(1), [128, nh, 16]), ALU.mult,
                   rd, [nm + "tmp"])
        tt("dve", dst, src, bc(cosv.unsqueeze(1), [128, nh, 64]), ALU.mult, rd, [nm])
        tt("dve", dst, dst, tmp, ALU.add, [nm, nm + "tmp"], [nm])

    lat2 = [lat, A.alloc([704], F32)]
    cs2 = [cs, A.alloc([2, 64], F32)]

    def p2b_front(g):
        rows, v = tile_rows(g)
        is_own = g in own
        lat = lat2[g % 2]; latn_ = "lat%d" % (g % 2)
        cs = cs2[g % 2]; csn = "cs%d" % (g % 2)
        make_hT(rows, v, hT2, "hT2")
        ncol = 704 if is_own else 320
        c_lo = 0 if is_own else 384
        b0 = nb()
        w0 = min(512, ncol)
        for k in range(8):
            mm(pf(b0)[:, 0:w0], hT2[:, k, :], Wmla[:, k, c_lo:c_lo + w0], k == 0, k == 7, ["hT2", "Wmla"], b0)
        cp("dve", lat[:, c_lo:c_lo + w0], pf(b0)[:, 0:w0], [PB(b0)], [latn_])
        if ncol > 512:
            b1 = nb()
            for k in range(8):
                mm(pf(b1)[:, 0:ncol - 512], hT2[:, k, :], Wmla[:, k, 512:ncol], k == 0, k == 7, ["hT2", "Wmla"], b1)
            cp("act", lat[:, 512:ncol], pf(b1)[:, 0:ncol - 512], [PB(b1)], [latn_])
        dma("sp", cs[:, 0, :], cos_d[g * 128:(g + 1) * 128, :], [], [csn])
        dma("sp", cs[:, 1, :], sin_d[g * 128:(g + 1) * 128, :], [], [csn])

    p2b_front(0)
    for g in range(NG):
        if g + 1 < NG:
            p2b_front(g + 1)
        is_own = g in own
        lat = lat2[g % 2]; cs = cs2[g % 2]
        LAT = "lat%d" % (g % 2); CS = "cs%d" % (g % 2)
        act(sqa, lat[:, 384:640], AF.Square, [LAT], ["sqa", "ss0"], accum=ssm[:, 0:1])
        rsqrt_small(ssm[:, 1:2], ssm[:, 0:1], 1.0 / 256, ["ss0"], ["ss1k"], ssm[:, 2:3], "ss2k")
        stt(latn[:, 384:640], lat[:, 384:640], ssm[:, 1:2], kvan_b, ALU.mult, ALU.mult, [LAT, "ss1k", "kvan"], ["latn_kv"])
        b = nb()
        for j in range(2):
            tr(pb16(b)[:, j * 128:(j + 1) * 128], latn[:, 384 + j * 128:384 + (j + 1) * 128], identB, ["latn_kv", "cB"], b)
        cp("act", ckvnT[:, :, g * 128:(g + 1) * 128], pb16(b)[:, 0:256].rearrange("p (a t) -> p a t", a=2), [PB(b)], ["ckvnT"])
        if sub == 2:
            return nc, P.emit(nc)
        for half in range(2):
            b = nb()
            for kk in range(2):
                mm(pf(b), ckvnT[:, kk, g * 128:(g + 1) * 128], Wuk[:, kk, half * 512:(half + 1) * 512], kk == 0, kk == 1, ["ckvnT", "Wuk"], b)
            act(sqk[:, half * 512:(half + 1) * 512], pf(b), AF.Square, [PB(b)], ["sqk"])
        red(ssm[:, 8:16], sqk.rearrange("p (h d) -> p h d", h=8), ["sqk"], ["ss8"])
        act(sqr, lat[:, 640:704], AF.Square, [LAT], ["sqr", "ss3"], accum=ssm[:, 3:4])
        ts("dve", ssm[:, 8:16], ssm[:, 8:16], ssm[:, 3:4], ALU.add, ["ss8", "ss3"], ["ss8"])
        act(ssm[:, 16:24], ssm[:, 8:16], AF.Ln, ["ss8"], ["ss16"], scale=1.0 / 192, bias=cols[:, 0:1])
        act(rstdk[:, g, :], ssm[:, 16:24], AF.Exp, ["ss16"], ["rstdk"], scale=-0.5, bias=cols[:, 4:5])
        if sub == 3:
            return nc, P.emit(nc)
        tt("dve", kr32, lat[:, 640:704], gk_b[:, 128:192], ALU.mult, [LAT, "gk"], ["kr32"])
        rope(krt.unsqueeze(1), kr32.unsqueeze(1), krtmp.unsqueeze(1), cs[:, 0, :], cs[:, 1, :], 1, ["kr32", CS], "krt")
        cp("dve", krb[:, 0:64], krt, ["krt"], ["krb"])
        cp("dve", krb[:, 64:128], krt, ["krt"], ["krb"])
        b = nb()
        tr(pb16(b)[:, 0:128], krb, identB, ["krb", "cB"], b)
        cp("act", krT[:, g * 128:(g + 1) * 128], pb16(b)[:, 0:128], [PB(b)], ["krT"])
        if sub == 4:
            return nc, P.emit(nc)
        if is_own:
            ot = own[g]
            act(sqc, lat[:, 0:384], AF.Square, [LAT], ["sqc", "ss4"], accum=ssm[:, 4:5])
            rsqrt_small(ssm[:, 5:6], ssm[:, 4:5], 1.0 / 384, ["ss4"], ["ss5"], ssm[:, 6:7], "ss6")
            stt(latn[:, 0:384], lat[:, 0:384], ssm[:, 5:6], qan_b, ALU.mult, ALU.mult, [LAT, "ss5", "qan"], ["latn_q"])
            b = nb()
            for j in range(3):
                tr(pb16(b)[:, j * 128:(j + 1) * 128], latn[:, j * 128:(j + 1) * 128], identB, ["latn_q", "cB"], b)
            cp("act", cqnT[:, :, ot * 128:(ot + 1) * 128], pb16(b)[:, 0:384].rearrange("p (a t) -> p a t", a=3), [PB(b)], ["cqnT"])
            if sub == 5:
                return nc, P.emit(nc)
            q2 = q32.rearrange("p h d -> p (h d)")
            for j in range(3):
                b = nb()
                for kk in range(3):
                    mm(pf(b), cqnT[:, kk, ot * 128:(ot + 1) * 128], Wuq[:, kk, j * 512:(j + 1) * 512], kk == 0, kk == 2, ["cqnT", "Wuq"], b)
                cp("dve", q2[:, j * 512:(j + 1) * 512], pf(b), [PB(b)], ["q32"])
                act(sqq[:, j * 512:(j + 1) * 512], q2[:, j * 512:(j + 1) * 512], AF.Square, ["q32"], ["sqq"])
            red(ssm[:, 24:32], sqq.rearrange("p (h d) -> p h d", h=8), ["sqq"], ["ss24"])
            act(ssm[:, 32:40], ssm[:, 24:32], AF.Ln, ["ss24"], ["ss32"], scale=1.0 / 192, bias=cols[:, 0:1])
            act(ssm[:, 32:40], ssm[:, 32:40], AF.Exp, ["ss32"], ["ss32"], scale=-0.5)
            tt("dve", q32, q32, bc(ssm[:, 32:40].unsqueeze(2), [128, 8, 192]), ALU.mult, ["q32", "ss32"], ["q32"])
            tt("dve", q32, q32, gq_b, ALU.mult, ["q32", "gq"], ["q32"])
            rope(qrt, q32[:, :, 128:192], qtmp, cs[:, 0, :], cs[:, 1, :], 8, ["q32", CS], "qrt")
            cp("dve", qbf[:, :, 0:128], q32[:, :, 0:128], ["q32"], ["qbfn"])
            cp("pool", qbf[:, :, 128:192], qrt, ["qrt"], ["qbfr"])
            if sub == 6:
                return nc, P.emit(nc)
            for hh in range(2):
                bn_ = nb()
                for j in range(4):
                    h = hh * 4 + j
                    tr(pb16(bn_)[:, j * 128:(j + 1) * 128], qbf[:, h, 0:128], identB, ["qbfn", "cB"], bn_)
                cp("act", qTs[:, hh * 4:(hh + 1) * 4, 0, :], pb16(bn_)[:, 0:512].rearrange("p (a t) -> p a t", a=4), [PB(bn_)], ["qTs%d" % hh])
                br_ = nb()
                for j in range(4):
                    h = hh * 4 + j
                    tr(pb16(br_)[0:64, j * 128:(j + 1) * 128], qbf[:, h, 128:192], identB, ["qbfr", "cB"], br_)
                cp("dve", qTs[0:64, hh * 4:(hh + 1) * 4, 1, :], pb16(br_)[0:64, 0:512].rearrange("p (a t) -> p a t", a=4), [PB(br_)], ["qTr%d" % hh])
            if sub == 7:
                return nc, P.emit(nc)
            dma("pool", qt_d[:, 0:128, ot * 128:(ot + 1) * 128].rearrange("h d t -> d h t"), qTs[:, :, 0, :], ["qTs0", "qTs1"], ["qTs0", "qTs1", "qt_d"])
            dma("pool", qt_d[:, 128:192, ot * 128:(ot + 1) * 128].rearrange("h d t -> d h t"), qTs[0:64, :, 1, :], ["qTr0", "qTr1"], ["qTr0", "qTr1", "qt_d"])

    if stop == 3:
        return nc, P.emit(nc)
    P.barrier()
    A.release(m2b)
    m3 = A.mark()
    KnT = A.alloc([NT], BF16); Vh = A.alloc([NG, 128], BF16)
    QTn = A.alloc([NO], BF16); QTr = A.alloc([NO], BF16)
    PT = [A.alloc([512], BF16) for _ in range(4)]
    rec = A.alloc([512], F32); acc = A.alloc([512], F32)
    oT = [A.alloc([512], BF16) for _ in range(2)]
    NQB = NO // 512
    NKB = (NT + 511) // 512
    bank_state.update(n=0, lo=0, hi=4)
    for h in range(8):
        dma("sp", QTn, qt_d[h, 0:128, :], ["qt_d"], ["QTn"])
        dma("sp", QTr[0:64, :], qt_d[h, 128:192, :], ["qt_d"], ["QTr"])
        dma("sp", QTr[64:128, :], qt_d[h, 128:192, :], ["qt_d"], ["QTr"])
        for kb in range(NKB):
            c0, c1 = kb * 512, min(NT, (kb + 1) * 512)
            b = nb()
            for kk in range(2):
                mm(pf(b)[:, 0:c1 - c0], Wuk[:, kk, h * 128:(h + 1) * 128], ckvnT[:, kk, c0:c1], kk == 0, kk == 1, ["Wuk", "ckvnT"], b)
            cp("dve" if kb % 2 else "act", KnT[:, c0:c1], pf(b)[:, 0:c1 - c0], [PB(b)], ["KnT"])
        for g4 in range(0, NG, 4):
            ng = min(4, NG - g4)
            b = nb()
            for j in range(ng):
                for kk in range(2):
                    mm(pf(b)[:, j * 128:(j + 1) * 128], ckvnT[:, kk, (g4 + j) * 128:(g4 + j + 1) * 128], Wuv[:, kk, h * 128:(h + 1) * 128],
                       kk == 0, kk == 1, ["ckvnT", "Wuv"], b)
            cp("dve", Vh[:, g4:g4 + ng, :], pf(b)[:, 0:ng * 128].rearrange("p (a t) -> p a t", a=ng), [PB(b)], ["Vh"])
        for qb in range(NQB):
            qs = slice(qb * 512, (qb + 1) * 512)
            ba, bs_ = (4, 5) if (h * NQB + qb) % 2 == 0 else (6, 7)

            def st2(gp):
                g0, g1 = 2 * gp, 2 * gp + 1
                b0 = nb(); b1 = nb()
                mm(pf(b0), KnT[:, g0 * 128:(g0 + 1) * 128], QTn[:, qs], True, False, ["KnT", "QTn"], b0)
                mm(pf(b1), KnT[:, g1 * 128:(g1 + 1) * 128], QTn[:, qs], True, False, ["KnT", "QTn"], b1)
                mm(pf(b0), krT[0:64, g0 * 128:(g0 + 1) * 128], QTr[0:64, qs], False, True, ["krT", "QTr"], b0)
                mm(pf(b1), krT[64:128, g1 * 128:(g1 + 1) * 128], QTr[64:128, qs], False, True, ["krT", "QTr"], b1)
                return (b0, b1)
            assert NG % 2 == 0
            bc_ = st2(0)
            for gp in range(NG // 2):
                bn_ = st2(gp + 1) if gp + 1 < NG // 2 else None
                for u in range(2):
                    g = 2 * gp + u
                    pt = PT[g % 4]
                    ptn = "PT%d" % (g % 4)
                    act(pt, pf(bc_[u]), AF.Exp, [PB(bc_[u]), "rstdk"], [ptn], scale=rstdk[:, g, h:h + 1])
                    mm(pf(ba), Vh[:, g, :], pt, g == 0, g == NG - 1, ["Vh", ptn], ba)
                    if g == 0:
                        cp("dve", acc, pt, [ptn], ["acc"])
                    else:
                        tt("dve", acc, acc, pt, ALU.add, ["acc", ptn], ["acc"])
                bc_ = bn_
            mm(pf(bs_), onesF, acc, True, True, ["cF", "acc"], bs_)
            P.op("dve", lambda e, bs_=bs_: e.reciprocal(out=rec, in_=pf(bs_)), reads=[PB(bs_)], writes=["rec"])
            o_ = oT[qb % 2]
            on = "oT%d" % (qb % 2)
            tt("dve", o_, pf(ba), rec, ALU.mult, [PB(ba), "rec"], [on])
            dma("pool", om_d[h, :, qs], o_, [on], [on, "om_d"])
    bank_state.update(n=0, lo=0, hi=8)
    P.barrier()
    A.release(m3)
    A.release(mm_)

    if stop == 4:
        return nc, P.emit(nc)
    m4 = A.mark()
    Wzg = A.alloc([8, 3072], BF16); Wg = A.alloc([8, D], BF16); Wmm = A.alloc([8, D], BF16); Wo = A.alloc([8, D], BF16)
    load_w(Wzg, wzg_d, D, 3072, "Wzg"); load_w(Wg, wg_d, D, D, "Wg"); load_w(Wmm, wm_d, D, D, "Wmm"); load_w(Wo, wo_d, D, D, "Wo")
    modb = A.alloc([2, D], F32)
    dgm = A.alloc([128], F32)
    for vi, e_ in enumerate((2, 5)):
        for t in range(8):
            ts("dve", dgm, identF, modF[:, e_, 0, t:t + 1], ALU.mult, ["cF", "modF"], ["dgm"])
            b = nb()
            mm(pf(b)[:, 0:128], onesF, dgm, True, True, ["cF", "dgm"], b)
            cp("act", modb[:, vi, t * 128:(t + 1) * 128], pf(b)[:, 0:128], [PB(b)], ["modb"])
    gon_b = A.alloc([128], F32)
    dma("sp", gon_b, gon_d.partition_broadcast(128), [], ["gon"])
    hT4s = [A.alloc([8, 128], BF16) for _ in range(2)]
    xks = [A.alloc([D], F32) for _ in range(2)]
    zs = A.alloc([D], F32); sg = A.alloc([2 * D], F32)
    ofb = A.alloc([D], F32); obb = A.alloc([D], F32)
    ssg = A.alloc([24], F32)
    yg = A.alloc([D], BF16); ygT = A.alloc([8, 128], BF16); omT = A.alloc([8, 128], BF16)
    t32 = A.alloc([D], F32); t32b = A.alloc([D], F32); ybf = A.alloc([D], BF16); yT = A.alloc([8, 128], BF16)
    x1t = [A.alloc([D], F32) for _ in range(2)]
    def p4a_front(ot):
        make_hT(x_d[ot * 128:(ot + 1) * 128, :], 0, hT4s[ot % 2], "hT4_%d" % (ot % 2), xt_keep=xks[ot % 2])

    p4a_front(0)
    for ot in range(NTO):
        if ot + 1 < NTO:
            p4a_front(ot + 1)
        hT4 = hT4s[ot % 2]; xk = xks[ot % 2]
        HT4 = "hT4_%d" % (ot % 2)
        dma("sp", ofb, of_d[ot * 128:(ot + 1) * 128, :], [], ["ofb"])
        dma("sp", obb, ob_d[ot * 128:(ot + 1) * 128, :], [], ["obb"])
        dma("sp", omT, om_d[:, :, ot * 128:(ot + 1) * 128].rearrange("h d t -> d h t"), [], ["omT"])
        for j in range(6):
            b = nb()
            for k in range(8):
                mm(pf(b), hT4[:, k, :], Wzg[:, k, j * 512:(j + 1) * 512], k == 0, k == 7, [HT4, "Wzg"], b)
            if j < 2:
                act(zs[:, j * 512:(j + 1) * 512], pf(b), AF.Silu, [PB(b)], ["zs"])
            else:
                act(sg[:, (j - 2) * 512:(j - 1) * 512], pf(b), AF.Sigmoid, [PB(b)], ["sg"])
        tt("pool", ofb, ofb, obb, ALU.add, ["ofb", "obb"], ["ofb"])
        tt("pool", t32, ofb, ofb, ALU.mult, ["ofb"], ["t32"])
        red(ssg[:, 0:8], t32.rearrange("p (h d) -> p h d", h=8), ["t32"], ["ssg0"])
        act(ssg[:, 8:16], ssg[:, 0:8], AF.Ln, ["ssg0"], ["ssg8"], scale=1.0 / 128, bias=cols[:, 0:1])
        act(ssg[:, 8:16], ssg[:, 8:16], AF.Exp, ["ssg8"], ["ssg8"], scale=-0.5)
        o3 = ofb.rearrange("p (h d) -> p h d", h=8)
        tt("dve", o3, o3, bc(ssg[:, 8:16].unsqueeze(2), [128, 8, 128]), ALU.mult, ["ofb", "ssg8"], ["ofb"])
        tt("dve", o3, o3, bc(gon_b.unsqueeze(1), [128, 8, 128]), ALU.mult, ["ofb", "gon"], ["ofb"])
        tt("dve", yg, ofb, zs, ALU.mult, ["ofb", "zs"], ["yg"])
        b = nb()
        for k in range(8):
            tr(pb16(b)[:, k * 128:(k + 1) * 128], yg[:, k * 128:(k + 1) * 128], identB, ["yg", "cB"], b)
        cp("act", ygT, pb16(b).rearrange("p (a t) -> p a t", a=8), [PB(b)], ["ygT"])
        for half in range(2):
            hs_ = slice(half * 512, (half + 1) * 512)
            b1 = nb()
            for k in range(8):
                mm(pf(b1), ygT[:, k, :], Wg[:, k, hs_], k == 0, k == 7, ["ygT", "Wg"], b1)
            b2 = nb()
            for k in range(8):
                mm(pf(b2), omT[:, k, :], Wmm[:, k, hs_], k == 0, k == 7, ["omT", "Wmm"], b2)
            tt("dve", t32[:, hs_], pf(b1), sg[:, hs_], ALU.mult, [PB(b1), "sg"], ["t32"])
            tt("dve", t32b[:, hs_], pf(b2), sg[:, D + half * 512:D + (half + 1) * 512], ALU.mult, [PB(b2), "sg"], ["t32b"])
            tt("pool", ybf[:, hs_], t32[:, hs_], t32b[:, hs_], ALU.add, ["t32", "t32b"], ["ybf"])
        b = nb()
        for k in range(8):
            tr(pb16(b)[:, k * 128:(k + 1) * 128], ybf[:, k * 128:(k + 1) * 128], identB, ["ybf", "cB"], b)
        cp("act", yT, pb16(b).rearrange("p (a t) -> p a t", a=8), [PB(b)], ["yT"])
        xo = x1t[ot % 2]
        xon = "x1t%d" % (ot % 2)
        for half in range(2):
            hs_ = slice(half * 512, (half + 1) * 512)
            b = nb()
            for k in range(8):
                mm(pf(b), yT[:, k, :], Wo[:, k, hs_], k == 0, k == 7, ["yT", "Wo"], b)
            tt("dve", t32[:, hs_], pf(b), modb[:, 0, hs_], ALU.mult, [PB(b), "modb"], ["t32"])
            tt("pool", xo[:, hs_], t32[:, hs_], xk[:, hs_], ALU.add, ["t32", HT4 + "xt"], [xon])
        dma("pool", x1_d[ot * 128:(ot + 1) * 128, :], xo, [xon], [xon, "x1_d"])
    P.barrier()
    A.release(m4)

    if stop == 5:
        return nc, P.emit(nc)
    W1 = A.alloc([8, 4 * D], BF16); W2 = A.alloc([32, D], BF16)
    load_w(W1, w1_d, D, 4 * D, "W1"); load_w(W2, w2_d, 4 * D, D, "W2")
    modb2 = A.alloc([D], F32)
    dgm2 = A.alloc([128], F32)
    for t in range(8):
        ts("dve", dgm2, identF, modF[:, 5, 0, t:t + 1], ALU.mult, ["cF", "modF"], ["dgm2"])
        b = nb()
        mm(pf(b)[:, 0:128], onesF, dgm2, True, True, ["cF", "dgm2"], b)
        cp("act", modb2[:, t * 128:(t + 1) * 128], pf(b)[:, 0:128], [PB(b)], ["modb2"])
    hT5s = [A.alloc([8, 128], BF16) for _ in range(2)]
    xk2 = [A.alloc([D], F32) for _ in range(2)]
    rl = A.alloc([4, 128], BF16); aT = A.alloc([32, 128], BF16)
    t5 = A.alloc([D], F32); outt = [A.alloc([D], F32) for _ in range(2)]
    def p4b_front(ot):
        make_hT(x1_d[ot * 128:(ot + 1) * 128, :], 0, hT5s[ot % 2], "hT5x%d" % (ot % 2), G=G2, SH=SH2, xt_keep=xk2[ot % 2])

    p4b_front(0)
    for ot in range(NTO):
        if ot + 1 < NTO:
            p4b_front(ot + 1)
        xkk = xk2[ot % 2]
        hT5 = hT5s[ot % 2]
        for f4 in range(8):
            b = nb()
            for j in range(4):
                f = f4 * 4 + j
                for k in range(8):
                    mm(pf(b)[:, j * 128:(j + 1) * 128], W1[:, k, f * 128:(f + 1) * 128], hT5[:, k, :], k == 0, k == 7, ["W1", "hT5x%d" % (ot % 2)], b)
            act(rl, pf(b).rearrange("p (a t) -> p a t", a=4), AF.Relu, [PB(b)], ["rl"])
            tt("pool", aT[:, f4 * 4:(f4 + 1) * 4, :], rl, rl, ALU.mult, ["rl"], ["aT"])
        oo = outt[ot % 2]
        for half in range(2):
            hs_ = slice(half * 512, (half + 1) * 512)
            b = nb()
            for f in range(32):
                mm(pf(b), aT[:, f, :], W2[:, f, hs_], f == 0, f == 31, ["aT", "W2"], b)
            tt("dve", t5[:, hs_], pf(b), modb2[:, hs_], ALU.mult, [PB(b), "modb2"], ["t5%d" % half])
            tt("pool", oo[:, hs_], t5[:, hs_], xkk[:, hs_], ALU.add, ["t5%d" % half, "hT5x%dxt" % (ot % 2)], ["oo%d%d" % (ot % 2, half)])
        dma("pool", out_d[ot * 128:(ot + 1) * 128, :], oo, ["oo%d0" % (ot % 2), "oo%d1" % (ot % 2)], ["oo%d0" % (ot % 2), "oo%d1" % (ot % 2)])
    nops = P.emit(nc)
    return nc, nops


def _consts():
    i = np.arange(128)
    P_, Q_ = np.meshgrid(i, i, indexing="ij")
    c = np.zeros((NCONST, 128, 128), np.float32)
    c[0] = np.eye(128); c[1] = 1.0
    c[2] = (P_ <= Q_); c[3] = (P_ > Q_); c[4] = (Q_ >= P_); c[5] = (Q_ > P_)
    c[13] = (P_ >= Q_); c[14] = (P_ < Q_); c[15] = (Q_ <= P_); c[16] = (Q_ < P_)
    for l in range(7):
        bs = 2 ** (l + 1)
        same = (P_ // bs) == (Q_ // bs)
        c[6 + l] = same & ((Q_ % bs) >= bs // 2) & ((P_ % bs) < bs // 2)
        c[17 + l] = same & ((P_ % bs) >= bs // 2) & ((Q_ % bs) < bs // 2)
    return np.ascontiguousarray(c.transpose(1, 0, 2).reshape(128, NCONST * 128))


def _fm(v, n):
    return np.ascontiguousarray(np.asarray(v, np.float32).reshape(n, 128).T)


def _core_inputs(inp, b, s, NL, NC):
    f32 = lambda a: np.ascontiguousarray(np.asarray(a, np.float32))
    w_in = np.asarray(inp["w_in"][0], np.float32)
    o_z, o_b, o_d, o_cq, o_ckv, o_kr, o_g = 3072, 4096, 4112, 4128, 4512, 4768, 4832
    x = np.asarray(inp["x"][b], np.float32)[:NL]
    ctx = np.asarray(inp["ctx"][b], np.float32)[:NC]
    conv = np.asarray(inp["conv_qkv"][0], np.float32)
    dF, dB = (0, 1) if s == 0 else (1, 0)
    pos = np.arange(NL)
    if s == 1:
        x = x[::-1]; ctx = ctx[::-1]; conv = conv[::-1]; pos = pos[::-1]
    beta = lambda d: w_in[:, o_b + 8 * d:o_b + 8 * d + 8]
    dec = lambda d: w_in[:, o_d + 8 * d:o_d + 8 * d + 8]
    w_bd = np.concatenate([beta(dF), dec(dF), beta(dB), dec(dB)], axis=1)
    inv = (10000.0 ** (-np.arange(16, dtype=np.float32) / np.float32(16))).astype(np.float32)
    row = (pos // GRID_W).astype(np.float32); col = (pos % GRID_W).astype(np.float32)
    ang = np.stack([row[:, None] * inv[None, :], col[:, None] * inv[None, :]], axis=1).astype(np.float32)
    cs_l = np.cos(ang).astype(np.float32); sn_l = np.sin(ang).astype(np.float32)
    cos_t = np.ones((NC + NL, 2, 2, 16), np.float32); sin_t = np.zeros((NC + NL, 2, 2, 16), np.float32)
    cos_t[NC:, :, 0, :] = cs_l; cos_t[NC:, :, 1, :] = cs_l
    sin_t[NC:, :, 0, :] = -sn_l; sin_t[NC:, :, 1, :] = sn_l
    w_ukv = np.asarray(inp["w_ukv"][0], np.float32).reshape(256, 8, 256)
    return {
        "x": f32(x), "ctx": f32(ctx),
        "cvec": f32(np.concatenate([_fm(inp["c"][b], 8), _fm(inp["c_ctx"], 8)], axis=1)),
        "w_mod": f32(inp["w_mod"][0]), "b_mod": _fm(inp["b_mod"][0], 48),
        "norm_attn": _fm(inp["norm_attn"][0], 8), "norm_mlp": _fm(inp["norm_mlp"][0], 8),
        "w_qkv": f32(w_in[:, 0:3072]), "w_zg": f32(np.concatenate([w_in[:, o_z:o_b], w_in[:, o_g:o_g + 2048]], axis=1)),
        "w_bd": f32(w_bd), "w_mla": f32(w_in[:, o_cq:o_g]),
        "conv_w": f32(conv.T.reshape(24, 128, 5).transpose(1, 0, 2).reshape(128, 120)),
        "a_log": f32(np.concatenate([inp["gdn_a_log"][0][dF], inp["gdn_a_log"][0][dB]])),
        "dt_bias": f32(np.concatenate([inp["gdn_dt_bias"][0][dF], inp["gdn_dt_bias"][0][dB]])),
        "gdn_out_norm": f32(inp["gdn_out_norm"][0]), "q_a_norm": f32(inp["mla_q_a_norm"][0]), "kv_a_norm": f32(inp["mla_kv_a_norm"][0]),
        "q_norm": f32(inp["q_norm"][0]), "k_norm": f32(inp["k_norm"][0]),
        "w_uq": f32(inp["w_uq"][0]), "w_uk": f32(w_ukv[:, :, 0:128].reshape(256, 1024)), "w_uv": f32(w_ukv[:, :, 128:256].reshape(256, 1024)),
        "w_bg": f32(inp["w_branch_gdn"][0]), "w_bm": f32(inp["w_branch_mla"][0]), "w_out": f32(inp["w_out"][0]),
        "w_mlp_in": f32(inp["w_mlp_in"][0]), "w_mlp_out": f32(inp["w_mlp_out"][0]),
        "consts": _consts(), "rope_cos": f32(cos_t.reshape(NC + NL, 64)), "rope_sin": f32(sin_t.reshape(NC + NL, 64)),
    }


_NC_CACHE = {}


def run(inp, B, NL, NC):
    key = (NL, NC)
    if key not in _NC_CACHE:
        _NC_CACHE[key] = build_nc(NL, NC)[0]
    nc = _NC_CACHE[key]
    cores = [(b, s) for b in range(B) for s in range(2)]
    in_maps = [_core_inputs(inp, b, s, NL, NC) for (b, s) in cores]
    res = run_bass_kernel_spmd(nc, in_maps, core_ids=list(range(len(cores))))
    NO = NL // 2
    out = np.zeros((B, NL, D), np.float32)
    for (b, s), r in zip(cores, res.results):
        o = np.asarray(r["out"], np.float32)
        if s == 0:
            out[b, :NO] = o
        else:
            out[b, NO:] = o[::-1]
    return out


def kernel(**inputs):
    inp = {k: np.asarray(v) for k, v in inputs.items()}
    B, NL, _ = inp["x"].shape
    NC = inp["ctx"].shape[1]
    return run(inp, B, NL, NC)
```

```python
import math
import numpy as np
import ml_dtypes
import concourse.bass as bass
import concourse.mybir as mybir
from concourse.bass_utils import run_bass_kernel_spmd

F32 = mybir.dt.float32
BF16 = mybir.dt.bfloat16
AF = mybir.ActivationFunctionType
ALU = mybir.AluOpType
AX = mybir.AxisListType

D = 1024
KD = 8
H = 8
EPS = 1e-6
GRID_W = 64
NCONST = 24


class Buf:
    __slots__ = ("name", "writer", "readers")

    def __init__(self, name):
        self.name = name
        self.writer = None
        self.readers = []


class Op:
    __slots__ = ("eng", "fn", "deps", "signal", "tok", "is_dma", "dsem")

    def __init__(self, eng, fn, is_dma=False):
        self.eng = eng
        self.fn = fn
        self.deps = []
        self.signal = False
        self.tok = None
        self.is_dma = is_dma
        self.dsem = None


class Prog:
    ENGS = ("pe", "act", "dve", "pool", "sp")
    NDSEM = 12

    def __init__(self):
        self.ops = []
        self.bufs = {}
        self.last = {}
        self.dma_since = []

    def _B(self, x):
        b = self.bufs.get(x)
        if b is None:
            b = Buf(x)
            self.bufs[x] = b
        return b

    def op(self, eng, fn, reads=(), writes=(), is_dma=False):
        idx = len(self.ops)
        o = Op(eng, fn, is_dma)
        deps = set()
        for r in reads:
            r = self._B(r)
            if r.writer is not None:
                deps.add(r.writer)
        for w in writes:
            w = self._B(w)
            if w.writer is not None:
                deps.add(w.writer)
            deps.update(w.readers)
        fin = []
        for d in deps:
            od = self.ops[d]
            if od.eng == eng and eng == "pe" and not od.is_dma and not is_dma:
                continue
            fin.append(d)
            od.signal = True
        o.deps = sorted(fin)
        self.ops.append(o)
        for r in reads:
            rb = self._B(r)
            if not is_dma:
                rb.readers = [q for q in rb.readers if self.ops[q].is_dma or self.ops[q].eng != eng]
            rb.readers.append(idx)
        for w in writes:
            w = self._B(w)
            w.writer = idx
            w.readers = []
        self.last[eng] = idx
        if is_dma:
            self.dma_since.append(idx)
        return idx

    def barrier(self):
        deps = sorted(set(list(self.last.values()) + self.dma_since))
        for d in deps:
            self.ops[d].signal = True
        for e in self.ENGS:
            o = Op(e, None)
            o.deps = list(deps)
            self.ops.append(o)
        self.dma_since = []
        for b in self.bufs.values():
            b.writer = None
            b.readers = []

    def emit(self, nc):
        sems = {e: nc.alloc_semaphore("S_" + e) for e in self.ENGS}
        dq = ("sp", "pool", "act")
        dsems = {e: [nc.alloc_semaphore("D_%s_%d" % (e, j)) for j in range(self.NDSEM)] for e in dq}
        cnt = {e: 0 for e in self.ENGS}
        dcnt = {e: [0] * self.NDSEM for e in dq}
        dnext = {e: 0 for e in dq}
        for o in self.ops:
            if o.is_dma:
                j = dnext[o.eng]
                dnext[o.eng] = (j + 1) % self.NDSEM
                prev = dcnt[o.eng][j]
                dcnt[o.eng][j] += 16
                o.dsem = (dsems[o.eng][j], prev)
                o.tok = (dsems[o.eng][j], dcnt[o.eng][j])
            elif o.signal and o.fn is not None:
                cnt[o.eng] += 1
                o.tok = (sems[o.eng], cnt[o.eng])
        per = {e: [] for e in self.ENGS}
        for o in self.ops:
            per[o.eng].append(o)
        ops = self.ops
        tail = [o for o in ops if o.is_dma]

        def run(ename):
            def body(eng):
                seen = {}

                def wait(tok):
                    if tok is None:
                        return
                    s, v = tok
                    k = id(s)
                    if seen.get(k, 0) >= v:
                        return
                    seen[k] = v
                    eng.wait_ge(s, v)

                for o in per[ename]:
                    for d in o.deps:
                        wait(ops[d].tok)
                    if o.fn is None:
                        continue
                    if o.is_dma:
                        s, prev = o.dsem
                        if prev > 0:
                            wait((s, prev))
                        o.fn(eng).then_inc(o.tok[0], 16)
                    else:
                        ins = o.fn(eng)
                        if o.signal:
                            ins.then_inc(o.tok[0], 1)
                if ename == "sp":
                    for o in tail:
                        wait(o.tok)
            return body

        with nc.Block() as block:
            block.tensor(run("pe"))
            block.scalar(run("act"))
            block.vector(run("dve"))
            block.gpsimd(run("pool"))
            block.sync(run("sp"))
        return len(ops)


class Arena:
    def __init__(self, nc, nbytes):
        self.t = nc.alloc_sbuf_tensor("arena", [128, nbytes // 4], F32)
        self.cap = nbytes // 4
        self.off = 0
        self.uid = 0

    def alloc(self, shape, dtype):
        n = 1
        for s in shape:
            n *= s
        words = n if dtype == F32 else (n + 1) // 2
        words = (words + 7) // 8 * 8
        assert self.off + words <= self.cap, "SBUF arena overflow %d+%d>%d" % (self.off, words, self.cap)
        v = self.t[:, self.off:self.off + words]
        self.off += words
        if dtype == BF16:
            v = v.bitcast(BF16)
        v = v[:, 0:n]
        if len(shape) == 2:
            v = v.rearrange("p (a b) -> p a b", a=shape[0])
        elif len(shape) == 3:
            v = v.rearrange("p (a b c) -> p a b c", a=shape[0], b=shape[1])
        self.uid += 1
        return v

    def mark(self):
        return self.off

    def release(self, m):
        self.off = m


def bc(ap, shape):
    return ap.to_broadcast(shape)


def build_nc(NL, NC, dbg=False, stop=99, sub=99):
    NO = NL // 2
    NT = NC + NL
    NTC = NC // 128
    NTL = NL // 128
    NTO = NO // 128
    NG = NTC + NTL
    nc = bass.Bass("TRN2", target_bir_lowering=False)

    def din(name, shape, dt=F32):
        return nc.dram_tensor(name, list(shape), dt, kind="ExternalInput").ap()

    x_d = din("x", [NL, D]); ctx_d = din("ctx", [NC, D])
    cvec_d = din("cvec", [128, 16]); wmod_d = din("w_mod", [D, 6 * D]); bmod_d = din("b_mod", [128, 48])
    nattn_d = din("norm_attn", [128, 8]); nmlp_d = din("norm_mlp", [128, 8])
    wqkv_d = din("w_qkv", [D, 3072]); wzg_d = din("w_zg", [D, 3072]); wbd_d = din("w_bd", [D, 32]); wmla_d = din("w_mla", [D, 704])
    conv_d = din("conv_w", [128, 24 * 5]); alog_d = din("a_log", [16]); dtb_d = din("dt_bias", [16])
    gon_d = din("gdn_out_norm", [128]); qan_d = din("q_a_norm", [384]); kvan_d = din("kv_a_norm", [256])
    qn_d = din("q_norm", [192]); kn_d = din("k_norm", [192])
    wuq_d = din("w_uq", [384, 1536]); wuk_d = din("w_uk", [256, 1024]); wuv_d = din("w_uv", [256, 1024])
    wg_d = din("w_bg", [D, D]); wm_d = din("w_bm", [D, D]); wo_d = din("w_out", [D, D])
    w1_d = din("w_mlp_in", [D, 4 * D]); w2_d = din("w_mlp_out", [4 * D, D])
    const_d = din("consts", [128, NCONST * 128]); cos_d = din("rope_cos", [NT, 64]); sin_d = din("rope_sin", [NT, 64])
    out_d = nc.dram_tensor("out", [NO, D], F32, kind="ExternalOutput").ap()

    def dscr(name, shape, dt):
        return nc.dram_tensor(name, list(shape), dt, kind="Internal").ap()

    of_d = dscr("scr_of", [NO, D], F32); ob_d = dscr("scr_ob", [NO, D], F32)
    qt_d = dscr("scr_qt", [H, 192, NO], BF16); om_d = dscr("scr_om", [H, 128, NO], BF16)
    x1_d = dscr("scr_x1", [NO, D], F32)

    P = Prog()
    A = Arena(nc, 207 * 1024)
    ps = [nc.alloc_psum_tensor("psb%d" % i, [128, 512], F32) for i in range(8)]
    bank_state = {"n": 0, "lo": 0, "hi": 8}

    def nb():
        r = bank_state["hi"] - bank_state["lo"]
        b = bank_state["lo"] + bank_state["n"] % r
        bank_state["n"] += 1
        return b

    def pf(b):
        return ps[b][:]

    def pb16(b):
        return ps[b][:].bitcast(BF16)

    def PB(b):
        return "ps%d" % b

    uid = [0]

    def U(prefix):
        uid[0] += 1
        return "%s#%d" % (prefix, uid[0])

    def dma(q, out, in_, reads, writes):
        P.op(q, lambda e: e.dma_start(out=out, in_=in_), reads=reads, writes=writes, is_dma=True)

    def mm(out, lhsT, rhs, start, stop, reads, bank):
        P.op("pe", lambda e: e.matmul(out, lhsT=lhsT, rhs=rhs, start=start, stop=stop), reads=reads, writes=[PB(bank)])

    def tr(out, in_, ident, reads, bank):
        P.op("pe", lambda e: e.transpose(out, in_, ident), reads=reads, writes=[PB(bank)])

    def act(out, in_, func, reads, writes, scale=None, bias=None, accum=None):
        kw = {}
        if scale is not None:
            kw["scale"] = scale
        if bias is not None:
            kw["bias"] = bias
        if accum is not None:
            kw["accum_out"] = accum
        P.op("act", lambda e: e.activation(out=out, in_=in_, func=func, **kw), reads=reads, writes=writes)

    def tt(eng, out, in0, in1, op, reads, writes):
        P.op(eng, lambda e: e.tensor_tensor(out=out, in0=in0, in1=in1, op=op), reads=reads, writes=writes)

    def ts(eng, out, in0, s1, op0, reads, writes, s2=None, op1=None):
        if op1 is None:
            P.op(eng, lambda e: e.tensor_scalar(out=out, in0=in0, scalar1=s1, scalar2=None, op0=op0), reads=reads, writes=writes)
        else:
            P.op(eng, lambda e: e.tensor_scalar(out=out, in0=in0, scalar1=s1, scalar2=s2, op0=op0, op1=op1), reads=reads, writes=writes)

    def stt(out, in0, scalar, in1, op0, op1, reads, writes):
        P.op("dve", lambda e: e.scalar_tensor_tensor(out=out, in0=in0, scalar=scalar, in1=in1, op0=op0, op1=op1), reads=reads, writes=writes)

    def cp(eng, out, in_, reads, writes):
        if eng == "act":
            act(out, in_, AF.Copy, reads, writes)
        else:
            P.op(eng, lambda e: e.tensor_copy(out=out, in_=in_), reads=reads, writes=writes)

    def red(out, in_, reads, writes):
        P.op("dve", lambda e: e.tensor_reduce(out=out, in_=in_, axis=AX.X, op=ALU.add), reads=reads, writes=writes)

    cF = A.alloc([6, 128], F32)
    cB = A.alloc([NCONST, 128], BF16)
    cols = A.alloc([8], F32)
    stage_all = A.alloc([2048], F32)
    stage = [stage_all[:, 0:1024], stage_all[:, 1024:2048]]
    cvec = A.alloc([16], F32)
    sc = A.alloc([16], F32)
    modF = A.alloc([6, 2, 8], F32)
    bmod = A.alloc([48], F32)
    nrm = A.alloc([2, 8], F32)
    G1 = A.alloc([2, 8], F32); G2 = A.alloc([8], F32)
    m0 = A.mark()
    ctemp = A.alloc([NCONST, 128], F32)
    dma("sp", ctemp.rearrange("p a b -> p (a b)"), const_d, [], ["ctemp"])
    dma("sp", cF[:, 0:4, :].rearrange("p a b -> p (a b)"), const_d[:, 0:4 * 128], [], ["cF"])
    dma("sp", cF[:, 4:6, :].rearrange("p a b -> p (a b)"), const_d[:, 13 * 128:15 * 128], [], ["cF"])
    cp("dve", cB, ctemp, ["ctemp"], ["cB"])
    for j, val in enumerate((EPS, 1.0, math.log(128 ** -0.5), 0.0, math.log(192 ** -0.5))):
        P.op("pool", lambda e, j=j, val=val: e.memset(cols[:, j:j + 1], val), writes=["cols"])
    identF = cF[:, 0, :]; onesF = cF[:, 1, :]
    identB = cB[:, 0, :]; onesB = cB[:, 1, :]
    CF = {"F": 2, "B": 4}
    CB = {"F": 2, "B": 13}

    def rsqrt_small(out, in_, mul, reads, writes, tmp, tmpn):
        act(tmp, in_, AF.Ln, reads, [tmpn], scale=mul, bias=cols[:, 0:1])
        act(out, tmp, AF.Exp, [tmpn], writes, scale=-0.5)

    stg = [0]

    def load_w(dst, src, K, N, name, eng_cycle=("act", "dve")):
        for k in range(K // 128):
            for c0 in range(0, N, 1024):
                c1 = min(N, c0 + 1024)
                i = stg[0] % 2
                stg[0] += 1
                st = stage[i][:, 0:c1 - c0]
                dma("sp", st, src[k * 128:(k + 1) * 128, c0:c1], [], ["stage%d" % i])
                cp(eng_cycle[stg[0] % len(eng_cycle)], dst[:, k, c0:c1], st, ["stage%d" % i], [name])

    dma("sp", cvec, cvec_d, [], ["cvec"])
    dma("sp", bmod, bmod_d, [], ["bmod"])
    dma("sp", nrm[:, 0, :], nattn_d, [], ["nrm"])
    dma("sp", nrm[:, 1, :], nmlp_d, [], ["nrm"])
    act(sc, cvec, AF.Silu, ["cvec"], ["sc"])
    scv = sc.rearrange("p (v k) -> p k v", v=2)
    wst = [A.alloc([8, 512], F32) for _ in range(2)]
    for blk in range(12):
        w = wst[blk % 2]
        dma("sp", w, wmod_d[:, blk * 512:(blk + 1) * 512].rearrange("(k p) n -> p k n", p=128), [], ["wst%d" % (blk % 2)])
        b = nb()
        for t4 in range(4):
            for k in range(8):
                mm(pf(b)[:, t4 * 2:(t4 + 1) * 2], w[:, k, t4 * 128:(t4 + 1) * 128], scv[:, k, :], k == 0, k == 7,
                   ["wst%d" % (blk % 2), "sc"], b)
        e = blk // 2
        t0 = (blk % 2) * 4
        tt("dve", modF[:, e, :, t0:t0 + 4], pf(b)[:, 0:8].rearrange("p (t v) -> p v t", v=2),
           bc(bmod[:, e * 8 + t0:e * 8 + t0 + 4].unsqueeze(1), [128, 2, 4]), ALU.add, [PB(b), "bmod"], ["modF"])
    A.release(m0)
    for v in range(2):
        stt(G1[:, v, :], modF[:, 1, v, :], 1.0, nrm[:, 0, :], ALU.add, ALU.mult, ["modF", "nrm"], ["G1"])
    stt(G2, modF[:, 4, 0, :], 1.0, nrm[:, 1, :], ALU.add, ALU.mult, ["modF", "nrm"], ["G2"])
    SH1 = modF[:, 0, :, :]
    SH2 = modF[:, 3, 0, :]

    P.barrier()

    if stop == 0:
        return nc, P.emit(nc)
    def make_hT(src_rows, v, hT, tag, G=None, SH=None, xt_keep=None):
        xt = xt_keep if xt_keep is not None else A_x[tag_i[0] % 2]
        xn = A_xn
        nm = "xt%d" % (tag_i[0] % 2) if xt_keep is None else tag + "xt"
        tag_i[0] += 1
        dma("sp", xt, src_rows, [], [nm])
        act(A_junk, xt, AF.Square, [nm], ["junk", "ss1"], accum=A_ss[:, 0:1])
        rsqrt_small(A_ss[:, 1:2], A_ss[:, 0:1], 1.0 / D, ["ss1"], ["rs1"], A_ss[:, 2:3], "ss1t")
        act(xn, xt, AF.Copy, [nm, "rs1"], ["xn"], scale=A_ss[:, 1:2])
        b = nb()
        for k in range(8):
            tr(pb16(b)[:, k * 128:(k + 1) * 128], xn[:, k * 128:(k + 1) * 128], identB, ["xn", "cB"], b)
        g = G if G is not None else G1[:, v, :]
        s = SH if SH is not None else SH1[:, v, :]
        tt("dve", A_ht32, pb16(b).rearrange("p (k t) -> p k t", k=8), bc(g.unsqueeze(2), [128, 8, 128]), ALU.mult,
           [PB(b), "G1", "G2"], ["ht32"])
        tt("dve", hT, A_ht32, bc(s.unsqueeze(2), [128, 8, 128]), ALU.add, ["ht32", "modF"], [tag])

    tag_i = [0]
    A_x = [A.alloc([D], F32) for _ in range(2)]
    A_xn = A.alloc([D], BF16)
    A_junk = A.alloc([D], BF16)
    A_ss = A.alloc([8], F32)
    A_ht32 = A.alloc([8, 128], F32)

    def tile_rows(g):
        if g < NTC:
            return ctx_d[g * 128:(g + 1) * 128, :], 1
        t = g - NTC
        return x_d[t * 128:(t + 1) * 128, :], 0

    mg = A.mark()
    Wqkv = A.alloc([8, 3072], BF16)
    Wbd = A.alloc([8, 32], BF16)
    convw = A.alloc([24, 5], F32)
    diagW = A.alloc([120, 128], BF16)
    negA = A.alloc([16], F32); dtb = A.alloc([16], F32)
    load_w(Wqkv, wqkv_d, D, 3072, "Wqkv")
    load_w(Wbd, wbd_d, D, 32, "Wbd")
    dma("sp", convw.rearrange("p a b -> p (a b)"), conv_d, [], ["convw"])
    dma("sp", negA, alog_d.partition_broadcast(128), [], ["negA"])
    dma("sp", dtb, dtb_d.partition_broadcast(128), [], ["dtb"])
    act(negA, negA, AF.Exp, ["negA"], ["negA"])
    ts("dve", negA, negA, -1.0, ALU.mult, ["negA"], ["negA"])
    for ct in range(24):
        for i in range(5):
            ts("pool" if (ct + i) % 2 else "dve", diagW[:, ct * 5 + i, :], identF, convw[:, ct, i:i + 1], ALU.mult,
               ["cF", "convw"], ["diagW"])

    P.barrier()
    hT = [A.alloc([8, 128], BF16) for _ in range(2)]
    Xw = [A.alloc([24, 132], BF16) for _ in range(2)]
    Xw.append(stage_all[:, 0:24 * 132 // 2].bitcast(BF16).rearrange("p (a b) -> p a b", a=24))
    Xe = [A.alloc([24, 4], BF16) for _ in range(4)]
    YT = A.alloc([24, 128], BF16)
    SQ = A.alloc([16, 128], BF16)
    RS = A.alloc([16, 128], F32)
    QKN = A.alloc([16, 128], BF16)
    Ktok = A.alloc([8, 128], BF16); Vtok = A.alloc([8, 128], BF16)
    sca = A.alloc([12, 8], F32)
    Rm = A.alloc([8, 128], F32); Fm = A.alloc([8, 128], F32)
    FB = A.alloc([8, 128], BF16); Fq = A.alloc([8, 128], BF16)
    BF_ = A.alloc([8, 128], BF16); Bl = [A.alloc([8, 128], BF16) for _ in range(2)]
    qkT = A.alloc([8, 128], BF16)
    Dm = [A.alloc([8, 128], BF16) for _ in range(2)]; Wm = [A.alloc([8, 128], BF16) for _ in range(2)]
    ImY = A.alloc([8, 128], BF16)
    Xp = A.alloc([8, 128], BF16); VN = A.alloc([8, 128], BF16); Kd = A.alloc([8, 128], BF16)
    tmpf = Rm
    S32 = A.alloc([8, 128], F32); Sbf = A.alloc([8, 128], BF16)
    o1 = Fm; osb = [A.alloc([8, 128], F32) for _ in range(2)]

    A_lg = [A.alloc([16], F32) for _ in range(3)]

    def gdn_sweep2(dirn, order, out_tiles, o_dram, extra=None):
        c0 = {"F": 2, "B": 13}[dirn]
        Uc = cF[:, CF[dirn], :]; SUc = cF[:, CF[dirn] + 1, :]
        UIm = cB[:, c0 + 2, :]; SUm = cB[:, c0 + 3, :]
        lvl = [cB[:, c0 + 4 + l, :] for l in range(7)]
        dcol = 0 if dirn == "F" else 16
        dsc = 0 if dirn == "F" else 8
        P.op("pool", lambda e: e.memset(S32, 0.0), writes=["S32a", "S32b"])
        P.op("pool", lambda e: e.memset(Sbf, 0.0), writes=["Sbfa", "Sbfb"])
        n_proc = len(order)
        order = list(order) + ([extra] if extra is not None else [])
        n_ord = len(order)
        asc = (dirn == "F")

        def seq_of(g):
            return 0 if g < NTC else 1

        def v4(b):
            return pf(b).rearrange("p (a t) -> p a t", a=4)

        def project(n):
            g = order[n]
            rows, v = tile_rows(g)
            h = hT[n % 2]
            hn = "hT%d" % (n % 2)
            make_hT(rows, v, h, hn)
            yield
            xw = Xw[n % 3]
            xwn = "Xw%d" % (n % 3)
            for c3 in range(8):
                b = nb()
                for j in range(3):
                    ct = c3 * 3 + j
                    for k in range(8):
                        mm(pf(b)[:, j * 128:(j + 1) * 128], Wqkv[:, k, ct * 128:(ct + 1) * 128], h[:, k, :], k == 0, k == 7, ["Wqkv", hn], b)
                cp("act" if c3 % 2 else "dve", xw[:, c3 * 3:(c3 + 1) * 3, 2:130], pf(b)[:, 0:384].rearrange("p (a t) -> p a t", a=3), [PB(b)], [xwn])
                yield
            xe = Xe[n % 4]
            cp("pool", xe[:, :, 0:2], xw[:, :, 2:4], [xwn], ["Xe%d" % (n % 4)])
            cp("pool", xe[:, :, 2:4], xw[:, :, 128:130], [xwn], ["Xe%d" % (n % 4)])
            b = nb()
            for k in range(8):
                mm(pf(b)[:, 0:16], h[:, k, :], Wbd[:, k, dcol:dcol + 16], k == 0, k == 7, [hn, "Wbd"], b)
            cp("act", A_lg[n % 3], pf(b)[:, 0:16], [PB(b)], ["lg%d" % (n % 3)])

        def chunk(m, fill):
            g = order[m]
            want_out = g in out_tiles
            xw = Xw[m % 3]
            xwn = "Xw%d" % (m % 3)
            lg = A_lg[m % 3]
            lgn = "lg%d" % (m % 3)

            def nbr(mm_):
                if mm_ < 0 or mm_ >= n_ord or seq_of(order[mm_]) != seq_of(g):
                    return None
                return Xe[mm_ % 4], "Xe%d" % (mm_ % 4)
            prv = nbr(m - 1) if asc else nbr(m + 1)
            nxt = nbr(m + 1) if asc else nbr(m - 1)
            if prv is not None:
                cp("pool", xw[:, :, 0:2], prv[0][:, :, 2:4], [prv[1], xwn], [xwn])
            else:
                P.op("pool", lambda e, xw=xw: e.memset(xw[:, :, 0:2], 0.0), reads=[xwn], writes=[xwn])
            if nxt is not None:
                cp("pool", xw[:, :, 130:132], nxt[0][:, :, 0:2], [nxt[1], xwn], [xwn])
            else:
                P.op("pool", lambda e, xw=xw: e.memset(xw[:, :, 130:132], 0.0), reads=[xwn], writes=[xwn])
            for c4 in range(6):
                b = nb()
                for j in range(4):
                    ct = c4 * 4 + j
                    o_ = pf(b)[:, j * 128:(j + 1) * 128]
                    for i in range(5):
                        mm(o_, diagW[:, ct * 5 + i, :], xw[:, ct, i:i + 128], i == 0, i == 4, ["diagW", xwn], b)
                act(YT[:, c4 * 4:(c4 + 1) * 4, :], v4(b), AF.Silu, [PB(b)], ["YT"])
            fill()
            tt("pool", SQ, YT[:, 0:16, :], YT[:, 0:16, :], ALU.mult, ["YT"], ["SQ"])
            for c4 in range(4):
                b = nb()
                for j in range(4):
                    mm(pf(b)[:, j * 128:(j + 1) * 128], onesB, SQ[:, c4 * 4 + j, :], True, True, ["cB", "SQ"], b)
                act(RS[:, c4 * 4:(c4 + 1) * 4, :], v4(b), AF.Ln, [PB(b)], ["RS%d" % c4], bias=cols[:, 0:1])
            act(RS[:, 0:8, :], RS[:, 0:8, :], AF.Exp, ["RS0", "RS1"], ["RS0", "RS1"], scale=-0.5, bias=cols[:, 2:3])
            act(RS[:, 8:16, :], RS[:, 8:16, :], AF.Exp, ["RS2", "RS3"], ["RS2", "RS3"], scale=-0.5)
            tt("dve", QKN, YT[:, 0:16, :], RS, ALU.mult, ["YT", "RS0", "RS1", "RS2", "RS3"], ["QKN"])
            bk = nb()
            for h in range(8):
                tr(pb16(bk)[:, h * 128:(h + 1) * 128], QKN[:, 8 + h, :], identB, ["QKN", "cB"], bk)
            bv = nb()
            for h in range(8):
                tr(pb16(bv)[:, h * 128:(h + 1) * 128], YT[:, 16 + h, :], identB, ["YT", "cB"], bv)
            cp("act", Ktok, pb16(bk).rearrange("p (a t) -> p a t", a=8), [PB(bk)], ["Ktok"])
            cp("dve", Vtok, pb16(bv).rearrange("p (a t) -> p a t", a=8), [PB(bv)], ["Vtok"])
            fill()
            beta = sca[:, 0, :]; xx = sca[:, 1, :]; ax = sca[:, 2, :]; g_ = sca[:, 3, :]
            egc = sca[:, 4, :]; negegc = sca[:, 5, :]; gcs = sca[:, 6, :]; edl = sca[:, 7, :]; etot = sca[:, 8, :]
            act(beta, lg[:, 0:8], AF.Exp, [lgn], ["beta"], scale=-1.0)
            ts("dve", beta, beta, 1.0, ALU.add, ["beta"], ["beta"])
            P.op("dve", lambda e: e.reciprocal(out=beta, in_=beta), reads=["beta"], writes=["beta"])
            tt("dve", xx, lg[:, 8:16], dtb[:, dsc:dsc + 8], ALU.add, [lgn, "dtb"], ["xx"])
            ts("dve", ax, xx, -1.0, ALU.mult, ["xx"], ["ax"])
            tt("dve", ax, ax, xx, ALU.min, ["xx", "ax"], ["ax"])
            act(ax, ax, AF.Exp, ["ax"], ["ax"])
            act(ax, ax, AF.Ln, ["ax"], ["ax"], bias=cols[:, 1:2])
            ts("dve", xx, xx, 0.0, ALU.max, ["xx"], ["xx"])
            tt("dve", xx, xx, ax, ALU.add, ["xx", "ax"], ["xx"])
            tt("dve", g_, xx, negA[:, dsc:dsc + 8], ALU.mult, ["xx", "negA"], ["g"])
            b = nb()
            mm(pf(b)[:, 0:8], Uc, g_, True, True, ["cF", "g"], b)
            mm(pf(b)[:, 8:16], onesF, g_, True, True, ["cF", "g"], b)
            act(egc, pf(b)[:, 0:8], AF.Exp, [PB(b)], ["egc"])
            act(gcs, pf(b)[:, 0:8], AF.Copy, [PB(b)], ["gcs"])
            act(etot, pf(b)[:, 8:16], AF.Exp, [PB(b)], ["etot"])
            ts("dve", negegc, egc, -1.0, ALU.mult, ["egc"], ["negegc"])
            tt("dve", edl, pf(b)[:, 8:16], gcs, ALU.subtract, [PB(b), "gcs", "etot", "egc"], ["edl"])
            act(edl, edl, AF.Exp, ["edl"], ["edl"])
            fill()
            tt("pool", Rm, bc(Uc.unsqueeze(1), [128, 8, 128]), bc(g_.unsqueeze(2), [128, 8, 128]), ALU.mult, ["cF", "g"], ["Rm"])
            for hh in range(2):
                b = nb()
                for j in range(4):
                    mm(pf(b)[:, j * 128:(j + 1) * 128], SUc, Rm[:, hh * 4 + j, :], True, True, ["cF", "Rm"], b)
                act(Fm[:, hh * 4:(hh + 1) * 4, :], v4(b), AF.Exp, [PB(b)], ["Fm%d" % hh])
            tt("pool", FB, Fm, bc(SUm.unsqueeze(1), [128, 8, 128]), ALU.mult, ["Fm0", "Fm1", "cB"], ["FB"])
            if want_out:
                tt("pool", Fq, Fm, bc(UIm.unsqueeze(1), [128, 8, 128]), ALU.mult, ["Fm0", "Fm1", "cB"], ["Fq"])
            for hh in range(2):
                sl = slice(hh * 4, (hh + 1) * 4)
                b = nb()
                for j in range(4):
                    h = hh * 4 + j
                    mm(pf(b)[:, j * 128:(j + 1) * 128], QKN[:, 8 + h, :], QKN[:, 8 + h, :], True, True, ["QKN"], b)
                tt("dve", BF_[:, sl, :], v4(b), FB[:, sl, :], ALU.mult, [PB(b), "FB"], ["BF%d" % hh])
                if want_out:
                    b2 = nb()
                    for j in range(4):
                        h = hh * 4 + j
                        mm(pf(b2)[:, j * 128:(j + 1) * 128], QKN[:, 8 + h, :], QKN[:, h, :], True, True, ["QKN"], b2)
                    tt("dve", qkT[:, sl, :], v4(b2), Fq[:, sl, :], ALU.mult, [PB(b2), "Fq"], ["qkT%d" % hh])
            tt("pool", Dm[0], bc(identB.unsqueeze(1), [128, 8, 128]), bc(beta.unsqueeze(2), [128, 8, 128]), ALU.mult, ["cB", "beta"], ["D0a", "D0b"])
            hs = ("a", "b")
            for l in range(7):
                fill()
                cur, nx = l % 2, (l + 1) % 2
                Wcur = Dm[0] if l == 0 else Wm[cur]
                wn_ = (lambda hh_: "D0" + hs[hh_]) if l == 0 else (lambda hh_, cur=cur: "W%d%s" % (cur, hs[hh_]))
                Bc = Bl[l % 2]
                bn = "Bl%d" % (l % 2)
                tt("pool", Bc, BF_, bc(lvl[l].unsqueeze(1), [128, 8, 128]), ALU.mult, ["BF0", "BF1", "cB"], [bn])
                for hh in range(2):
                    sl = slice(hh * 4, (hh + 1) * 4)
                    b = nb()
                    for j in range(4):
                        h = hh * 4 + j
                        mm(pf(b)[:, j * 128:(j + 1) * 128], Bc[:, h, :], Dm[cur][:, h, :], True, True, [bn, "D%d%s" % (cur, hs[hh])], b)
                    tt("dve", ImY[:, sl, :], bc(identB.unsqueeze(1), [128, 4, 128]), v4(b), ALU.subtract, [PB(b), "cB"], ["ImY%d" % hh])
                for hh in range(2):
                    sl = slice(hh * 4, (hh + 1) * 4)
                    if l < 6:
                        b = nb()
                        for j in range(4):
                            h = hh * 4 + j
                            mm(pf(b)[:, j * 128:(j + 1) * 128], Wcur[:, h, :], ImY[:, h, :], True, True, [wn_(hh), "ImY%d" % hh], b)
                        cp("act", Dm[nx][:, sl, :], v4(b), [PB(b)], ["D%d%s" % (nx, hs[hh])])
                    b = nb()
                    for j in range(4):
                        h = hh * 4 + j
                        mm(pf(b)[:, j * 128:(j + 1) * 128], ImY[:, h, :], Wcur[:, h, :], True, True, [wn_(hh), "ImY%d" % hh], b)
                    cp("act" if hh else "dve", Wm[nx][:, sl, :], v4(b), [PB(b)], ["W%d%s" % (nx, hs[hh])])
            WT = Wm[1]
            tt("pool", Kd, Ktok, bc(edl.unsqueeze(2), [128, 8, 128]), ALU.mult, ["Ktok", "edl"], ["Kd"])
            for hh in range(2):
                sl = slice(hh * 4, (hh + 1) * 4)
                b = nb()
                for j in range(4):
                    h = hh * 4 + j
                    mm(pf(b)[:, j * 128:(j + 1) * 128], QKN[:, 8 + h, :], Sbf[:, h, :], True, True, ["QKN", "Sbf" + hs[hh]], b)
                tt("dve", tmpf[:, sl, :], v4(b), bc(negegc[:, sl].unsqueeze(2), [128, 4, 128]), ALU.mult, [PB(b), "negegc"], ["Rm"])
                tt("dve", Xp[:, sl, :], tmpf[:, sl, :], Vtok[:, sl, :], ALU.add, ["Rm", "Vtok"], ["Xp%d" % hh])
            for hh in range(2):
                sl = slice(hh * 4, (hh + 1) * 4)
                b = nb()
                for j in range(4):
                    h = hh * 4 + j
                    mm(pf(b)[:, j * 128:(j + 1) * 128], WT[:, h, :], Xp[:, h, :], True, True, ["W1%s" % hs[hh], "Xp%d" % hh], b)
                cp("act", VN[:, sl, :], v4(b), [PB(b)], ["VN%d" % hh])
            if want_out:
                ot = out_tiles[g]
                ob_ = osb[ot % 2]
                obn = "osb%d" % (ot % 2)
                for hh in range(2):
                    sl = slice(hh * 4, (hh + 1) * 4)
                    b = nb()
                    for j in range(4):
                        h = hh * 4 + j
                        mm(pf(b)[:, j * 128:(j + 1) * 128], QKN[:, h, :], Sbf[:, h, :], True, True, ["QKN", "Sbf" + hs[hh]], b)
                    tt("dve", o1[:, sl, :], v4(b), bc(egc[:, sl].unsqueeze(2), [128, 4, 128]), ALU.mult, [PB(b), "egc"], ["Fm%d" % hh])
                    b = nb()
                    for j in range(4):
                        h = hh * 4 + j
                        mm(pf(b)[:, j * 128:(j + 1) * 128], qkT[:, h, :], VN[:, h, :], True, True, ["qkT%d" % hh, "VN%d" % hh], b)
                    tt("dve", ob_[:, sl, :], v4(b), o1[:, sl, :], ALU.add, [PB(b), "Fm%d" % hh], [obn + hs[hh]])
                dma("pool", o_dram[ot * 128:(ot + 1) * 128, :], ob_.rearrange("p a t -> p (a t)"), [obn + "a", obn + "b"], [obn + "a", obn + "b"])
            for hh in range(2):
                sl = slice(hh * 4, (hh + 1) * 4)
                b = nb()
                for j in range(4):
                    h = hh * 4 + j
                    mm(pf(b)[:, j * 128:(j + 1) * 128], Kd[:, h, :], VN[:, h, :], True, True, ["Kd", "VN%d" % hh], b)
                for j in range(4):
                    h = hh * 4 + j
                    stt(S32[:, h, :], S32[:, h, :], etot[:, h:h + 1], pf(b)[:, j * 128:(j + 1) * 128], ALU.mult, ALU.add,
                        [PB(b), "S32" + hs[hh], "etot"], ["S32" + hs[hh]])
                cp("act", Sbf[:, sl, :], S32[:, sl, :], ["S32" + hs[hh]], ["Sbf" + hs[hh]])

        def drain(gen):
            for _ in gen:
                pass

        drain(project(0))
        if n_ord > 1:
            drain(project(1))
        for m in range(n_proc):
            gen = project(m + 2) if m + 2 < n_ord else iter(())

            def fill(gen=gen):
                next(gen, None)
            chunk(m, fill)
            drain(gen)

    own = {NTC + t: t for t in range(NTO)}
    orderF = list(range(NTC)) + [NTC + t for t in range(NTO)]
    orderB = list(range(NTC - 1, -1, -1)) + [NTC + t for t in range(NTL - 1, -1, -1)]
    P.barrier()
    gdn_sweep2("F", orderF, own, of_d, extra=NTC + NTO)
    if stop == 1:
        return nc, P.emit(nc)
    gdn_sweep2("B", orderB, own, ob_d)
    P.barrier()
    A.release(mg)

    if stop == 2:
        return nc, P.emit(nc)
    mm_ = A.mark()
    Wuk = A.alloc([2, 1024], BF16); Wuv = A.alloc([2, 1024], BF16)
    ckvnT = A.alloc([2, NT], BF16); krT = A.alloc([NT], BF16)
    rstdk = A.alloc([NG, 8], F32)
    m2b = A.mark()
    Wmla = A.alloc([8, 704], BF16); Wuq = A.alloc([3, 1536], BF16)
    load_w(Wmla, wmla_d, D, 704, "Wmla"); load_w(Wuq, wuq_d, 384, 1536, "Wuq")
    load_w(Wuk, wuk_d, 256, 1024, "Wuk"); load_w(Wuv, wuv_d, 256, 1024, "Wuv")
    qan_b = A.alloc([384], F32); kvan_b = A.alloc([256], F32); gq_b = A.alloc([8, 192], F32); gk_b = A.alloc([192], F32)
    dma("sp", qan_b, qan_d.partition_broadcast(128), [], ["qan"])
    dma("sp", kvan_b, kvan_d.partition_broadcast(128), [], ["kvan"])
    dma("sp", gq_b[:, 0, :], qn_d.partition_broadcast(128), [], ["gq"])
    dma("sp", gk_b, kn_d.partition_broadcast(128), [], ["gk"])
    tt("dve", gq_b[:, 0, 0:128], gq_b[:, 0, 0:128], gk_b[:, 0:128], ALU.mult, ["gq", "gk"], ["gq"])
    for h in range(1, 8):
        cp("pool", gq_b[:, h, :], gq_b[:, 0, :], ["gq"], ["gq"])
    cqnT = A.alloc([3, NO], BF16)
    hT2 = A.alloc([8, 128], BF16)
    lat = A.alloc([704], F32); latn = A.alloc([640], BF16)
    sqa = A.alloc([256], F32); sqk = A.alloc([1024], F32); sqq = A.alloc([1536], F32); sqc = A.alloc([384], F32)
    sqr = A.alloc([64], F32)
    ssm = A.alloc([48], F32)
    cs = A.alloc([2, 64], F32)
    kr32 = A.alloc([64], F32); krt = A.alloc([64], F32); krtmp = A.alloc([64], F32); krb = A.alloc([128], BF16)
    q32 = A.alloc([8, 192], F32); qrt = A.alloc([8, 64], F32); qtmp = A.alloc([8, 64], F32); qbf = A.alloc([8, 192], BF16)
    qTs = A.alloc([8, 2, 128], BF16)

    def rope(dst, src, tmp, cosv, sinv, nh, rd, nm):
        s5 = src.rearrange("p h (a c f) -> p h a c f", a=2, c=2)
        t5 = tmp.rearrange("p h (a c f) -> p h a c f", a=2, c=2)
        sn5 = sinv.rearrange("p (a c f) -> p a c f", a=2, c=2)
        for c in range(2):
            for a_ in range(2):
                tt("dve", t5[:, :, a_, c, :], s5[:, :, a_, 1 - c, :], bc(sn5[:, a_, c, :].unsqueeze(1), [128, nh, 16]), ALU.mult,
                   rd, [nm + "tmp"])
        tt("dve", dst, src, bc(cosv.unsqueeze(1), [128, nh, 64]), ALU.mult, rd, [nm])
        tt("dve", dst, dst, tmp, ALU.add, [nm, nm + "tmp"], [nm])

    lat2 = [lat, A.alloc([704], F32)]
    cs2 = [cs, A.alloc([2, 64], F32)]

    def p2b_front(g):
        rows, v = tile_rows(g)
        is_own = g in own
        lat = lat2[g % 2]; latn_ = "lat%d" % (g % 2)
        cs = cs2[g % 2]; csn = "cs%d" % (g % 2)
        make_hT(rows, v, hT2, "hT2")
        ncol = 704 if is_own else 320
        c_lo = 0 if is_own else 384
        b0 = nb()
        w0 = min(512, ncol)
        for k in range(8):
            mm(pf(b0)[:, 0:w0], hT2[:, k, :], Wmla[:, k, c_lo:c_lo + w0], k == 0, k == 7, ["hT2", "Wmla"], b0)
        cp("dve", lat[:, c_lo:c_lo + w0], pf(b0)[:, 0:w0], [PB(b0)], [latn_])
        if ncol > 512:
            b1 = nb()
            for k in range(8):
                mm(pf(b1)[:, 0:ncol - 512], hT2[:, k, :], Wmla[:, k, 512:ncol], k == 0, k == 7, ["hT2", "Wmla"], b1)
            cp("act", lat[:, 512:ncol], pf(b1)[:, 0:ncol - 512], [PB(b1)], [latn_])
        dma("sp", cs[:, 0, :], cos_d[g * 128:(g + 1) * 128, :], [], [csn])
        dma("sp", cs[:, 1, :], sin_d[g * 128:(g + 1) * 128, :], [], [csn])

    p2b_front(0)
    for g in range(NG):
        if g + 1 < NG:
            p2b_front(g + 1)
        is_own = g in own
        lat = lat2[g % 2]; cs = cs2[g % 2]
        LAT = "lat%d" % (g % 2); CS = "cs%d" % (g % 2)
        act(sqa, lat[:, 384:640], AF.Square, [LAT], ["sqa", "ss0"], accum=ssm[:, 0:1])
        rsqrt_small(ssm[:, 1:2], ssm[:, 0:1], 1.0 / 256, ["ss0"], ["ss1k"], ssm[:, 2:3], "ss2k")
        stt(latn[:, 384:640], lat[:, 384:640], ssm[:, 1:2], kvan_b, ALU.mult, ALU.mult, [LAT, "ss1k", "kvan"], ["latn_kv"])
        b = nb()
        for j in range(2):
            tr(pb16(b)[:, j * 128:(j + 1) * 128], latn[:, 384 + j * 128:384 + (j + 1) * 128], identB, ["latn_kv", "cB"], b)
        cp("act", ckvnT[:, :, g * 128:(g + 1) * 128], pb16(b)[:, 0:256].rearrange("p (a t) -> p a t", a=2), [PB(b)], ["ckvnT"])
        if sub == 2:
            return nc, P.emit(nc)
        for half in range(2):
            b = nb()
            for kk in range(2):
                mm(pf(b), ckvnT[:, kk, g * 128:(g + 1) * 128], Wuk[:, kk, half * 512:(half + 1) * 512], kk == 0, kk == 1, ["ckvnT", "Wuk"], b)
            act(sqk[:, half * 512:(half + 1) * 512], pf(b), AF.Square, [PB(b)], ["sqk"])
        red(ssm[:, 8:16], sqk.rearrange("p (h d) -> p h d", h=8), ["sqk"], ["ss8"])
        act(sqr, lat[:, 640:704], AF.Square, [LAT], ["sqr", "ss3"], accum=ssm[:, 3:4])
        ts("dve", ssm[:, 8:16], ssm[:, 8:16], ssm[:, 3:4], ALU.add, ["ss8", "ss3"], ["ss8"])
        act(ssm[:, 16:24], ssm[:, 8:16], AF.Ln, ["ss8"], ["ss16"], scale=1.0 / 192, bias=cols[:, 0:1])
        act(rstdk[:, g, :], ssm[:, 16:24], AF.Exp, ["ss16"], ["rstdk"], scale=-0.5, bias=cols[:, 4:5])
        if sub == 3:
            return nc, P.emit(nc)
        tt("dve", kr32, lat[:, 640:704], gk_b[:, 128:192], ALU.mult, [LAT, "gk"], ["kr32"])
        rope(krt.unsqueeze(1), kr32.unsqueeze(1), krtmp.unsqueeze(1), cs[:, 0, :], cs[:, 1, :], 1, ["kr32", CS], "krt")
        cp("dve", krb[:, 0:64], krt, ["krt"], ["krb"])
        cp("dve", krb[:, 64:128], krt, ["krt"], ["krb"])
        b = nb()
        tr(pb16(b)[:, 0:128], krb, identB, ["krb", "cB"], b)
        cp("act", krT[:, g * 128:(g + 1) * 128], pb16(b)[:, 0:128], [PB(b)], ["krT"])
        if sub == 4:
            return nc, P.emit(nc)
        if is_own:
            ot = own[g]
            act(sqc, lat[:, 0:384], AF.Square, [LAT], ["sqc", "ss4"], accum=ssm[:, 4:5])
            rsqrt_small(ssm[:, 5:6], ssm[:, 4:5], 1.0 / 384, ["ss4"], ["ss5"], ssm[:, 6:7], "ss6")
            stt(latn[:, 0:384], lat[:, 0:384], ssm[:, 5:6], qan_b, ALU.mult, ALU.mult, [LAT, "ss5", "qan"], ["latn_q"])
            b = nb()
            for j in range(3):
                tr(pb16(b)[:, j * 128:(j + 1) * 128], latn[:, j * 128:(j + 1) * 128], identB, ["latn_q", "cB"], b)
            cp("act", cqnT[:, :, ot * 128:(ot + 1) * 128], pb16(b)[:, 0:384].rearrange("p (a t) -> p a t", a=3), [PB(b)], ["cqnT"])
            if sub == 5:
                return nc, P.emit(nc)
            q2 = q32.rearrange("p h d -> p (h d)")
            for j in range(3):
                b = nb()
                for kk in range(3):
                    mm(pf(b), cqnT[:, kk, ot * 128:(ot + 1) * 128], Wuq[:, kk, j * 512:(j + 1) * 512], kk == 0, kk == 2, ["cqnT", "Wuq"], b)
                cp("dve", q2[:, j * 512:(j + 1) * 512], pf(b), [PB(b)], ["q32"])
                act(sqq[:, j * 512:(j + 1) * 512], q2[:, j * 512:(j + 1) * 512], AF.Square, ["q32"], ["sqq"])
            red(ssm[:, 24:32], sqq.rearrange("p (h d) -> p h d", h=8), ["sqq"], ["ss24"])
            act(ssm[:, 32:40], ssm[:, 24:32], AF.Ln, ["ss24"], ["ss32"], scale=1.0 / 192, bias=cols[:, 0:1])
            act(ssm[:, 32:40], ssm[:, 32:40], AF.Exp, ["ss32"], ["ss32"], scale=-0.5)
            tt("dve", q32, q32, bc(ssm[:, 32:40].unsqueeze(2), [128, 8, 192]), ALU.mult, ["q32", "ss32"], ["q32"])
            tt("dve", q32, q32, gq_b, ALU.mult, ["q32", "gq"], ["q32"])
            rope(qrt, q32[:, :, 128:192], qtmp, cs[:, 0, :], cs[:, 1, :], 8, ["q32", CS], "qrt")
            cp("dve", qbf[:, :, 0:128], q32[:, :, 0:128], ["q32"], ["qbfn"])
            cp("pool", qbf[:, :, 128:192], qrt, ["qrt"], ["qbfr"])
            if sub == 6:
                return nc, P.emit(nc)
            for hh in range(2):
                bn_ = nb()
                for j in range(4):
                    h = hh * 4 + j
                    tr(pb16(bn_)[:, j * 128:(j + 1) * 128], qbf[:, h, 0:128], identB, ["qbfn", "cB"], bn_)
                cp("act", qTs[:, hh * 4:(hh + 1) * 4, 0, :], pb16(bn_)[:, 0:512].rearrange("p (a t) -> p a t", a=4), [PB(bn_)], ["qTs%d" % hh])
                br_ = nb()
                for j in range(4):
                    h = hh * 4 + j
                    tr(pb16(br_)[0:64, j * 128:(j + 1) * 128], qbf[:, h, 128:192], identB, ["qbfr", "cB"], br_)
                cp("dve", qTs[0:64, hh * 4:(hh + 1) * 4, 1, :], pb16(br_)[0:64, 0:512].rearrange("p (a t) -> p a t", a=4), [PB(br_)], ["qTr%d" % hh])
            if sub == 7:
                return nc, P.emit(nc)
            dma("pool", qt_d[:, 0:128, ot * 128:(ot + 1) * 128].rearrange("h d t -> d h t"), qTs[:, :, 0, :], ["qTs0", "qTs1"], ["qTs0", "qTs1", "qt_d"])
            dma("pool", qt_d[:, 128:192, ot * 128:(ot + 1) * 128].rearrange("h d t -> d h t"), qTs[0:64, :, 1, :], ["qTr0", "qTr1"], ["qTr0", "qTr1", "qt_d"])

    if stop == 3:
        return nc, P.emit(nc)
    P.barrier()
    A.release(m2b)
    m3 = A.mark()
    KnT = A.alloc([NT], BF16); Vh = A.alloc([NG, 128], BF16)
    QTn = A.alloc([NO], BF16); QTr = A.alloc([NO], BF16)
    PT = [A.alloc([512], BF16) for _ in range(4)]
    rec = A.alloc([512], F32); acc = A.alloc([512], F32)
    oT = [A.alloc([512], BF16) for _ in range(2)]
    NQB = NO // 512
    NKB = (NT + 511) // 512
    bank_state.update(n=0, lo=0, hi=4)
    for h in range(8):
        dma("sp", QTn, qt_d[h, 0:128, :], ["qt_d"], ["QTn"])
        dma("sp", QTr[0:64, :], qt_d[h, 128:192, :], ["qt_d"], ["QTr"])
        dma("sp", QTr[64:128, :], qt_d[h, 128:192, :], ["qt_d"], ["QTr"])
        for kb in range(NKB):
            c0, c1 = kb * 512, min(NT, (kb + 1) * 512)
            b = nb()
            for kk in range(2):
                mm(pf(b)[:, 0:c1 - c0], Wuk[:, kk, h * 128:(h + 1) * 128], ckvnT[:, kk, c0:c1], kk == 0, kk == 1, ["Wuk", "ckvnT"], b)
            cp("dve" if kb % 2 else "act", KnT[:, c0:c1], pf(b)[:, 0:c1 - c0], [PB(b)], ["KnT"])
        for g4 in range(0, NG, 4):
            ng = min(4, NG - g4)
            b = nb()
            for j in range(ng):
                for kk in range(2):
                    mm(pf(b)[:, j * 128:(j + 1) * 128], ckvnT[:, kk, (g4 + j) * 128:(g4 + j + 1) * 128], Wuv[:, kk, h * 128:(h + 1) * 128],
                       kk == 0, kk == 1, ["ckvnT", "Wuv"], b)
            cp("dve", Vh[:, g4:g4 + ng, :], pf(b)[:, 0:ng * 128].rearrange("p (a t) -> p a t", a=ng), [PB(b)], ["Vh"])
        for qb in range(NQB):
            qs = slice(qb * 512, (qb + 1) * 512)
            ba, bs_ = (4, 5) if (h * NQB + qb) % 2 == 0 else (6, 7)

            def st2(gp):
                g0, g1 = 2 * gp, 2 * gp + 1
                b0 = nb(); b1 = nb()
                mm(pf(b0), KnT[:, g0 * 128:(g0 + 1) * 128], QTn[:, qs], True, False, ["KnT", "QTn"], b0)
                mm(pf(b1), KnT[:, g1 * 128:(g1 + 1) * 128], QTn[:, qs], True, False, ["KnT", "QTn"], b1)
                mm(pf(b0), krT[0:64, g0 * 128:(g0 + 1) * 128], QTr[0:64, qs], False, True, ["krT", "QTr"], b0)
                mm(pf(b1), krT[64:128, g1 * 128:(g1 + 1) * 128], QTr[64:128, qs], False, True, ["krT", "QTr"], b1)
                return (b0, b1)
            assert NG % 2 == 0
            bc_ = st2(0)
            for gp in range(NG // 2):
                bn_ = st2(gp + 1) if gp + 1 < NG // 2 else None
                for u in range(2):
                    g = 2 * gp + u
                    pt = PT[g % 4]
                    ptn = "PT%d" % (g % 4)
                    act(pt, pf(bc_[u]), AF.Exp, [PB(bc_[u]), "rstdk"], [ptn], scale=rstdk[:, g, h:h + 1])
                    mm(pf(ba), Vh[:, g, :], pt, g == 0, g == NG - 1, ["Vh", ptn], ba)
                    if g == 0:
                        cp("dve", acc, pt, [ptn], ["acc"])
                    else:
                        tt("dve", acc, acc, pt, ALU.add, ["acc", ptn], ["acc"])
                bc_ = bn_
            mm(pf(bs_), onesF, acc, True, True, ["cF", "acc"], bs_)
            P.op("dve", lambda e, bs_=bs_: e.reciprocal(out=rec, in_=pf(bs_)), reads=[PB(bs_)], writes=["rec"])
            o_ = oT[qb % 2]
            on = "oT%d" % (qb % 2)
            tt("dve", o_, pf(ba), rec, ALU.mult, [PB(ba), "rec"], [on])
            dma("pool", om_d[h, :, qs], o_, [on], [on, "om_d"])
    bank_state.update(n=0, lo=0, hi=8)
    P.barrier()
    A.release(m3)
    A.release(mm_)

    if stop == 4:
        return nc, P.emit(nc)
    m4 = A.mark()
    Wzg = A.alloc([8, 3072], BF16); Wg = A.alloc([8, D], BF16); Wmm = A.alloc([8, D], BF16); Wo = A.alloc([8, D], BF16)
    load_w(Wzg, wzg_d, D, 3072, "Wzg"); load_w(Wg, wg_d, D, D, "Wg"); load_w(Wmm, wm_d, D, D, "Wmm"); load_w(Wo, wo_d, D, D, "Wo")
    modb = A.alloc([2, D], F32)
    dgm = A.alloc([128], F32)
    for vi, e_ in enumerate((2, 5)):
        for t in range(8):
            ts("dve", dgm, identF, modF[:, e_, 0, t:t + 1], ALU.mult, ["cF", "modF"], ["dgm"])
            b = nb()
            mm(pf(b)[:, 0:128], onesF, dgm, True, True, ["cF", "dgm"], b)
            cp("act", modb[:, vi, t * 128:(t + 1) * 128], pf(b)[:, 0:128], [PB(b)], ["modb"])
    gon_b = A.alloc([128], F32)
    dma("sp", gon_b, gon_d.partition_broadcast(128), [], ["gon"])
    hT4s = [A.alloc([8, 128], BF16) for _ in range(2)]
    xks = [A.alloc([D], F32) for _ in range(2)]
    zs = A.alloc([D], F32); sg = A.alloc([2 * D], F32)
    ofb = A.alloc([D], F32); obb = A.alloc([D], F32)
    ssg = A.alloc([24], F32)
    yg = A.alloc([D], BF16); ygT = A.alloc([8, 128], BF16); omT = A.alloc([8, 128], BF16)
    t32 = A.alloc([D], F32); t32b = A.alloc([D], F32); ybf = A.alloc([D], BF16); yT = A.alloc([8, 128], BF16)
    x1t = [A.alloc([D], F32) for _ in range(2)]
    def p4a_front(ot):
        make_hT(x_d[ot * 128:(ot + 1) * 128, :], 0, hT4s[ot % 2], "hT4_%d" % (ot % 2), xt_keep=xks[ot % 2])

    p4a_front(0)
    for ot in range(NTO):
        if ot + 1 < NTO:
            p4a_front(ot + 1)
        hT4 = hT4s[ot % 2]; xk = xks[ot % 2]
        HT4 = "hT4_%d" % (ot % 2)
        dma("sp", ofb, of_d[ot * 128:(ot + 1) * 128, :], [], ["ofb"])
        dma("sp", obb, ob_d[ot * 128:(ot + 1) * 128, :], [], ["obb"])
        dma("sp", omT, om_d[:, :, ot * 128:(ot + 1) * 128].rearrange("h d t -> d h t"), [], ["omT"])
        for j in range(6):
            b = nb()
            for k in range(8):
                mm(pf(b), hT4[:, k, :], Wzg[:, k, j * 512:(j + 1) * 512], k == 0, k == 7, [HT4, "Wzg"], b)
            if j < 2:
                act(zs[:, j * 512:(j + 1) * 512], pf(b), AF.Silu, [PB(b)], ["zs"])
            else:
                act(sg[:, (j - 2) * 512:(j - 1) * 512], pf(b), AF.Sigmoid, [PB(b)], ["sg"])
        tt("pool", ofb, ofb, obb, ALU.add, ["ofb", "obb"], ["ofb"])
        tt("pool", t32, ofb, ofb, ALU.mult, ["ofb"], ["t32"])
        red(ssg[:, 0:8], t32.rearrange("p (h d) -> p h d", h=8), ["t32"], ["ssg0"])
        act(ssg[:, 8:16], ssg[:, 0:8], AF.Ln, ["ssg0"], ["ssg8"], scale=1.0 / 128, bias=cols[:, 0:1])
        act(ssg[:, 8:16], ssg[:, 8:16], AF.Exp, ["ssg8"], ["ssg8"], scale=-0.5)
        o3 = ofb.rearrange("p (h d) -> p h d", h=8)
        tt("dve", o3, o3, bc(ssg[:, 8:16].unsqueeze(2), [128, 8, 128]), ALU.mult, ["ofb", "ssg8"], ["ofb"])
        tt("dve", o3, o3, bc(gon_b.unsqueeze(1), [128, 8, 128]), ALU.mult, ["ofb", "gon"], ["ofb"])
        tt("dve", yg, ofb, zs, ALU.mult, ["ofb", "zs"], ["yg"])
        b = nb()
        for k in range(8):
            tr(pb16(b)[:, k * 128:(k + 1) * 128], yg[:, k * 128:(k + 1) * 128], identB, ["yg", "cB"], b)
        cp("act", ygT, pb16(b).rearrange("p (a t) -> p a t", a=8), [PB(b)], ["ygT"])
        for half in range(2):
            hs_ = slice(half * 512, (half + 1) * 512)
            b1 = nb()
            for k in range(8):
                mm(pf(b1), ygT[:, k, :], Wg[:, k, hs_], k == 0, k == 7, ["ygT", "Wg"], b1)
            b2 = nb()
            for k in range(8):
                mm(pf(b2), omT[:, k, :], Wmm[:, k, hs_], k == 0, k == 7, ["omT", "Wmm"], b2)
            tt("dve", t32[:, hs_], pf(b1), sg[:, hs_], ALU.mult, [PB(b1), "sg"], ["t32"])
            tt("dve", t32b[:, hs_], pf(b2), sg[:, D + half * 512:D + (half + 1) * 512], ALU.mult, [PB(b2), "sg"], ["t32b"])
            tt("pool", ybf[:, hs_], t32[:, hs_], t32b[:, hs_], ALU.add, ["t32", "t32b"], ["ybf"])
        b = nb()
        for k in range(8):
            tr(pb16(b)[:, k * 128:(k + 1) * 128], ybf[:, k * 128:(k + 1) * 128], identB, ["ybf", "cB"], b)
        cp("act", yT, pb16(b).rearrange("p (a t) -> p a t", a=8), [PB(b)], ["yT"])
        xo = x1t[ot % 2]
        xon = "x1t%d" % (ot % 2)
        for half in range(2):
            hs_ = slice(half * 512, (half + 1) * 512)
            b = nb()
            for k in range(8):
                mm(pf(b), yT[:, k, :], Wo[:, k, hs_], k == 0, k == 7, ["yT", "Wo"], b)
            tt("dve", t32[:, hs_], pf(b), modb[:, 0, hs_], ALU.mult, [PB(b), "modb"], ["t32"])
            tt("pool", xo[:, hs_], t32[:, hs_], xk[:, hs_], ALU.add, ["t32", HT4 + "xt"], [xon])
        dma("pool", x1_d[ot * 128:(ot + 1) * 128, :], xo, [xon], [xon, "x1_d"])
    P.barrier()
    A.release(m4)

    if stop == 5:
        return nc, P.emit(nc)
    W1 = A.alloc([8, 4 * D], BF16); W2 = A.alloc([32, D], BF16)
    load_w(W1, w1_d, D, 4 * D, "W1"); load_w(W2, w2_d, 4 * D, D, "W2")
    modb2 = A.alloc([D], F32)
    dgm2 = A.alloc([128], F32)
    for t in range(8):
        ts("dve", dgm2, identF, modF[:, 5, 0, t:t + 1], ALU.mult, ["cF", "modF"], ["dgm2"])
        b = nb()
        mm(pf(b)[:, 0:128], onesF, dgm2, True, True, ["cF", "dgm2"], b)
        cp("act", modb2[:, t * 128:(t + 1) * 128], pf(b)[:, 0:128], [PB(b)], ["modb2"])
    hT5s = [A.alloc([8, 128], BF16) for _ in range(2)]
    xk2 = [A.alloc([D], F32) for _ in range(2)]
    rl = A.alloc([4, 128], BF16); aT = A.alloc([32, 128], BF16)
    t5 = A.alloc([D], F32); outt = [A.alloc([D], F32) for _ in range(2)]
    def p4b_front(ot):
        make_hT(x1_d[ot * 128:(ot + 1) * 128, :], 0, hT5s[ot % 2], "hT5x%d" % (ot % 2), G=G2, SH=SH2, xt_keep=xk2[ot % 2])

    p4b_front(0)
    for ot in range(NTO):
        if ot + 1 < NTO:
            p4b_front(ot + 1)
        xkk = xk2[ot % 2]
        hT5 = hT5s[ot % 2]
        for f4 in range(8):
            b = nb()
            for j in range(4):
                f = f4 * 4 + j
                for k in range(8):
                    mm(pf(b)[:, j * 128:(j + 1) * 128], W1[:, k, f * 128:(f + 1) * 128], hT5[:, k, :], k == 0, k == 7, ["W1", "hT5x%d" % (ot % 2)], b)
            act(rl, pf(b).rearrange("p (a t) -> p a t", a=4), AF.Relu, [PB(b)], ["rl"])
            tt("pool", aT[:, f4 * 4:(f4 + 1) * 4, :], rl, rl, ALU.mult, ["rl"], ["aT"])
        oo = outt[ot % 2]
        for half in range(2):
            hs_ = slice(half * 512, (half + 1) * 512)
            b = nb()
            for f in range(32):
                mm(pf(b), aT[:, f, :], W2[:, f, hs_], f == 0, f == 31, ["aT", "W2"], b)
            tt("dve", t5[:, hs_], pf(b), modb2[:, hs_], ALU.mult, [PB(b), "modb2"], ["t5%d" % half])
            tt("pool", oo[:, hs_], t5[:, hs_], xkk[:, hs_], ALU.add, ["t5%d" % half, "hT5x%dxt" % (ot % 2)], ["oo%d%d" % (ot % 2, half)])
        dma("pool", out_d[ot * 128:(ot + 1) * 128, :], oo, ["oo%d0" % (ot % 2), "oo%d1" % (ot % 2)], ["oo%d0" % (ot % 2), "oo%d1" % (ot % 2)])
    nops = P.emit(nc)
    return nc, nops


def _consts():
    i = np.arange(128)
    P_, Q_ = np.meshgrid(i, i, indexing="ij")
    c = np.zeros((NCONST, 128, 128), np.float32)
    c[0] = np.eye(128); c[1] = 1.0
    c[2] = (P_ <= Q_); c[3] = (P_ > Q_); c[4] = (Q_ >= P_); c[5] = (Q_ > P_)
    c[13] = (P_ >= Q_); c[14] = (P_ < Q_); c[15] = (Q_ <= P_); c[16] = (Q_ < P_)
    for l in range(7):
        bs = 2 ** (l + 1)
        same = (P_ // bs) == (Q_ // bs)
        c[6 + l] = same & ((Q_ % bs) >= bs // 2) & ((P_ % bs) < bs // 2)
        c[17 + l] = same & ((P_ % bs) >= bs // 2) & ((Q_ % bs) < bs // 2)
    return np.ascontiguousarray(c.transpose(1, 0, 2).reshape(128, NCONST * 128))


def _fm(v, n):
    return np.ascontiguousarray(np.asarray(v, np.float32).reshape(n, 128).T)


def _core_inputs(inp, b, s, NL, NC):
    f32 = lambda a: np.ascontiguousarray(np.asarray(a, np.float32))
    w_in = np.asarray(inp["w_in"][0], np.float32)
    o_z, o_b, o_d, o_cq, o_ckv, o_kr, o_g = 3072, 4096, 4112, 4128, 4512, 4768, 4832
    x = np.asarray(inp["x"][b], np.float32)[:NL]
    ctx = np.asarray(inp["ctx"][b], np.float32)[:NC]
    conv = np.asarray(inp["conv_qkv"][0], np.float32)
    dF, dB = (0, 1) if s == 0 else (1, 0)
    pos = np.arange(NL)
    if s == 1:
        x = x[::-1]; ctx = ctx[::-1]; conv = conv[::-1]; pos = pos[::-1]
    beta = lambda d: w_in[:, o_b + 8 * d:o_b + 8 * d + 8]
    dec = lambda d: w_in[:, o_d + 8 * d:o_d + 8 * d + 8]
    w_bd = np.concatenate([beta(dF), dec(dF), beta(dB), dec(dB)], axis=1)
    inv = (10000.0 ** (-np.arange(16, dtype=np.float32) / np.float32(16))).astype(np.float32)
    row = (pos // GRID_W).astype(np.float32); col = (pos % GRID_W).astype(np.float32)
    ang = np.stack([row[:, None] * inv[None, :], col[:, None] * inv[None, :]], axis=1).astype(np.float32)
    cs_l = np.cos(ang).astype(np.float32); sn_l = np.sin(ang).astype(np.float32)
    cos_t = np.ones((NC + NL, 2, 2, 16), np.float32); sin_t = np.zeros((NC + NL, 2, 2, 16), np.float32)
    cos_t[NC:, :, 0, :] = cs_l; cos_t[NC:, :, 1, :] = cs_l
    sin_t[NC:, :, 0, :] = -sn_l; sin_t[NC:, :, 1, :] = sn_l
    w_ukv = np.asarray(inp["w_ukv"][0], np.float32).reshape(256, 8, 256)
    return {
        "x": f32(x), "ctx": f32(ctx),
        "cvec": f32(np.concatenate([_fm(inp["c"][b], 8), _fm(inp["c_ctx"], 8)], axis=1)),
        "w_mod": f32(inp["w_mod"][0]), "b_mod": _fm(inp["b_mod"][0], 48),
        "norm_attn": _fm(inp["norm_attn"][0], 8), "norm_mlp": _fm(inp["norm_mlp"][0], 8),
        "w_qkv": f32(w_in[:, 0:3072]), "w_zg": f32(np.concatenate([w_in[:, o_z:o_b], w_in[:, o_g:o_g + 2048]], axis=1)),
        "w_bd": f32(w_bd), "w_mla": f32(w_in[:, o_cq:o_g]),
        "conv_w": f32(conv.T.reshape(24, 128, 5).transpose(1, 0, 2).reshape(128, 120)),
        "a_log": f32(np.concatenate([inp["gdn_a_log"][0][dF], inp["gdn_a_log"][0][dB]])),
        "dt_bias": f32(np.concatenate([inp["gdn_dt_bias"][0][dF], inp["gdn_dt_bias"][0][dB]])),
        "gdn_out_norm": f32(inp["gdn_out_norm"][0]), "q_a_norm": f32(inp["mla_q_a_norm"][0]), "kv_a_norm": f32(inp["mla_kv_a_norm"][0]),
        "q_norm": f32(inp["q_norm"][0]), "k_norm": f32(inp["k_norm"][0]),
        "w_uq": f32(inp["w_uq"][0]), "w_uk": f32(w_ukv[:, :, 0:128].reshape(256, 1024)), "w_uv": f32(w_ukv[:, :, 128:256].reshape(256, 1024)),
        "w_bg": f32(inp["w_branch_gdn"][0]), "w_bm": f32(inp["w_branch_mla"][0]), "w_out": f32(inp["w_out"][0]),
        "w_mlp_in": f32(inp["w_mlp_in"][0]), "w_mlp_out": f32(inp["w_mlp_out"][0]),
        "consts": _consts(), "rope_cos": f32(cos_t.reshape(NC + NL, 64)), "rope_sin": f32(sin_t.reshape(NC + NL, 64)),
    }


_NC_CACHE = {}


def run(inp, B, NL, NC):
    key = (NL, NC)
    if key not in _NC_CACHE:
        _NC_CACHE[key] = build_nc(NL, NC)[0]
    nc = _NC_CACHE[key]
    cores = [(b, s) for b in range(B) for s in range(2)]
    in_maps = [_core_inputs(inp, b, s, NL, NC) for (b, s) in cores]
    res = run_bass_kernel_spmd(nc, in_maps, core_ids=list(range(len(cores))))
    NO = NL // 2
    out = np.zeros((B, NL, D), np.float32)
    for (b, s), r in zip(cores, res.results):
        o = np.asarray(r["out"], np.float32)
        if s == 0:
            out[b, :NO] = o
        else:
            out[b, NO:] = o[::-1]
    return out


def kernel(**inputs):
    inp = {k: np.asarray(v) for k, v in inputs.items()}
    B, NL, _ = inp["x"].shape
    NC = inp["ctx"].shape[1]
    return run(inp, B, NL, NC)
```

```python
import math
import numpy as np
import ml_dtypes
import concourse.bass as bass
import concourse.mybir as mybir
from concourse.bass_utils import run_bass_kernel_spmd

F32 = mybir.dt.float32
BF16 = mybir.dt.bfloat16
AF = mybir.ActivationFunctionType
ALU = mybir.AluOpType
AX = mybir.AxisListType

D = 1024
KD = 8
H = 8
EPS = 1e-6
GRID_W = 64
NCONST = 24


class Buf:
    __slots__ = ("name", "writer", "readers")

    def __init__(self, name):
        self.name = name
        self.writer = None
        self.readers = []


class Op:
    __slots__ = ("eng", "fn", "deps", "signal", "tok", "is_dma", "dsem")

    def __init__(self, eng, fn, is_dma=False):
        self.eng = eng
        self.fn = fn
        self.deps = []
        self.signal = False
        self.tok = None
        self.is_dma = is_dma
        self.dsem = None


class Prog:
    ENGS = ("pe", "act", "dve", "pool", "sp")
    NDSEM = 12

    def __init__(self):
        self.ops = []
        self.bufs = {}
        self.last = {}
        self.dma_since = []

    def _B(self, x):
        b = self.bufs.get(x)
        if b is None:
            b = Buf(x)
            self.bufs[x] = b
        return b

    def op(self, eng, fn, reads=(), writes=(), is_dma=False):
        idx = len(self.ops)
        o = Op(eng, fn, is_dma)
        deps = set()
        for r in reads:
            r = self._B(r)
            if r.writer is not None:
                deps.add(r.writer)
        for w in writes:
            w = self._B(w)
            if w.writer is not None:
                deps.add(w.writer)
            deps.update(w.readers)
        fin = []
        for d in deps:
            od = self.ops[d]
            if od.eng == eng and eng == "pe" and not od.is_dma and not is_dma:
                continue
            fin.append(d)
            od.signal = True
        o.deps = sorted(fin)
        self.ops.append(o)
        for r in reads:
            rb = self._B(r)
            if not is_dma:
                rb.readers = [q for q in rb.readers if self.ops[q].is_dma or self.ops[q].eng != eng]
            rb.readers.append(idx)
        for w in writes:
            w = self._B(w)
            w.writer = idx
            w.readers = []
        self.last[eng] = idx
        if is_dma:
            self.dma_since.append(idx)
        return idx

    def barrier(self):
        deps = sorted(set(list(self.last.values()) + self.dma_since))
        for d in deps:
            self.ops[d].signal = True
        for e in self.ENGS:
            o = Op(e, None)
            o.deps = list(deps)
            self.ops.append(o)
        self.dma_since = []
        for b in self.bufs.values():
            b.writer = None
            b.readers = []

    def emit(self, nc):
        sems = {e: nc.alloc_semaphore("S_" + e) for e in self.ENGS}
        dq = ("sp", "pool", "act")
        dsems = {e: [nc.alloc_semaphore("D_%s_%d" % (e, j)) for j in range(self.NDSEM)] for e in dq}
        cnt = {e: 0 for e in self.ENGS}
        dcnt = {e: [0] * self.NDSEM for e in dq}
        dnext = {e: 0 for e in dq}
        for o in self.ops:
            if o.is_dma:
                j = dnext[o.eng]
                dnext[o.eng] = (j + 1) % self.NDSEM
                prev = dcnt[o.eng][j]
                dcnt[o.eng][j] += 16
                o.dsem = (dsems[o.eng][j], prev)
                o.tok = (dsems[o.eng][j], dcnt[o.eng][j])
            elif o.signal and o.fn is not None:
                cnt[o.eng] += 1
                o.tok = (sems[o.eng], cnt[o.eng])
        per = {e: [] for e in self.ENGS}
        for o in self.ops:
            per[o.eng].append(o)
        ops = self.ops
        tail = [o for o in ops if o.is_dma]

        def run(ename):
            def body(eng):
                seen = {}

                def wait(tok):
                    if tok is None:
                        return
                    s, v = tok
                    k = id(s)
                    if seen.get(k, 0) >= v:
                        return
                    seen[k] = v
                    eng.wait_ge(s, v)

                for o in per[ename]:
                    for d in o.deps:
                        wait(ops[d].tok)
                    if o.fn is None:
                        continue
                    if o.is_dma:
                        s, prev = o.dsem
                        if prev > 0:
                            wait((s, prev))
                        o.fn(eng).then_inc(o.tok[0], 16)
                    else:
                        ins = o.fn(eng)
                        if o.signal:
                            ins.then_inc(o.tok[0], 1)
                if ename == "sp":
                    for o in tail:
                        wait(o.tok)
            return body

        with nc.Block() as block:
            block.tensor(run("pe"))
            block.scalar(run("act"))
            block.vector(run("dve"))
            block.gpsimd(run("pool"))
            block.sync(run("sp"))
        return len(ops)


class Arena:
    def __init__(self, nc, nbytes):
        self.t = nc.alloc_sbuf_tensor("arena", [128, nbytes // 4], F32)
        self.cap = nbytes // 4
        self.off = 0
        self.uid = 0

    def alloc(self, shape, dtype):
        n = 1
        for s in shape:
            n *= s
        words = n if dtype == F32 else (n + 1) // 2
        words = (words + 7) // 8 * 8
        assert self.off + words <= self.cap, "SBUF arena overflow %d+%d>%d" % (self.off, words, self.cap)
        v = self.t[:, self.off:self.off + words]
        self.off += words
        if dtype == BF16:
            v = v.bitcast(BF16)
        v = v[:, 0:n]
        if len(shape) == 2:
            v = v.rearrange("p (a b) -> p a b", a=shape[0])
        elif len(shape) == 3:
            v = v.rearrange("p (a b c) -> p a b c", a=shape[0], b=shape[1])
        self.uid += 1
        return v

    def mark(self):
        return self.off

    def release(self, m):
        self.off = m


def bc(ap, shape):
    return ap.to_broadcast(shape)


def build_nc(NL, NC, dbg=False, stop=99, sub=99):
    NO = NL // 2
    NT = NC + NL
    NTC = NC // 128
    NTL = NL // 128
    NTO = NO // 128
    NG = NTC + NTL
    nc = bass.Bass("TRN2", target_bir_lowering=False)

    def din(name, shape, dt=F32):
        return nc.dram_tensor(name, list(shape), dt, kind="ExternalInput").ap()

    x_d = din("x", [NL, D]); ctx_d = din("ctx", [NC, D])
    cvec_d = din("cvec", [128, 16]); wmod_d = din("w_mod", [D, 6 * D]); bmod_d = din("b_mod", [128, 48])
    nattn_d = din("norm_attn", [128, 8]); nmlp_d = din("norm_mlp", [128, 8])
    wqkv_d = din("w_qkv", [D, 3072]); wzg_d = din("w_zg", [D, 3072]); wbd_d = din("w_bd", [D, 32]); wmla_d = din("w_mla", [D, 704])
    conv_d = din("conv_w", [128, 24 * 5]); alog_d = din("a_log", [16]); dtb_d = din("dt_bias", [16])
    gon_d = din("gdn_out_norm", [128]); qan_d = din("q_a_norm", [384]); kvan_d = din("kv_a_norm", [256])
    qn_d = din("q_norm", [192]); kn_d = din("k_norm", [192])
    wuq_d = din("w_uq", [384, 1536]); wuk_d = din("w_uk", [256, 1024]); wuv_d = din("w_uv", [256, 1024])
    wg_d = din("w_bg", [D, D]); wm_d = din("w_bm", [D, D]); wo_d = din("w_out", [D, D])
    w1_d = din("w_mlp_in", [D, 4 * D]); w2_d = din("w_mlp_out", [4 * D, D])
    const_d = din("consts", [128, NCONST * 128]); cos_d = din("rope_cos", [NT, 64]); sin_d = din("rope_sin", [NT, 64])
    out_d = nc.dram_tensor("out", [NO, D], F32, kind="ExternalOutput").ap()

    def dscr(name, shape, dt):
        return nc.dram_tensor(name, list(shape), dt, kind="Internal").ap()

    of_d = dscr("scr_of", [NO, D], F32); ob_d = dscr("scr_ob", [NO, D], F32)
    qt_d = dscr("scr_qt", [H, 192, NO], BF16); om_d = dscr("scr_om", [H, 128, NO], BF16)
    x1_d = dscr("scr_x1", [NO, D], F32)

    P = Prog()
    A = Arena(nc, 207 * 1024)
    ps = [nc.alloc_psum_tensor("psb%d" % i, [128, 512], F32) for i in range(8)]
    bank_state = {"n": 0, "lo": 0, "hi": 8}

    def nb():
        r = bank_state["hi"] - bank_state["lo"]
        b = bank_state["lo"] + bank_state["n"] % r
        bank_state["n"] += 1
        return b

    def pf(b):
        return ps[b][:]

    def pb16(b):
        return ps[b][:].bitcast(BF16)

    def PB(b):
        return "ps%d" % b

    uid = [0]

    def U(prefix):
        uid[0] += 1
        return "%s#%d" % (prefix, uid[0])

    def dma(q, out, in_, reads, writes):
        P.op(q, lambda e: e.dma_start(out=out, in_=in_), reads=reads, writes=writes, is_dma=True)

    def mm(out, lhsT, rhs, start, stop, reads, bank):
        P.op("pe", lambda e: e.matmul(out, lhsT=lhsT, rhs=rhs, start=start, stop=stop), reads=reads, writes=[PB(bank)])

    def tr(out, in_, ident, reads, bank):
        P.op("pe", lambda e: e.transpose(out, in_, ident), reads=reads, writes=[PB(bank)])

    def act(out, in_, func, reads, writes, scale=None, bias=None, accum=None):
        kw = {}
        if scale is not None:
            kw["scale"] = scale
        if bias is not None:
            kw["bias"] = bias
        if accum is not None:
            kw["accum_out"] = accum
        P.op("act", lambda e: e.activation(out=out, in_=in_, func=func, **kw), reads=reads, writes=writes)

    def tt(eng, out, in0, in1, op, reads, writes):
        P.op(eng, lambda e: e.tensor_tensor(out=out, in0=in0, in1=in1, op=op), reads=reads, writes=writes)

    def ts(eng, out, in0, s1, op0, reads, writes, s2=None, op1=None):
        if op1 is None:
            P.op(eng, lambda e: e.tensor_scalar(out=out, in0=in0, scalar1=s1, scalar2=None, op0=op0), reads=reads, writes=writes)
        else:
            P.op(eng, lambda e: e.tensor_scalar(out=out, in0=in0, scalar1=s1, scalar2=s2, op0=op0, op1=op1), reads=reads, writes=writes)

    def stt(out, in0, scalar, in1, op0, op1, reads, writes):
        P.op("dve", lambda e: e.scalar_tensor_tensor(out=out, in0=in0, scalar=scalar, in1=in1, op0=op0, op1=op1), reads=reads, writes=writes)

    def cp(eng, out, in_, reads, writes):
        if eng == "act":
            act(out, in_, AF.Copy, reads, writes)
        else:
            P.op(eng, lambda e: e.tensor_copy(out=out, in_=in_), reads=reads, writes=writes)

    def red(out, in_, reads, writes):
        P.op("dve", lambda e: e.tensor_reduce(out=out, in_=in_, axis=AX.X, op=ALU.add), reads=reads, writes=writes)

    cF = A.alloc([6, 128], F32)
    cB = A.alloc([NCONST, 128], BF16)
    cols = A.alloc([8], F32)
    stage_all = A.alloc([2048], F32)
    stage = [stage_all[:, 0:1024], stage_all[:, 1024:2048]]
    cvec = A.alloc([16], F32)
    sc = A.alloc([16], F32)
    modF = A.alloc([6, 2, 8], F32)
    bmod = A.alloc([48], F32)
    nrm = A.alloc([2, 8], F32)
    G1 = A.alloc([2, 8], F32); G2 = A.alloc([8], F32)
    m0 = A.mark()
    ctemp = A.alloc([NCONST, 128], F32)
    dma("sp", ctemp.rearrange("p a b -> p (a b)"), const_d, [], ["ctemp"])
    dma("sp", cF[:, 0:4, :].rearrange("p a b -> p (a b)"), const_d[:, 0:4 * 128], [], ["cF"])
    dma("sp", cF[:, 4:6, :].rearrange("p a b -> p (a b)"), const_d[:, 13 * 128:15 * 128], [], ["cF"])
    cp("dve", cB, ctemp, ["ctemp"], ["cB"])
    for j, val in enumerate((EPS, 1.0, math.log(128 ** -0.5), 0.0, math.log(192 ** -0.5))):
        P.op("pool", lambda e, j=j, val=val: e.memset(cols[:, j:j + 1], val), writes=["cols"])
    identF = cF[:, 0, :]; onesF = cF[:, 1, :]
    identB = cB[:, 0, :]; onesB = cB[:, 1, :]
    CF = {"F": 2, "B": 4}
    CB = {"F": 2, "B": 13}

    def rsqrt_small(out, in_, mul, reads, writes, tmp, tmpn):
        act(tmp, in_, AF.Ln, reads, [tmpn], scale=mul, bias=cols[:, 0:1])
        act(out, tmp, AF.Exp, [tmpn], writes, scale=-0.5)

    stg = [0]

    def load_w(dst, src, K, N, name, eng_cycle=("act", "dve")):
        for k in range(K // 128):
            for c0 in range(0, N, 1024):
                c1 = min(N, c0 + 1024)
                i = stg[0] % 2
                stg[0] += 1
                st = stage[i][:, 0:c1 - c0]
                dma("sp", st, src[k * 128:(k + 1) * 128, c0:c1], [], ["stage%d" % i])
                cp(eng_cycle[stg[0] % len(eng_cycle)], dst[:, k, c0:c1], st, ["stage%d" % i], [name])

    dma("sp", cvec, cvec_d, [], ["cvec"])
    dma("sp", bmod, bmod_d, [], ["bmod"])
    dma("sp", nrm[:, 0, :], nattn_d, [], ["nrm"])
    dma("sp", nrm[:, 1, :], nmlp_d, [], ["nrm"])
    act(sc, cvec, AF.Silu, ["cvec"], ["sc"])
    scv = sc.rearrange("p (v k) -> p k v", v=2)
    wst = [A.alloc([8, 512], F32) for _ in range(2)]
    for blk in range(12):
        w = wst[blk % 2]
        dma("sp", w, wmod_d[:, blk * 512:(blk + 1) * 512].rearrange("(k p) n -> p k n", p=128), [], ["wst%d" % (blk % 2)])
        b = nb()
        for t4 in range(4):
            for k in range(8):
                mm(pf(b)[:, t4 * 2:(t4 + 1) * 2], w[:, k, t4 * 128:(t4 + 1) * 128], scv[:, k, :], k == 0, k == 7,
                   ["wst%d" % (blk % 2), "sc"], b)
        e = blk // 2
        t0 = (blk % 2) * 4
        tt("dve", modF[:, e, :, t0:t0 + 4], pf(b)[:, 0:8].rearrange("p (t v) -> p v t", v=2),
           bc(bmod[:, e * 8 + t0:e * 8 + t0 + 4].unsqueeze(1), [128, 2, 4]), ALU.add, [PB(b), "bmod"], ["modF"])
    A.release(m0)
    for v in range(2):
        stt(G1[:, v, :], modF[:, 1, v, :], 1.0, nrm[:, 0, :], ALU.add, ALU.mult, ["modF", "nrm"], ["G1"])
    stt(G2, modF[:, 4, 0, :], 1.0, nrm[:, 1, :], ALU.add, ALU.mult, ["modF", "nrm"], ["G2"])
    SH1 = modF[:, 0, :, :]
    SH2 = modF[:, 3, 0, :]

    P.barrier()

    if stop == 0:
        return nc, P.emit(nc)
    def make_hT(src_rows, v, hT, tag, G=None, SH=None, xt_keep=None):
        xt = xt_keep if xt_keep is not None else A_x[tag_i[0] % 2]
        xn = A_xn
        nm = "xt%d" % (tag_i[0] % 2) if xt_keep is None else tag + "xt"
        tag_i[0] += 1
        dma("sp", xt, src_rows, [], [nm])
        act(A_junk, xt, AF.Square, [nm], ["junk", "ss1"], accum=A_ss[:, 0:1])
        rsqrt_small(A_ss[:, 1:2], A_ss[:, 0:1], 1.0 / D, ["ss1"], ["rs1"], A_ss[:, 2:3], "ss1t")
        act(xn, xt, AF.Copy, [nm, "rs1"], ["xn"], scale=A_ss[:, 1:2])
        b = nb()
        for k in range(8):
            tr(pb16(b)[:, k * 128:(k + 1) * 128], xn[:, k * 128:(k + 1) * 128], identB, ["xn", "cB"], b)
        g = G if G is not None else G1[:, v, :]
        s = SH if SH is not None else SH1[:, v, :]
        tt("dve", A_ht32, pb16(b).rearrange("p (k t) -> p k t", k=8), bc(g.unsqueeze(2), [128, 8, 128]), ALU.mult,
           [PB(b), "G1", "G2"], ["ht32"])
        tt("dve", hT, A_ht32, bc(s.unsqueeze(2), [128, 8, 128]), ALU.add, ["ht32", "modF"], [tag])

    tag_i = [0]
    A_x = [A.alloc([D], F32) for _ in range(2)]
    A_xn = A.alloc([D], BF16)
    A_junk = A.alloc([D], BF16)
    A_ss = A.alloc([8], F32)
    A_ht32 = A.alloc([8, 128], F32)

    def tile_rows(g):
        if g < NTC:
            return ctx_d[g * 128:(g + 1) * 128, :], 1
        t = g - NTC
        return x_d[t * 128:(t + 1) * 128, :], 0

    mg = A.mark()
    Wqkv = A.alloc([8, 3072], BF16)
    Wbd = A.alloc([8, 32], BF16)
    convw = A.alloc([24, 5], F32)
    diagW = A.alloc([120, 128], BF16)
    negA = A.alloc([16], F32); dtb = A.alloc([16], F32)
    load_w(Wqkv, wqkv_d, D, 3072, "Wqkv")
    load_w(Wbd, wbd_d, D, 32, "Wbd")
    dma("sp", convw.rearrange("p a b -> p (a b)"), conv_d, [], ["convw"])
    dma("sp", negA, alog_d.partition_broadcast(128), [], ["negA"])
    dma("sp", dtb, dtb_d.partition_broadcast(128), [], ["dtb"])
    act(negA, negA, AF.Exp, ["negA"], ["negA"])
    ts("dve", negA, negA, -1.0, ALU.mult, ["negA"], ["negA"])
    for ct in range(24):
        for i in range(5):
            if (ct + i) % 2:
                act(diagW[:, ct * 5 + i, :], identF, AF.Identity, ["cF", "convw"], ["diagW"], scale=convw[:, ct, i:i + 1])
            else:
                ts("dve", diagW[:, ct * 5 + i, :], identF, convw[:, ct, i:i + 1], ALU.mult, ["cF", "convw"], ["diagW"])

    P.barrier()
    hT = [A.alloc([8, 128], BF16) for _ in range(2)]
    Xw = [A.alloc([24, 132], BF16) for _ in range(2)]
    Xw.append(stage_all[:, 0:24 * 132 // 2].bitcast(BF16).rearrange("p (a b) -> p a b", a=24))
    Xe = [A.alloc([24, 4], BF16) for _ in range(4)]
    YT = A.alloc([24, 128], BF16)
    SQ = A.alloc([16, 128], BF16)
    RS = A.alloc([16, 128], F32)
    QKN = A.alloc([16, 128], BF16)
    Ktok = A.alloc([8, 128], BF16); Vtok = A.alloc([8, 128], BF16)
    sca = A.alloc([12, 8], F32)
    Rm = A.alloc([8, 128], F32); Fm = A.alloc([8, 128], F32)
    FB = A.alloc([8, 128], BF16); Fq = A.alloc([8, 128], BF16)
    BF_ = A.alloc([8, 128], BF16); Bl = [A.alloc([8, 128], BF16) for _ in range(2)]
    qkT = A.alloc([8, 128], BF16)
    Dm = [A.alloc([8, 128], BF16) for _ in range(2)]; Wm = [A.alloc([8, 128], BF16) for _ in range(2)]
    ImY = A.alloc([8, 128], BF16)
    Xp = A.alloc([8, 128], BF16); VN = A.alloc([8, 128], BF16); Kd = A.alloc([8, 128], BF16)
    tmpf = Rm
    S32 = A.alloc([8, 128], F32); Sbf = A.alloc([8, 128], BF16)
    o1 = Fm; osb = [A.alloc([8, 128], F32) for _ in range(2)]

    A_lg = [A.alloc([16], F32) for _ in range(3)]

    def gdn_sweep2(dirn, order, out_tiles, o_dram, extra=None):
        c0 = {"F": 2, "B": 13}[dirn]
        Uc = cF[:, CF[dirn], :]; SUc = cF[:, CF[dirn] + 1, :]
        UIm = cB[:, c0 + 2, :]; SUm = cB[:, c0 + 3, :]
        lvl = [cB[:, c0 + 4 + l, :] for l in range(7)]
        dcol = 0 if dirn == "F" else 16
        dsc = 0 if dirn == "F" else 8
        P.op("pool", lambda e: e.memset(S32, 0.0), writes=["S32a", "S32b"])
        P.op("pool", lambda e: e.memset(Sbf, 0.0), writes=["Sbfa", "Sbfb"])
        n_proc = len(order)
        order = list(order) + ([extra] if extra is not None else [])
        n_ord = len(order)
        asc = (dirn == "F")

        def seq_of(g):
            return 0 if g < NTC else 1

        def v4(b):
            return pf(b).rearrange("p (a t) -> p a t", a=4)

        def project(n):
            g = order[n]
            rows, v = tile_rows(g)
            h = hT[n % 2]
            hn = "hT%d" % (n % 2)
            make_hT(rows, v, h, hn)
            yield
            xw = Xw[n % 3]
            xwn = "Xw%d" % (n % 3)
            for c3 in range(8):
                b = nb()
                for j in range(3):
                    ct = c3 * 3 + j
                    for k in range(8):
                        mm(pf(b)[:, j * 128:(j + 1) * 128], Wqkv[:, k, ct * 128:(ct + 1) * 128], h[:, k, :], k == 0, k == 7, ["Wqkv", hn], b)
                cp("act" if c3 % 2 else "dve", xw[:, c3 * 3:(c3 + 1) * 3, 2:130], pf(b)[:, 0:384].rearrange("p (a t) -> p a t", a=3), [PB(b)], [xwn])
                yield
            xe = Xe[n % 4]
            cp("pool", xe[:, :, 0:2], xw[:, :, 2:4], [xwn], ["Xe%d" % (n % 4)])
            cp("pool", xe[:, :, 2:4], xw[:, :, 128:130], [xwn], ["Xe%d" % (n % 4)])
            b = nb()
            for k in range(8):
                mm(pf(b)[:, 0:16], h[:, k, :], Wbd[:, k, dcol:dcol + 16], k == 0, k == 7, [hn, "Wbd"], b)
            cp("act", A_lg[n % 3], pf(b)[:, 0:16], [PB(b)], ["lg%d" % (n % 3)])

        def chunk(m, fill):
            g = order[m]
            want_out = g in out_tiles
            xw = Xw[m % 3]
            xwn = "Xw%d" % (m % 3)
            lg = A_lg[m % 3]
            lgn = "lg%d" % (m % 3)

            def nbr(mm_):
                if mm_ < 0 or mm_ >= n_ord or seq_of(order[mm_]) != seq_of(g):
                    return None
                return Xe[mm_ % 4], "Xe%d" % (mm_ % 4)
            prv = nbr(m - 1) if asc else nbr(m + 1)
            nxt = nbr(m + 1) if asc else nbr(m - 1)
            if prv is not None:
                cp("pool", xw[:, :, 0:2], prv[0][:, :, 2:4], [prv[1], xwn], [xwn])
            else:
                P.op("pool", lambda e, xw=xw: e.memset(xw[:, :, 0:2], 0.0), reads=[xwn], writes=[xwn])
            if nxt is not None:
                cp("pool", xw[:, :, 130:132], nxt[0][:, :, 0:2], [nxt[1], xwn], [xwn])
            else:
                P.op("pool", lambda e, xw=xw: e.memset(xw[:, :, 130:132], 0.0), reads=[xwn], writes=[xwn])
            for c4 in range(6):
                b = nb()
                for j in range(4):
                    ct = c4 * 4 + j
                    o_ = pf(b)[:, j * 128:(j + 1) * 128]
                    for i in range(5):
                        mm(o_, diagW[:, ct * 5 + i, :], xw[:, ct, i:i + 128], i == 0, i == 4, ["diagW", xwn], b)
                act(YT[:, c4 * 4:(c4 + 1) * 4, :], v4(b), AF.Silu, [PB(b)], ["YT"])
            fill()
            tt("pool", SQ, YT[:, 0:16, :], YT[:, 0:16, :], ALU.mult, ["YT"], ["SQ"])
            for c4 in range(4):
                b = nb()
                for j in range(4):
                    mm(pf(b)[:, j * 128:(j + 1) * 128], onesB, SQ[:, c4 * 4 + j, :], True, True, ["cB", "SQ"], b)
                act(RS[:, c4 * 4:(c4 + 1) * 4, :], v4(b), AF.Ln, [PB(b)], ["RS%d" % c4], bias=cols[:, 0:1])
            act(RS[:, 0:8, :], RS[:, 0:8, :], AF.Exp, ["RS0", "RS1"], ["RS0", "RS1"], scale=-0.5, bias=cols[:, 2:3])
            act(RS[:, 8:16, :], RS[:, 8:16, :], AF.Exp, ["RS2", "RS3"], ["RS2", "RS3"], scale=-0.5)
            tt("dve", QKN, YT[:, 0:16, :], RS, ALU.mult, ["YT", "RS0", "RS1", "RS2", "RS3"], ["QKN"])
            bk = nb()
            for h in range(8):
                tr(pb16(bk)[:, h * 128:(h + 1) * 128], QKN[:, 8 + h, :], identB, ["QKN", "cB"], bk)
            bv = nb()
            for h in range(8):
                tr(pb16(bv)[:, h * 128:(h + 1) * 128], YT[:, 16 + h, :], identB, ["YT", "cB"], bv)
            cp("act", Ktok, pb16(bk).rearrange("p (a t) -> p a t", a=8), [PB(bk)], ["Ktok"])
            cp("dve", Vtok, pb16(bv).rearrange("p (a t) -> p a t", a=8), [PB(bv)], ["Vtok"])
            fill()
            beta = sca[:, 0, :]; xx = sca[:, 1, :]; ax = sca[:, 2, :]; g_ = sca[:, 3, :]
            egc = sca[:, 4, :]; negegc = sca[:, 5, :]; gcs = sca[:, 6, :]; edl = sca[:, 7, :]; etot = sca[:, 8, :]
            act(beta, lg[:, 0:8], AF.Exp, [lgn], ["beta"], scale=-1.0)
            ts("dve", beta, beta, 1.0, ALU.add, ["beta"], ["beta"])
            P.op("dve", lambda e: e.reciprocal(out=beta, in_=beta), reads=["beta"], writes=["beta"])
            tt("dve", xx, lg[:, 8:16], dtb[:, dsc:dsc + 8], ALU.add, [lgn, "dtb"], ["xx"])
            ts("dve", ax, xx, -1.0, ALU.mult, ["xx"], ["ax"])
            tt("dve", ax, ax, xx, ALU.min, ["xx", "ax"], ["ax"])
            act(ax, ax, AF.Exp, ["ax"], ["ax"])
            act(ax, ax, AF.Ln, ["ax"], ["ax"], bias=cols[:, 1:2])
            ts("dve", xx, xx, 0.0, ALU.max, ["xx"], ["xx"])
            tt("dve", xx, xx, ax, ALU.add, ["xx", "ax"], ["xx"])
            tt("dve", g_, xx, negA[:, dsc:dsc + 8], ALU.mult, ["xx", "negA"], ["g"])
            b = nb()
            mm(pf(b)[:, 0:8], Uc, g_, True, True, ["cF", "g"], b)
            mm(pf(b)[:, 8:16], onesF, g_, True, True, ["cF", "g"], b)
            act(egc, pf(b)[:, 0:8], AF.Exp, [PB(b)], ["egc"])
            act(gcs, pf(b)[:, 0:8], AF.Copy, [PB(b)], ["gcs"])
            act(etot, pf(b)[:, 8:16], AF.Exp, [PB(b)], ["etot"])
            ts("dve", negegc, egc, -1.0, ALU.mult, ["egc"], ["negegc"])
            tt("dve", edl, pf(b)[:, 8:16], gcs, ALU.subtract, [PB(b), "gcs", "etot", "egc"], ["edl"])
            act(edl, edl, AF.Exp, ["edl"], ["edl"])
            fill()
            tt("pool", Rm, bc(Uc.unsqueeze(1), [128, 8, 128]), bc(g_.unsqueeze(2), [128, 8, 128]), ALU.mult, ["cF", "g"], ["Rm"])
            for hh in range(2):
                b = nb()
                for j in range(4):
                    mm(pf(b)[:, j * 128:(j + 1) * 128], SUc, Rm[:, hh * 4 + j, :], True, True, ["cF", "Rm"], b)
                act(Fm[:, hh * 4:(hh + 1) * 4, :], v4(b), AF.Exp, [PB(b)], ["Fm%d" % hh])
            tt("pool", FB, Fm, bc(SUm.unsqueeze(1), [128, 8, 128]), ALU.mult, ["Fm0", "Fm1", "cB"], ["FB"])
            if want_out:
                tt("pool", Fq, Fm, bc(UIm.unsqueeze(1), [128, 8, 128]), ALU.mult, ["Fm0", "Fm1", "cB"], ["Fq"])
            for hh in range(2):
                sl = slice(hh * 4, (hh + 1) * 4)
                b = nb()
                for j in range(4):
                    h = hh * 4 + j
                    mm(pf(b)[:, j * 128:(j + 1) * 128], QKN[:, 8 + h, :], QKN[:, 8 + h, :], True, True, ["QKN"], b)
                tt("dve", BF_[:, sl, :], v4(b), FB[:, sl, :], ALU.mult, [PB(b), "FB"], ["BF%d" % hh])
                if want_out:
                    b2 = nb()
                    for j in range(4):
                        h = hh * 4 + j
                        mm(pf(b2)[:, j * 128:(j + 1) * 128], QKN[:, 8 + h, :], QKN[:, h, :], True, True, ["QKN"], b2)
                    tt("dve", qkT[:, sl, :], v4(b2), Fq[:, sl, :], ALU.mult, [PB(b2), "Fq"], ["qkT%d" % hh])
            tt("pool", Dm[0], bc(identB.unsqueeze(1), [128, 8, 128]), bc(beta.unsqueeze(2), [128, 8, 128]), ALU.mult, ["cB", "beta"], ["D0a", "D0b"])
            hs = ("a", "b")
            for l in range(7):
                fill()
                cur, nx = l % 2, (l + 1) % 2
                Wcur = Dm[0] if l == 0 else Wm[cur]
                wn_ = (lambda hh_: "D0" + hs[hh_]) if l == 0 else (lambda hh_, cur=cur: "W%d%s" % (cur, hs[hh_]))
                Bc = Bl[l % 2]
                bn = "Bl%d" % (l % 2)
                tt("pool", Bc, BF_, bc(lvl[l].unsqueeze(1), [128, 8, 128]), ALU.mult, ["BF0", "BF1", "cB"], [bn])
                for hh in range(2):
                    sl = slice(hh * 4, (hh + 1) * 4)
                    b = nb()
                    for j in range(4):
                        h = hh * 4 + j
                        mm(pf(b)[:, j * 128:(j + 1) * 128], Bc[:, h, :], Dm[cur][:, h, :], True, True, [bn, "D%d%s" % (cur, hs[hh])], b)
                    tt("dve", ImY[:, sl, :], bc(identB.unsqueeze(1), [128, 4, 128]), v4(b), ALU.subtract, [PB(b), "cB"], ["ImY%d" % hh])
                for hh in range(2):
                    sl = slice(hh * 4, (hh + 1) * 4)
                    if l < 6:
                        b = nb()
                        for j in range(4):
                            h = hh * 4 + j
                            mm(pf(b)[:, j * 128:(j + 1) * 128], Wcur[:, h, :], ImY[:, h, :], True, True, [wn_(hh), "ImY%d" % hh], b)
                        cp("act", Dm[nx][:, sl, :], v4(b), [PB(b)], ["D%d%s" % (nx, hs[hh])])
                    b = nb()
                    for j in range(4):
                        h = hh * 4 + j
                        mm(pf(b)[:, j * 128:(j + 1) * 128], ImY[:, h, :], Wcur[:, h, :], True, True, [wn_(hh), "ImY%d" % hh], b)
                    cp("act" if hh else "dve", Wm[nx][:, sl, :], v4(b), [PB(b)], ["W%d%s" % (nx, hs[hh])])
            WT = Wm[1]
            tt("pool", Kd, Ktok, bc(edl.unsqueeze(2), [128, 8, 128]), ALU.mult, ["Ktok", "edl"], ["Kd"])
            for hh in range(2):
                sl = slice(hh * 4, (hh + 1) * 4)
                b = nb()
                for j in range(4):
                    h = hh * 4 + j
                    mm(pf(b)[:, j * 128:(j + 1) * 128], QKN[:, 8 + h, :], Sbf[:, h, :], True, True, ["QKN", "Sbf" + hs[hh]], b)
                tt("dve", tmpf[:, sl, :], v4(b), bc(negegc[:, sl].unsqueeze(2), [128, 4, 128]), ALU.mult, [PB(b), "negegc"], ["Rm"])
                tt("dve", Xp[:, sl, :], tmpf[:, sl, :], Vtok[:, sl, :], ALU.add, ["Rm", "Vtok"], ["Xp%d" % hh])
            for hh in range(2):
                sl = slice(hh * 4, (hh + 1) * 4)
                b = nb()
                for j in range(4):
                    h = hh * 4 + j
                    mm(pf(b)[:, j * 128:(j + 1) * 128], WT[:, h, :], Xp[:, h, :], True, True, ["W1%s" % hs[hh], "Xp%d" % hh], b)
                cp("act", VN[:, sl, :], v4(b), [PB(b)], ["VN%d" % hh])
            if want_out:
                ot = out_tiles[g]
                ob_ = osb[ot % 2]
                obn = "osb%d" % (ot % 2)
                for hh in range(2):
                    sl = slice(hh * 4, (hh + 1) * 4)
                    b = nb()
                    for j in range(4):
                        h = hh * 4 + j
                        mm(pf(b)[:, j * 128:(j + 1) * 128], QKN[:, h, :], Sbf[:, h, :], True, True, ["QKN", "Sbf" + hs[hh]], b)
                    tt("dve", o1[:, sl, :], v4(b), bc(egc[:, sl].unsqueeze(2), [128, 4, 128]), ALU.mult, [PB(b), "egc"], ["Fm%d" % hh])
                    b = nb()
                    for j in range(4):
                        h = hh * 4 + j
                        mm(pf(b)[:, j * 128:(j + 1) * 128], qkT[:, h, :], VN[:, h, :], True, True, ["qkT%d" % hh, "VN%d" % hh], b)
                    tt("dve", ob_[:, sl, :], v4(b), o1[:, sl, :], ALU.add, [PB(b), "Fm%d" % hh], [obn + hs[hh]])
                dma("pool", o_dram[ot * 128:(ot + 1) * 128, :], ob_.rearrange("p a t -> p (a t)"), [obn + "a", obn + "b"], [obn + "a", obn + "b"])
            for hh in range(2):
                sl = slice(hh * 4, (hh + 1) * 4)
                b = nb()
                for j in range(4):
                    h = hh * 4 + j
                    mm(pf(b)[:, j * 128:(j + 1) * 128], Kd[:, h, :], VN[:, h, :], True, True, ["Kd", "VN%d" % hh], b)
                for j in range(4):
                    h = hh * 4 + j
                    stt(S32[:, h, :], S32[:, h, :], etot[:, h:h + 1], pf(b)[:, j * 128:(j + 1) * 128], ALU.mult, ALU.add,
                        [PB(b), "S32" + hs[hh], "etot"], ["S32" + hs[hh]])
                cp("act", Sbf[:, sl, :], S32[:, sl, :], ["S32" + hs[hh]], ["Sbf" + hs[hh]])

        def drain(gen):
            for _ in gen:
                pass

        drain(project(0))
        if n_ord > 1:
            drain(project(1))
        for m in range(n_proc):
            gen = project(m + 2) if m + 2 < n_ord else iter(())

            def fill(gen=gen):
                next(gen, None)
            chunk(m, fill)
            drain(gen)

    own = {NTC + t: t for t in range(NTO)}
    orderF = list(range(NTC)) + [NTC + t for t in range(NTO)]
    orderB = list(range(NTC - 1, -1, -1)) + [NTC + t for t in range(NTL - 1, -1, -1)]
    P.barrier()
    gdn_sweep2("F", orderF, own, of_d, extra=NTC + NTO)
    if stop == 1:
        return nc, P.emit(nc)
    gdn_sweep2("B", orderB, own, ob_d)
    P.barrier()
    A.release(mg)

    if stop == 2:
        return nc, P.emit(nc)
    mm_ = A.mark()
    Wuk = A.alloc([2, 1024], BF16); Wuv = A.alloc([2, 1024], BF16)
    ckvnT = A.alloc([2, NT], BF16); krT = A.alloc([NT], BF16)
    rstdk = A.alloc([NG, 8], F32)
    m2b = A.mark()
    Wmla = A.alloc([8, 704], BF16); Wuq = A.alloc([3, 1536], BF16)
    load_w(Wmla, wmla_d, D, 704, "Wmla"); load_w(Wuq, wuq_d, 384, 1536, "Wuq")
    load_w(Wuk, wuk_d, 256, 1024, "Wuk"); load_w(Wuv, wuv_d, 256, 1024, "Wuv")
    qan_b = A.alloc([384], F32); kvan_b = A.alloc([256], F32); gq_b = A.alloc([8, 192], F32); gk_b = A.alloc([192], F32)
    dma("sp", qan_b, qan_d.partition_broadcast(128), [], ["qan"])
    dma("sp", kvan_b, kvan_d.partition_broadcast(128), [], ["kvan"])
    dma("sp", gq_b[:, 0, :], qn_d.partition_broadcast(128), [], ["gq"])
    dma("sp", gk_b, kn_d.partition_broadcast(128), [], ["gk"])
    tt("dve", gq_b[:, 0, 0:128], gq_b[:, 0, 0:128], gk_b[:, 0:128], ALU.mult, ["gq", "gk"], ["gq"])
    for h in range(1, 8):
        cp("pool", gq_b[:, h, :], gq_b[:, 0, :], ["gq"], ["gq"])
    cqnT = A.alloc([3, NO], BF16)
    hT2 = A.alloc([8, 128], BF16)
    lat = A.alloc([704], F32); latn = A.alloc([640], BF16)
    sqa = A.alloc([256], F32); sqk = A.alloc([1024], F32); sqq = A.alloc([1536], F32); sqc = A.alloc([384], F32)
    sqr = A.alloc([64], F32)
    ssm = A.alloc([48], F32)
    cs = A.alloc([2, 64], F32)
    kr32 = A.alloc([64], F32); krt = A.alloc([64], F32); krtmp = A.alloc([64], F32); krb = A.alloc([128], BF16)
    q32 = A.alloc([8, 192], F32); qrt = A.alloc([8, 64], F32); qtmp = A.alloc([8, 64], F32); qbf = A.alloc([8, 192], BF16)
    qTs = A.alloc([8, 2, 128], BF16)

    def rope(dst, src, tmp, cosv, sinv, nh, rd, nm):
        s5 = src.rearrange("p h (a c f) -> p h a c f", a=2, c=2)
        t5 = tmp.rearrange("p h (a c f) -> p h a c f", a=2, c=2)
        sn5 = sinv.rearrange("p (a c f) -> p a c f", a=2, c=2)
        for c in range(2):
            for a_ in range(2):
                tt("dve", t5[:, :, a_, c, :], s5[:, :, a_, 1 - c, :], bc(sn5[:, a_, c, :].unsqueeze(1), [128, nh, 16]), ALU.mult,
                   rd, [nm + "tmp"])
        tt("dve", dst, src, bc(cosv.unsqueeze(1), [128, nh, 64]), ALU.mult, rd, [nm])
        tt("dve", dst, dst, tmp, ALU.add, [nm, nm + "tmp"], [nm])

    lat2 = [lat, A.alloc([704], F32)]
    cs2 = [cs, A.alloc([2, 64], F32)]

    def p2b_front(g):
        rows, v = tile_rows(g)
        is_own = g in own
        lat = lat2[g % 2]; latn_ = "lat%d" % (g % 2)
        cs = cs2[g % 2]; csn = "cs%d" % (g % 2)
        make_hT(rows, v, hT2, "hT2")
        ncol = 704 if is_own else 320
        c_lo = 0 if is_own else 384
        b0 = nb()
        w0 = min(512, ncol)
        for k in range(8):
            mm(pf(b0)[:, 0:w0], hT2[:, k, :], Wmla[:, k, c_lo:c_lo + w0], k == 0, k == 7, ["hT2", "Wmla"], b0)
        cp("dve", lat[:, c_lo:c_lo + w0], pf(b0)[:, 0:w0], [PB(b0)], [latn_])
        if ncol > 512:
            b1 = nb()
            for k in range(8):
                mm(pf(b1)[:, 0:ncol - 512], hT2[:, k, :], Wmla[:, k, 512:ncol], k == 0, k == 7, ["hT2", "Wmla"], b1)
            cp("act", lat[:, 512:ncol], pf(b1)[:, 0:ncol - 512], [PB(b1)], [latn_])
        dma("sp", cs[:, 0, :], cos_d[g * 128:(g + 1) * 128, :], [], [csn])
        dma("sp", cs[:, 1, :], sin_d[g * 128:(g + 1) * 128, :], [], [csn])

    p2b_front(0)
    for g in range(NG):
        if g + 1 < NG:
            p2b_front(g + 1)
        is_own = g in own
        lat = lat2[g % 2]; cs = cs2[g % 2]
        LAT = "lat%d" % (g % 2); CS = "cs%d" % (g % 2)
        act(sqa, lat[:, 384:640], AF.Square, [LAT], ["sqa", "ss0"], accum=ssm[:, 0:1])
        rsqrt_small(ssm[:, 1:2], ssm[:, 0:1], 1.0 / 256, ["ss0"], ["ss1k"], ssm[:, 2:3], "ss2k")
        stt(latn[:, 384:640], lat[:, 384:640], ssm[:, 1:2], kvan_b, ALU.mult, ALU.mult, [LAT, "ss1k", "kvan"], ["latn_kv"])
        b = nb()
        for j in range(2):
            tr(pb16(b)[:, j * 128:(j + 1) * 128], latn[:, 384 + j * 128:384 + (j + 1) * 128], identB, ["latn_kv", "cB"], b)
        cp("act", ckvnT[:, :, g * 128:(g + 1) * 128], pb16(b)[:, 0:256].rearrange("p (a t) -> p a t", a=2), [PB(b)], ["ckvnT"])
        if sub == 2:
            return nc, P.emit(nc)
        for half in range(2):
            b = nb()
            for kk in range(2):
                mm(pf(b), ckvnT[:, kk, g * 128:(g + 1) * 128], Wuk[:, kk, half * 512:(half + 1) * 512], kk == 0, kk == 1, ["ckvnT", "Wuk"], b)
            act(sqk[:, half * 512:(half + 1) * 512], pf(b), AF.Square, [PB(b)], ["sqk"])
        red(ssm[:, 8:16], sqk.rearrange("p (h d) -> p h d", h=8), ["sqk"], ["ss8"])
        act(sqr, lat[:, 640:704], AF.Square, [LAT], ["sqr", "ss3"], accum=ssm[:, 3:4])
        ts("dve", ssm[:, 8:16], ssm[:, 8:16], ssm[:, 3:4], ALU.add, ["ss8", "ss3"], ["ss8"])
        act(ssm[:, 16:24], ssm[:, 8:16], AF.Ln, ["ss8"], ["ss16"], scale=1.0 / 192, bias=cols[:, 0:1])
        act(rstdk[:, g, :], ssm[:, 16:24], AF.Exp, ["ss16"], ["rstdk"], scale=-0.5, bias=cols[:, 4:5])
        if sub == 3:
            return nc, P.emit(nc)
        tt("dve", kr32, lat[:, 640:704], gk_b[:, 128:192], ALU.mult, [LAT, "gk"], ["kr32"])
        rope(krt.unsqueeze(1), kr32.unsqueeze(1), krtmp.unsqueeze(1), cs[:, 0, :], cs[:, 1, :], 1, ["kr32", CS], "krt")
        cp("dve", krb[:, 0:64], krt, ["krt"], ["krb"])
        cp("dve", krb[:, 64:128], krt, ["krt"], ["krb"])
        b = nb()
        tr(pb16(b)[:, 0:128], krb, identB, ["krb", "cB"], b)
        cp("act", krT[:, g * 128:(g + 1) * 128], pb16(b)[:, 0:128], [PB(b)], ["krT"])
        if sub == 4:
            return nc, P.emit(nc)
        if is_own:
            ot = own[g]
            act(sqc, lat[:, 0:384], AF.Square, [LAT], ["sqc", "ss4"], accum=ssm[:, 4:5])
            rsqrt_small(ssm[:, 5:6], ssm[:, 4:5], 1.0 / 384, ["ss4"], ["ss5"], ssm[:, 6:7], "ss6")
            stt(latn[:, 0:384], lat[:, 0:384], ssm[:, 5:6], qan_b, ALU.mult, ALU.mult, [LAT, "ss5", "qan"], ["latn_q"])
            b = nb()
            for j in range(3):
                tr(pb16(b)[:, j * 128:(j + 1) * 128], latn[:, j * 128:(j + 1) * 128], identB, ["latn_q", "cB"], b)
            cp("act", cqnT[:, :, ot * 128:(ot + 1) * 128], pb16(b)[:, 0:384].rearrange("p (a t) -> p a t", a=3), [PB(b)], ["cqnT"])
            if sub == 5:
                return nc, P.emit(nc)
            q2 = q32.rearrange("p h d -> p (h d)")
            for j in range(3):
                b = nb()
                for kk in range(3):
                    mm(pf(b), cqnT[:, kk, ot * 128:(ot + 1) * 128], Wuq[:, kk, j * 512:(j + 1) * 512], kk == 0, kk == 2, ["cqnT", "Wuq"], b)
                cp("dve", q2[:, j * 512:(j + 1) * 512], pf(b), [PB(b)], ["q32"])
                act(sqq[:, j * 512:(j + 1) * 512], q2[:, j * 512:(j + 1) * 512], AF.Square, ["q32"], ["sqq"])
            red(ssm[:, 24:32], sqq.rearrange("p (h d) -> p h d", h=8), ["sqq"], ["ss24"])
            act(ssm[:, 32:40], ssm[:, 24:32], AF.Ln, ["ss24"], ["ss32"], scale=1.0 / 192, bias=cols[:, 0:1])
            act(ssm[:, 32:40], ssm[:, 32:40], AF.Exp, ["ss32"], ["ss32"], scale=-0.5)
            tt("dve", q32, q32, bc(ssm[:, 32:40].unsqueeze(2), [128, 8, 192]), ALU.mult, ["q32", "ss32"], ["q32"])
            tt("dve", q32, q32, gq_b, ALU.mult, ["q32", "gq"], ["q32"])
            rope(qrt, q32[:, :, 128:192], qtmp, cs[:, 0, :], cs[:, 1, :], 8, ["q32", CS], "qrt")
            cp("dve", qbf[:, :, 0:128], q32[:, :, 0:128], ["q32"], ["qbfn"])
            cp("pool", qbf[:, :, 128:192], qrt, ["qrt"], ["qbfr"])
            if sub == 6:
                return nc, P.emit(nc)
            for hh in range(2):
                bn_ = nb()
                for j in range(4):
                    h = hh * 4 + j
                    tr(pb16(bn_)[:, j * 128:(j + 1) * 128], qbf[:, h, 0:128], identB, ["qbfn", "cB"], bn_)
                cp("act", qTs[:, hh * 4:(hh + 1) * 4, 0, :], pb16(bn_)[:, 0:512].rearrange("p (a t) -> p a t", a=4), [PB(bn_)], ["qTs%d" % hh])
                br_ = nb()
                for j in range(4):
                    h = hh * 4 + j
                    tr(pb16(br_)[0:64, j * 128:(j + 1) * 128], qbf[:, h, 128:192], identB, ["qbfr", "cB"], br_)
                cp("dve", qTs[0:64, hh * 4:(hh + 1) * 4, 1, :], pb16(br_)[0:64, 0:512].rearrange("p (a t) -> p a t", a=4), [PB(br_)], ["qTr%d" % hh])
            if sub == 7:
                return nc, P.emit(nc)
            dma("pool", qt_d[:, 0:128, ot * 128:(ot + 1) * 128].rearrange("h d t -> d h t"), qTs[:, :, 0, :], ["qTs0", "qTs1"], ["qTs0", "qTs1", "qt_d"])
            dma("pool", qt_d[:, 128:192, ot * 128:(ot + 1) * 128].rearrange("h d t -> d h t"), qTs[0:64, :, 1, :], ["qTr0", "qTr1"], ["qTr0", "qTr1", "qt_d"])

    if stop == 3:
        return nc, P.emit(nc)
    P.barrier()
    A.release(m2b)
    m3 = A.mark()
    KnT = A.alloc([NT], BF16); Vh = A.alloc([NG, 128], BF16)
    QTn = A.alloc([NO], BF16); QTr = A.alloc([NO], BF16)
    PT = [A.alloc([512], BF16) for _ in range(4)]
    rec = A.alloc([512], F32); acc = A.alloc([512], F32)
    oT = [A.alloc([512], BF16) for _ in range(2)]
    NQB = NO // 512
    NKB = (NT + 511) // 512
    bank_state.update(n=0, lo=0, hi=4)
    for h in range(8):
        dma("sp", QTn, qt_d[h, 0:128, :], ["qt_d"], ["QTn"])
        dma("sp", QTr[0:64, :], qt_d[h, 128:192, :], ["qt_d"], ["QTr"])
        dma("sp", QTr[64:128, :], qt_d[h, 128:192, :], ["qt_d"], ["QTr"])
        for kb in range(NKB):
            c0, c1 = kb * 512, min(NT, (kb + 1) * 512)
            b = nb()
            for kk in range(2):
                mm(pf(b)[:, 0:c1 - c0], Wuk[:, kk, h * 128:(h + 1) * 128], ckvnT[:, kk, c0:c1], kk == 0, kk == 1, ["Wuk", "ckvnT"], b)
            cp("dve" if kb % 2 else "act", KnT[:, c0:c1], pf(b)[:, 0:c1 - c0], [PB(b)], ["KnT"])
        for g4 in range(0, NG, 4):
            ng = min(4, NG - g4)
            b = nb()
            for j in range(ng):
                for kk in range(2):
                    mm(pf(b)[:, j * 128:(j + 1) * 128], ckvnT[:, kk, (g4 + j) * 128:(g4 + j + 1) * 128], Wuv[:, kk, h * 128:(h + 1) * 128],
                       kk == 0, kk == 1, ["ckvnT", "Wuv"], b)
            cp("dve", Vh[:, g4:g4 + ng, :], pf(b)[:, 0:ng * 128].rearrange("p (a t) -> p a t", a=ng), [PB(b)], ["Vh"])
        for qb in range(NQB):
            qs = slice(qb * 512, (qb + 1) * 512)
            ba, bs_ = (4, 5) if (h * NQB + qb) % 2 == 0 else (6, 7)

            def st2(gp):
                g0, g1 = 2 * gp, 2 * gp + 1
                b0 = nb(); b1 = nb()
                mm(pf(b0), KnT[:, g0 * 128:(g0 + 1) * 128], QTn[:, qs], True, False, ["KnT", "QTn"], b0)
                mm(pf(b1), KnT[:, g1 * 128:(g1 + 1) * 128], QTn[:, qs], True, False, ["KnT", "QTn"], b1)
                mm(pf(b0), krT[0:64, g0 * 128:(g0 + 1) * 128], QTr[0:64, qs], False, True, ["krT", "QTr"], b0)
                mm(pf(b1), krT[64:128, g1 * 128:(g1 + 1) * 128], QTr[64:128, qs], False, True, ["krT", "QTr"], b1)
                return (b0, b1)
            assert NG % 2 == 0
            bc_ = st2(0)
            for gp in range(NG // 2):
                bn_ = st2(gp + 1) if gp + 1 < NG // 2 else None
                for u in range(2):
                    g = 2 * gp + u
                    pt = PT[g % 4]
                    ptn = "PT%d" % (g % 4)
                    act(pt, pf(bc_[u]), AF.Exp, [PB(bc_[u]), "rstdk"], [ptn], scale=rstdk[:, g, h:h + 1])
                    mm(pf(ba), Vh[:, g, :], pt, g == 0, g == NG - 1, ["Vh", ptn], ba)
                    if g == 0:
                        cp("dve", acc, pt, [ptn], ["acc"])
                    else:
                        tt("dve", acc, acc, pt, ALU.add, ["acc", ptn], ["acc"])
                bc_ = bn_
            mm(pf(bs_), onesF, acc, True, True, ["cF", "acc"], bs_)
            P.op("dve", lambda e, bs_=bs_: e.reciprocal(out=rec, in_=pf(bs_)), reads=[PB(bs_)], writes=["rec"])
            o_ = oT[qb % 2]
            on = "oT%d" % (qb % 2)
            tt("dve", o_, pf(ba), rec, ALU.mult, [PB(ba), "rec"], [on])
            dma("pool", om_d[h, :, qs], o_, [on], [on, "om_d"])
    bank_state.update(n=0, lo=0, hi=8)
    P.barrier()
    A.release(m3)
    A.release(mm_)

    if stop == 4:
        return nc, P.emit(nc)
    m4 = A.mark()
    Wzg = A.alloc([8, 3072], BF16); Wg = A.alloc([8, D], BF16); Wmm = A.alloc([8, D], BF16); Wo = A.alloc([8, D], BF16)
    load_w(Wzg, wzg_d, D, 3072, "Wzg"); load_w(Wg, wg_d, D, D, "Wg"); load_w(Wmm, wm_d, D, D, "Wmm"); load_w(Wo, wo_d, D, D, "Wo")
    modb = A.alloc([2, D], F32)
    dgm = A.alloc([128], F32)
    for vi, e_ in enumerate((2, 5)):
        for t in range(8):
            ts("dve", dgm, identF, modF[:, e_, 0, t:t + 1], ALU.mult, ["cF", "modF"], ["dgm"])
            b = nb()
            mm(pf(b)[:, 0:128], onesF, dgm, True, True, ["cF", "dgm"], b)
            cp("act", modb[:, vi, t * 128:(t + 1) * 128], pf(b)[:, 0:128], [PB(b)], ["modb"])
    gon_b = A.alloc([128], F32)
    dma("sp", gon_b, gon_d.partition_broadcast(128), [], ["gon"])
    hT4s = [A.alloc([8, 128], BF16) for _ in range(2)]
    xks = [A.alloc([D], F32) for _ in range(2)]
    zs = A.alloc([D], F32); sg = A.alloc([2 * D], F32)
    ofb = A.alloc([D], F32); obb = A.alloc([D], F32)
    ssg = A.alloc([24], F32)
    yg = A.alloc([D], BF16); ygT = A.alloc([8, 128], BF16); omT = A.alloc([8, 128], BF16)
    t32 = A.alloc([D], F32); t32b = A.alloc([D], F32); ybf = A.alloc([D], BF16); yT = A.alloc([8, 128], BF16)
    x1t = [A.alloc([D], F32) for _ in range(2)]
    def p4a_front(ot):
        make_hT(x_d[ot * 128:(ot + 1) * 128, :], 0, hT4s[ot % 2], "hT4_%d" % (ot % 2), xt_keep=xks[ot % 2])

    p4a_front(0)
    for ot in range(NTO):
        if ot + 1 < NTO:
            p4a_front(ot + 1)
        hT4 = hT4s[ot % 2]; xk = xks[ot % 2]
        HT4 = "hT4_%d" % (ot % 2)
        dma("sp", ofb, of_d[ot * 128:(ot + 1) * 128, :], [], ["ofb"])
        dma("sp", obb, ob_d[ot * 128:(ot + 1) * 128, :], [], ["obb"])
        dma("sp", omT, om_d[:, :, ot * 128:(ot + 1) * 128].rearrange("h d t -> d h t"), [], ["omT"])
        for j in range(6):
            b = nb()
            for k in range(8):
                mm(pf(b), hT4[:, k, :], Wzg[:, k, j * 512:(j + 1) * 512], k == 0, k == 7, [HT4, "Wzg"], b)
            if j < 2:
                act(zs[:, j * 512:(j + 1) * 512], pf(b), AF.Silu, [PB(b)], ["zs"])
            else:
                act(sg[:, (j - 2) * 512:(j - 1) * 512], pf(b), AF.Sigmoid, [PB(b)], ["sg"])
        tt("pool", ofb, ofb, obb, ALU.add, ["ofb", "obb"], ["ofb"])
        tt("pool", t32, ofb, ofb, ALU.mult, ["ofb"], ["t32"])
        red(ssg[:, 0:8], t32.rearrange("p (h d) -> p h d", h=8), ["t32"], ["ssg0"])
        act(ssg[:, 8:16], ssg[:, 0:8], AF.Ln, ["ssg0"], ["ssg8"], scale=1.0 / 128, bias=cols[:, 0:1])
        act(ssg[:, 8:16], ssg[:, 8:16], AF.Exp, ["ssg8"], ["ssg8"], scale=-0.5)
        o3 = ofb.rearrange("p (h d) -> p h d", h=8)
        tt("dve", o3, o3, bc(ssg[:, 8:16].unsqueeze(2), [128, 8, 128]), ALU.mult, ["ofb", "ssg8"], ["ofb"])
        tt("dve", o3, o3, bc(gon_b.unsqueeze(1), [128, 8, 128]), ALU.mult, ["ofb", "gon"], ["ofb"])
        tt("dve", yg, ofb, zs, ALU.mult, ["ofb", "zs"], ["yg"])
        b = nb()
        for k in range(8):
            tr(pb16(b)[:, k * 128:(k + 1) * 128], yg[:, k * 128:(k + 1) * 128], identB, ["yg", "cB"], b)
        cp("act", ygT, pb16(b).rearrange("p (a t) -> p a t", a=8), [PB(b)], ["ygT"])
        for half in range(2):
            hs_ = slice(half * 512, (half + 1) * 512)
            b1 = nb()
            for k in range(8):
                mm(pf(b1), ygT[:, k, :], Wg[:, k, hs_], k == 0, k == 7, ["ygT", "Wg"], b1)
            b2 = nb()
            for k in range(8):
                mm(pf(b2), omT[:, k, :], Wmm[:, k, hs_], k == 0, k == 7, ["omT", "Wmm"], b2)
            tt("dve", t32[:, hs_], pf(b1), sg[:, hs_], ALU.mult, [PB(b1), "sg"], ["t32"])
            tt("dve", t32b[:, hs_], pf(b2), sg[:, D + half * 512:D + (half + 1) * 512], ALU.mult, [PB(b2), "sg"], ["t32b"])
            tt("pool", ybf[:, hs_], t32[:, hs_], t32b[:, hs_], ALU.add, ["t32", "t32b"], ["ybf"])
        b = nb()
        for k in range(8):
            tr(pb16(b)[:, k * 128:(k + 1) * 128], ybf[:, k * 128:(k + 1) * 128], identB, ["ybf", "cB"], b)
        cp("act", yT, pb16(b).rearrange("p (a t) -> p a t", a=8), [PB(b)], ["yT"])
        xo = x1t[ot % 2]
        xon = "x1t%d" % (ot % 2)
        for half in range(2):
            hs_ = slice(half * 512, (half + 1) * 512)
            b = nb()
            for k in range(8):
                mm(pf(b), yT[:, k, :], Wo[:, k, hs_], k == 0, k == 7, ["yT", "Wo"], b)
            tt("dve", t32[:, hs_], pf(b), modb[:, 0, hs_], ALU.mult, [PB(b), "modb"], ["t32"])
            tt("pool", xo[:, hs_], t32[:, hs_], xk[:, hs_], ALU.add, ["t32", HT4 + "xt"], [xon])
        dma("pool", x1_d[ot * 128:(ot + 1) * 128, :], xo, [xon], [xon, "x1_d"])
    P.barrier()
    A.release(m4)

    if stop == 5:
        return nc, P.emit(nc)
    W1 = A.alloc([8, 4 * D], BF16); W2 = A.alloc([32, D], BF16)
    load_w(W1, w1_d, D, 4 * D, "W1"); load_w(W2, w2_d, 4 * D, D, "W2")
    modb2 = A.alloc([D], F32)
    dgm2 = A.alloc([128], F32)
    for t in range(8):
        ts("dve", dgm2, identF, modF[:, 5, 0, t:t + 1], ALU.mult, ["cF", "modF"], ["dgm2"])
        b = nb()
        mm(pf(b)[:, 0:128], onesF, dgm2, True, True, ["cF", "dgm2"], b)
        cp("act", modb2[:, t * 128:(t + 1) * 128], pf(b)[:, 0:128], [PB(b)], ["modb2"])
    hT5s = [A.alloc([8, 128], BF16) for _ in range(2)]
    xk2 = [A.alloc([D], F32) for _ in range(2)]
    rl = A.alloc([4, 128], BF16); aT = A.alloc([32, 128], BF16)
    t5 = A.alloc([D], F32); outt = [A.alloc([D], F32) for _ in range(2)]
    def p4b_front(ot):
        make_hT(x1_d[ot * 128:(ot + 1) * 128, :], 0, hT5s[ot % 2], "hT5x%d" % (ot % 2), G=G2, SH=SH2, xt_keep=xk2[ot % 2])

    p4b_front(0)
    for ot in range(NTO):
        if ot + 1 < NTO:
            p4b_front(ot + 1)
        xkk = xk2[ot % 2]
        hT5 = hT5s[ot % 2]
        for f4 in range(8):
            b = nb()
            for j in range(4):
                f = f4 * 4 + j
                for k in range(8):
                    mm(pf(b)[:, j * 128:(j + 1) * 128], W1[:, k, f * 128:(f + 1) * 128], hT5[:, k, :], k == 0, k == 7, ["W1", "hT5x%d" % (ot % 2)], b)
            act(rl, pf(b).rearrange("p (a t) -> p a t", a=4), AF.Relu, [PB(b)], ["rl"])
            tt("pool", aT[:, f4 * 4:(f4 + 1) * 4, :], rl, rl, ALU.mult, ["rl"], ["aT"])
        oo = outt[ot % 2]
        for half in range(2):
            hs_ = slice(half * 512, (half + 1) * 512)
            b = nb()
            for f in range(32):
                mm(pf(b), aT[:, f, :], W2[:, f, hs_], f == 0, f == 31, ["aT", "W2"], b)
            tt("dve", t5[:, hs_], pf(b), modb2[:, hs_], ALU.mult, [PB(b), "modb2"], ["t5%d" % half])
            tt("pool", oo[:, hs_], t5[:, hs_], xkk[:, hs_], ALU.add, ["t5%d" % half, "hT5x%dxt" % (ot % 2)], ["oo%d%d" % (ot % 2, half)])
        dma("pool", out_d[ot * 128:(ot + 1) * 128, :], oo, ["oo%d0" % (ot % 2), "oo%d1" % (ot % 2)], ["oo%d0" % (ot % 2), "oo%d1" % (ot % 2)])
    nops = P.emit(nc)
    return nc, nops


def _consts():
    i = np.arange(128)
    P_, Q_ = np.meshgrid(i, i, indexing="ij")
    c = np.zeros((NCONST, 128, 128), np.float32)
    c[0] = np.eye(128); c[1] = 1.0
    c[2] = (P_ <= Q_); c[3] = (P_ > Q_); c[4] = (Q_ >= P_); c[5] = (Q_ > P_)
    c[13] = (P_ >= Q_); c[14] = (P_ < Q_); c[15] = (Q_ <= P_); c[16] = (Q_ < P_)
    for l in range(7):
        bs = 2 ** (l + 1)
        same = (P_ // bs) == (Q_ // bs)
        c[6 + l] = same & ((Q_ % bs) >= bs // 2) & ((P_ % bs) < bs // 2)
        c[17 + l] = same & ((P_ % bs) >= bs // 2) & ((Q_ % bs) < bs // 2)
    return np.ascontiguousarray(c.transpose(1, 0, 2).reshape(128, NCONST * 128))


def _fm(v, n):
    return np.ascontiguousarray(np.asarray(v, np.float32).reshape(n, 128).T)


def _core_inputs(inp, b, s, NL, NC):
    f32 = lambda a: np.ascontiguousarray(np.asarray(a, np.float32))
    w_in = np.asarray(inp["w_in"][0], np.float32)
    o_z, o_b, o_d, o_cq, o_ckv, o_kr, o_g = 3072, 4096, 4112, 4128, 4512, 4768, 4832
    x = np.asarray(inp["x"][b], np.float32)[:NL]
    ctx = np.asarray(inp["ctx"][b], np.float32)[:NC]
    conv = np.asarray(inp["conv_qkv"][0], np.float32)
    dF, dB = (0, 1) if s == 0 else (1, 0)
    pos = np.arange(NL)
    if s == 1:
        x = x[::-1]; ctx = ctx[::-1]; conv = conv[::-1]; pos = pos[::-1]
    beta = lambda d: w_in[:, o_b + 8 * d:o_b + 8 * d + 8]
    dec = lambda d: w_in[:, o_d + 8 * d:o_d + 8 * d + 8]
    w_bd = np.concatenate([beta(dF), dec(dF), beta(dB), dec(dB)], axis=1)
    inv = (10000.0 ** (-np.arange(16, dtype=np.float32) / np.float32(16))).astype(np.float32)
    row = (pos // GRID_W).astype(np.float32); col = (pos % GRID_W).astype(np.float32)
    ang = np.stack([row[:, None] * inv[None, :], col[:, None] * inv[None, :]], axis=1).astype(np.float32)
    cs_l = np.cos(ang).astype(np.float32); sn_l = np.sin(ang).astype(np.float32)
    cos_t = np.ones((NC + NL, 2, 2, 16), np.float32); sin_t = np.zeros((NC + NL, 2, 2, 16), np.float32)
    cos_t[NC:, :, 0, :] = cs_l; cos_t[NC:, :, 1, :] = cs_l
    sin_t[NC:, :, 0, :] = -sn_l; sin_t[NC:, :, 1, :] = sn_l
    w_ukv = np.asarray(inp["w_ukv"][0], np.float32).reshape(256, 8, 256)
    return {
        "x": f32(x), "ctx": f32(ctx),
        "cvec": f32(np.concatenate([_fm(inp["c"][b], 8), _fm(inp["c_ctx"], 8)], axis=1)),
        "w_mod": f32(inp["w_mod"][0]), "b_mod": _fm(inp["b_mod"][0], 48),
        "norm_attn": _fm(inp["norm_attn"][0], 8), "norm_mlp": _fm(inp["norm_mlp"][0], 8),
        "w_qkv": f32(w_in[:, 0:3072]), "w_zg": f32(np.concatenate([w_in[:, o_z:o_b], w_in[:, o_g:o_g + 2048]], axis=1)),
        "w_bd": f32(w_bd), "w_mla": f32(w_in[:, o_cq:o_g]),
        "conv_w": f32(conv.T.reshape(24, 128, 5).transpose(1, 0, 2).reshape(128, 120)),
        "a_log": f32(np.concatenate([inp["gdn_a_log"][0][dF], inp["gdn_a_log"][0][dB]])),
        "dt_bias": f32(np.concatenate([inp["gdn_dt_bias"][0][dF], inp["gdn_dt_bias"][0][dB]])),
        "gdn_out_norm": f32(inp["gdn_out_norm"][0]), "q_a_norm": f32(inp["mla_q_a_norm"][0]), "kv_a_norm": f32(inp["mla_kv_a_norm"][0]),
        "q_norm": f32(inp["q_norm"][0]), "k_norm": f32(inp["k_norm"][0]),
        "w_uq": f32(inp["w_uq"][0]), "w_uk": f32(w_ukv[:, :, 0:128].reshape(256, 1024)), "w_uv": f32(w_ukv[:, :, 128:256].reshape(256, 1024)),
        "w_bg": f32(inp["w_branch_gdn"][0]), "w_bm": f32(inp["w_branch_mla"][0]), "w_out": f32(inp["w_out"][0]),
        "w_mlp_in": f32(inp["w_mlp_in"][0]), "w_mlp_out": f32(inp["w_mlp_out"][0]),
        "consts": _consts(), "rope_cos": f32(cos_t.reshape(NC + NL, 64)), "rope_sin": f32(sin_t.reshape(NC + NL, 64)),
    }


_NC_CACHE = {}


def run(inp, B, NL, NC):
    key = (NL, NC)
    if key not in _NC_CACHE:
        _NC_CACHE[key] = build_nc(NL, NC)[0]
    nc = _NC_CACHE[key]
    cores = [(b, s) for b in range(B) for s in range(2)]
    in_maps = [_core_inputs(inp, b, s, NL, NC) for (b, s) in cores]
    res = run_bass_kernel_spmd(nc, in_maps, core_ids=list(range(len(cores))))
    NO = NL // 2
    out = np.zeros((B, NL, D), np.float32)
    for (b, s), r in zip(cores, res.results):
        o = np.asarray(r["out"], np.float32)
        if s == 0:
            out[b, :NO] = o
        else:
            out[b, NO:] = o[::-1]
    return out


def kernel(**inputs):
    inp = {k: np.asarray(v) for k, v in inputs.items()}
    B, NL, _ = inp["x"].shape
    NC = inp["ctx"].shape[1]
    return run(inp, B, NL, NC)
```
